# Optimizing a Trainium2 kernel written in Bass

```python
import math
import jax, jax.numpy as jnp
from jax import lax
import numpy as np

D_MODEL = 1024
BATCH = 8
SEQ = 4096
DEPTH = 1
DEC_BATCH = 1
DEC_SEQ = 16384
PAST_LEN = 128

N_HEADS = 8
QK_NOPE = 64
QK_ROPE = 32
V_DIM = 64
Q_RANK = 256
KV_RANK = 128
ATTN_W = N_HEADS * V_DIM
Q_BLOCK = 128
ROPE_THETA = 10000.0
SSM_W = D_MODEL // 2
SSM_GROUP = 16
SSM_GROUPS = SSM_W // SSM_GROUP
SSM_STATE = 64
N_DIR = 2
DT_MIN = 1e-3
DT_MAX = 1e-1
IN_W = Q_RANK + KV_RANK + QK_ROPE + SSM_W
MIX_W = ATTN_W + SSM_W
D_FF = 4 * D_MODEL
EPS = 1e-6

kernel_name = "hymba_s5_mla_encoder"


def rms_norm(x, g):
    xf = x.astype(jnp.float32)
    y = xf * lax.rsqrt(jnp.mean(xf * xf, axis=-1, keepdims=True) + EPS)
    return (y * g.astype(jnp.float32)).astype(x.dtype)


def rope_tables(length):
    pos = jnp.arange(length, dtype=jnp.float32)
    inv = ROPE_THETA ** (-jnp.arange(0, QK_ROPE, 2, dtype=jnp.float32) / QK_ROPE)
    ang = pos[:, None] * inv[None, :]
    return jnp.cos(ang), jnp.sin(ang)


def apply_rope(x, cos, sin):
    half = QK_ROPE // 2
    xf = x.astype(jnp.float32)
    x1, x2 = xf[..., :half], xf[..., half:]
    return jnp.concatenate([x1 * cos - x2 * sin, x2 * cos + x1 * sin], axis=-1).astype(x.dtype)


def mla_attention(c_q, c_kv, k_rope_raw, q_norm_g, w_uq, kv_norm_g, w_ukv):
    b, l, _ = c_q.shape
    q = (rms_norm(c_q, q_norm_g) @ w_uq).reshape(b, l, N_HEADS, QK_NOPE + QK_ROPE)
    kv = (rms_norm(c_kv, kv_norm_g) @ w_ukv).reshape(b, l, N_HEADS, QK_NOPE + V_DIM)
    k_nope, v = kv[..., :QK_NOPE], kv[..., QK_NOPE:]
    cos, sin = rope_tables(l)
    q_rope = apply_rope(q[..., QK_NOPE:], cos[:, None, :], sin[:, None, :])
    k_rope = apply_rope(k_rope_raw, cos, sin)
    q = jnp.concatenate([q[..., :QK_NOPE], q_rope], axis=-1)
    k = jnp.concatenate(
        [k_nope, jnp.broadcast_to(k_rope[:, :, None, :], (b, l, N_HEADS, QK_ROPE))], axis=-1)
    scale = 1.0 / math.sqrt(QK_NOPE + QK_ROPE)
    nb = l // Q_BLOCK
    q_blocks = q.reshape(b, nb, Q_BLOCK, N_HEADS, QK_NOPE + QK_ROPE).transpose(1, 0, 2, 3, 4)

    def attend(qb):
        s = jnp.einsum('bqhd,bkhd->bhqk', qb, k).astype(jnp.float32) * scale
        p = jax.nn.softmax(s, axis=-1).astype(v.dtype)
        return jnp.einsum('bhqk,bkhd->bqhd', p, v)

    o = lax.map(attend, q_blocks)
    return o.transpose(1, 0, 2, 3, 4).reshape(b, l, ATTN_W)


def _ssm_combine(left, right):
    a1, b1 = left
    a2, b2 = right
    return a1 * a2, a2 * b1 + b2


def s5_bidirectional_glu(u, lam_re, lam_im, log_dt, b_re, b_im, c_re, c_im, d_skip, w_glu, b_glu):
    bsz, l, _ = u.shape
    ug = u.astype(jnp.float32).reshape(bsz, l, SSM_GROUPS, SSM_GROUP)
    uc = ug.astype(jnp.complex64)
    y = jnp.zeros_like(ug)
    for d in range(N_DIR):
        lam = lax.complex(lam_re[d].astype(jnp.float32), lam_im[d].astype(jnp.float32))
        dt = jnp.exp(log_dt[d].astype(jnp.float32))[:, None]
        a_bar = jnp.exp(lam * dt)
        b_mat = lax.complex(b_re[d].astype(jnp.float32), b_im[d].astype(jnp.float32))
        c_mat = lax.complex(c_re[d].astype(jnp.float32), c_im[d].astype(jnp.float32))
        b_bar = ((a_bar - 1.0) / lam)[..., None] * b_mat
        bu = jnp.einsum('blgh,gnh->blgn', uc, b_bar)
        a_seq = jnp.broadcast_to(a_bar, bu.shape)
        _, states = lax.associative_scan(_ssm_combine, (a_seq, bu), axis=1, reverse=(d == 1))
        y = y + jnp.real(jnp.einsum('blgn,ghn->blgh', states, c_mat))
    y = y + d_skip.astype(jnp.float32).reshape(SSM_GROUPS, SSM_GROUP) * ug
    y = y.reshape(bsz, l, SSM_W).astype(u.dtype)
    g = jax.nn.gelu(y)
    return g * jax.nn.sigmoid(g @ w_glu + b_glu)


def encoder_layer(x, norm1_g, w_in, q_norm_g, w_uq, kv_norm_g, w_ukv,
                  lam_re, lam_im, log_dt, b_re, b_im, c_re, c_im, d_skip, w_glu, b_glu,
                  attn_out_g, ssm_out_g, w_out, norm2_g, w_mlp1, w_mlp2):
    h = rms_norm(x, norm1_g)
    proj = h @ w_in
    c_q, c_kv, k_rope, u = jnp.split(
        proj, [Q_RANK, Q_RANK + KV_RANK, Q_RANK + KV_RANK + QK_ROPE], axis=-1)
    a = mla_attention(c_q, c_kv, k_rope, q_norm_g, w_uq, kv_norm_g, w_ukv)
    s = s5_bidirectional_glu(u, lam_re, lam_im, log_dt, b_re, b_im, c_re, c_im,
                             d_skip, w_glu, b_glu)
    mixed = jnp.concatenate([rms_norm(a, attn_out_g), rms_norm(s, ssm_out_g)], axis=-1)
    x = x + mixed @ w_out
    h = rms_norm(x, norm2_g)
    x = x + jnp.square(jax.nn.relu(h @ w_mlp1)) @ w_mlp2
    return x


def trunk(x, norm1_g, w_in, q_norm_g, w_uq, kv_norm_g, w_ukv,
          lam_re, lam_im, log_dt, b_re, b_im, c_re, c_im, d_skip, w_glu, b_glu,
          attn_out_g, ssm_out_g, w_out, norm2_g, w_mlp1, w_mlp2, final_g):
    for i in range(DEPTH):
        x = encoder_layer(x, norm1_g[i], w_in[i], q_norm_g[i], w_uq[i], kv_norm_g[i], w_ukv[i],
                          lam_re[i], lam_im[i], log_dt[i], b_re[i], b_im[i], c_re[i], c_im[i],
                          d_skip[i], w_glu[i], b_glu[i], attn_out_g[i], ssm_out_g[i], w_out[i],
                          norm2_g[i], w_mlp1[i], w_mlp2[i])
    return rms_norm(x, final_g)


def setup_inputs(seed: int = 0) -> dict:
    key = jax.random.key(seed)
    ks = jax.random.split(key, 32)
    f32 = jnp.float32

    def nrm(k, shape, scale):
        return jax.random.normal(k, shape, f32) * scale

    def gain(k, shape):
        return 1.0 + 0.02 * jax.random.normal(k, shape, f32)

    G, N, H = SSM_GROUPS, SSM_STATE, SSM_GROUP
    lam_re = -0.5 + 0.01 * jax.random.normal(ks[8], (DEPTH, N_DIR, G, N), f32)
    lam_im = math.pi * jnp.arange(N, dtype=f32) + 0.01 * jax.random.normal(ks[9], (DEPTH, N_DIR, G, N), f32)
    log_dt = jax.random.uniform(ks[10], (DEPTH, N_DIR, G), f32,
                                minval=math.log(DT_MIN), maxval=math.log(DT_MAX))
    return {
        "x_prompt": jax.random.normal(ks[0], (BATCH, SEQ, D_MODEL), f32),
        "x_sample": jax.random.normal(ks[1], (DEC_BATCH, DEC_SEQ, D_MODEL), f32),
        "norm1_g": gain(ks[2], (DEPTH, D_MODEL)),
        "w_in": nrm(ks[3], (DEPTH, D_MODEL, IN_W), D_MODEL ** -0.5),
        "q_norm_g": gain(ks[4], (DEPTH, Q_RANK)),
        "w_uq": nrm(ks[5], (DEPTH, Q_RANK, N_HEADS * (QK_NOPE + QK_ROPE)), Q_RANK ** -0.5),
        "kv_norm_g": gain(ks[6], (DEPTH, KV_RANK)),
        "w_ukv": nrm(ks[7], (DEPTH, KV_RANK, N_HEADS * (QK_NOPE + V_DIM)), KV_RANK ** -0.5),
        "lam_re": lam_re,
        "lam_im": lam_im,
        "log_dt": log_dt,
        "b_re": nrm(ks[11], (DEPTH, N_DIR, G, N, H), (2.0 * H) ** -0.5),
        "b_im": nrm(ks[12], (DEPTH, N_DIR, G, N, H), (2.0 * H) ** -0.5),
        "c_re": nrm(ks[13], (DEPTH, N_DIR, G, H, N), (2.0 * N) ** -0.5),
        "c_im": nrm(ks[14], (DEPTH, N_DIR, G, H, N), (2.0 * N) ** -0.5),
        "d_skip": nrm(ks[15], (DEPTH, SSM_W), 1.0),
        "w_glu": nrm(ks[16], (DEPTH, SSM_W, SSM_W), SSM_W ** -0.5),
        "b_glu": nrm(ks[17], (DEPTH, SSM_W), 0.01),
        "attn_out_g": gain(ks[18], (DEPTH, ATTN_W)),
        "ssm_out_g": gain(ks[19], (DEPTH, SSM_W)),
        "w_out": nrm(ks[20], (DEPTH, MIX_W, D_MODEL), MIX_W ** -0.5),
        "norm2_g": gain(ks[21], (DEPTH, D_MODEL)),
        "w_mlp1": nrm(ks[22], (DEPTH, D_MODEL, D_FF), D_MODEL ** -0.5),
        "w_mlp2": nrm(ks[23], (DEPTH, D_FF, D_MODEL), D_FF ** -0.5),
        "final_g": gain(ks[24], (D_MODEL,)),
    }


def reference(x_prompt, x_sample, norm1_g, w_in, q_norm_g, w_uq, kv_norm_g, w_ukv,
              lam_re, lam_im, log_dt, b_re, b_im, c_re, c_im, d_skip, w_glu, b_glu,
              attn_out_g, ssm_out_g, w_out, norm2_g, w_mlp1, w_mlp2, final_g):
    y_prompt = trunk(x_prompt, norm1_g, w_in, q_norm_g, w_uq, kv_norm_g, w_ukv,
                     lam_re, lam_im, log_dt, b_re, b_im, c_re, c_im, d_skip, w_glu, b_glu,
                     attn_out_g, ssm_out_g, w_out, norm2_g, w_mlp1, w_mlp2, final_g)
    y_sample = trunk(x_sample, norm1_g, w_in, q_norm_g, w_uq, kv_norm_g, w_ukv,
                     lam_re, lam_im, log_dt, b_re, b_im, c_re, c_im, d_skip, w_glu, b_glu,
                     attn_out_g, ssm_out_g, w_out, norm2_g, w_mlp1, w_mlp2, final_g)
    return (y_prompt, y_sample)
```

```python
import contextlib
import math
import numpy as np
import concourse.bass as bass
import concourse.mybir as mybir
from concourse.bass_utils import run_bass_kernel_spmd
from concourse.alu_op_type import AluOpType as ALU

F32 = mybir.dt.float32
BF16 = mybir.dt.bfloat16
I32 = mybir.dt.int32
AF = mybir.ActivationFunctionType

NCORES = 8
D = 1024
LP = 4096
LS = 16384
LO = 2048
EPS = 1e-6
PI = math.pi
TWO_PI = 2.0 * math.pi
CW1 = 6.28125
CW2 = TWO_PI - 6.28125

C_ID = 0
C_BM = 128
C_ME = 256
C_MO = 257
C_INV = 258
C_SGN = 259
C_OFF = 260
C_SEL = 261
C_NME = 269
C_NMO = 270
NCONST = 272


class Res:
    __slots__ = ("w", "r", "name")

    def __init__(self, name=""):
        self.w = None
        self.r = []
        self.name = name


class DSem:
    def __init__(self, sem):
        self.sem = sem
        self.n = 0


class Eng:
    def __init__(self, name, sem):
        self.name = name
        self.sem = sem
        self.ops = []
        self.n = 0
        self.seen = {}

    def _waits(self, toks):
        w = []
        for t in toks:
            if t is None:
                continue
            s, v = t
            if isinstance(s, DSem):
                v = max(v, s.n)
                s = s.sem
            if self.name == "pe" and s is self.sem:
                continue
            kk = id(s)
            if self.seen.get(kk, 0) >= v:
                continue
            self.seen[kk] = v
            w.append((s, v))
        return w

    def op(self, fn, reads=(), writes=(), extra=(), mark=True, dsem=None):
        toks = list(extra)
        for r in reads:
            toks.append(r.w)
        for r in writes:
            toks.append(r.w)
            toks.extend(r.r)
        w = self._waits(toks)
        if dsem is not None:
            dsem.n += 16
            tok = (dsem, dsem.n)
            inc = (dsem.sem, 16)
        elif mark:
            self.n += 1
            tok = (self.sem, self.n)
            inc = (self.sem, 1)
        else:
            tok = None
            inc = None
        self.ops.append((w, fn, inc))
        if tok is not None:
            for r in reads:
                r.r.append(tok)
            for r in writes:
                r.w = tok
                r.r = []
        return tok

    def emit(self, e):
        for (w, fn, inc) in self.ops:
            for (s, v) in w:
                e.wait_ge(s, v)
            inst = fn(e)
            if inc is not None:
                inst.then_inc(inc[0], inc[1])


class Ctx:
    pass


def _build(debug=None):
    nc = bass.Bass("TRN2", target_bir_lowering=False)
    st = contextlib.ExitStack()
    E = st.enter_context

    dbg_outs = {}

    def din(name, shape, dt=F32):
        return nc.dram_tensor(name, list(shape), dt, kind="ExternalInput").ap()

    def dscr(name, shape, dt):
        if debug is not None and name in debug:
            dbg_outs[name] = (list(shape), dt)
            return nc.dram_tensor(name, list(shape), dt, kind="ExternalOutput").ap()
        return nc.dram_tensor(name, list(shape), dt, kind="Internal").ap()

    def dout(name, shape, dt=F32):
        return nc.dram_tensor(name, list(shape), dt, kind="ExternalOutput").ap()

    xp = din("xp", [LP, D])
    xs = din("xs", [debug.get("LSd", LS) if debug else LS, D])
    xo = din("xo", [debug.get("LOd", LO) if debug else LO, D])
    w_in = din("w_in", [D, 928])
    wkr = din("wkr", [D, 64])
    wq_nope = din("wq_nope", [256, 512])
    wq_rope = din("wq_rope", [256, 256])
    wq_rsw = din("wq_rsw", [256, 256])
    wk_nope = din("wk_nope", [128, 512])
    wv = din("wv", [128, 512])
    g1 = din("g1", [1, D])
    gq = din("gq", [128, 2])
    gkv = din("gkv", [128, 1])
    consts = din("consts", [128, NCONST])
    posv = din("posv", [1, LS])
    lamr_p = din("lamr_p", [128, 32])
    lami_p = din("lami_p", [128, 32])
    ldt_p = din("ldt_p", [128, 32])
    bre_p = din("bre_p", [128, 512])
    bim_p = din("bim_p", [128, 512])
    cre_p = din("cre_p", [128, 512])
    cim_p = din("cim_p", [128, 512])
    dskip_p = din("dskip_p", [128, 4])
    bglu_p = din("bglu_p", [128, 4])
    gs_p = din("gs_p", [128, 4])
    w_glu = din("w_glu", [512, 512])
    woa_d = din("woa_d", [512, D])
    wos_d = din("wos_d", [512, D])
    ga_d = din("ga_d", [128, 4])
    g2 = din("g2", [1, D])
    gf_d = din("gf_d", [1, D])
    w1_d = din("w1_d", [D, 4096])
    w2_d = din("w2_d", [4096, D])

    yp = dout("yp", [LP, D])
    yo = dout("yo", [LO, D])

    tabc_S = dscr("tabc_S", [128, LS], F32)
    tabs_S = dscr("tabs_S", [128, LS], F32)
    tabc_O = dscr("tabc_O", [128, LO], F32)
    tabs_O = dscr("tabs_O", [128, LO], F32)
    uT = {"P": dscr("uT_P", [512, LP], BF16), "S": dscr("uT_S", [512, LS], BF16), "O": dscr("uT_O", [512, LO], BF16)}
    QT = {"P": dscr("QT_P", [8, 96, LP], BF16), "O": dscr("QT_O", [8, 96, LO], BF16)}
    KT = {"P": dscr("KT_P", [8, 96, LP], BF16), "S": dscr("KT_S", [8, 96, LS], BF16)}
    VV = {"P": dscr("V_P", [8, 128, LP // 128, 65], BF16), "S": dscr("V_S", [8, 128, LS // 128, 65], BF16)}

    CsT_d = dscr("CsT_d", [128, 9 * 2 * 32 * 32], BF16)
    Gm_d = dscr("Gm_d", [128, 4 * 15 * 128], BF16)
    traj = {"P": dscr("traj_P", [128, 64, LP // 8], BF16), "O": dscr("traj_O", [128, 64, LO // 8], BF16)}
    sT = {"P": dscr("sT_P", [512, LP], BF16), "O": dscr("sT_O", [512, LO], BF16)}
    ydbg = {"P": dscr("ydbg_P", [512, LP], F32), "O": dscr("ydbg_O", [512, LO], F32)}
    aT = {"P": dscr("aT_P", [8, 64, LP], F32), "O": dscr("aT_O", [8, 64, LO], F32)}
    x2s = {"P": dscr("x2_P", [LP, D], F32), "O": dscr("x2_O", [LO, D], F32)}
    h2T = {"P": dscr("h2T_P", [D, LP], BF16), "O": dscr("h2T_O", [D, LO], BF16)}
    sems = {}
    for nm in ["sp", "act", "dve", "pool", "pe"]:
        sems[nm] = E(nc.semaphore("sem_" + nm))
    SP = Eng("sp", sems["sp"])
    ACT = Eng("act", sems["act"])
    DVE = Eng("dve", sems["dve"])
    POOL = Eng("pool", sems["pool"])
    PE = Eng("pe", sems["pe"])
    ENGS = [SP, ACT, DVE, POOL, PE]
    dsems = []
    import itertools
    _uid = itertools.count()

    def new_dsem(name):
        d = DSem(E(nc.semaphore("ds_" + name)))
        dsems.append(d)
        return d

    dump_list = []

    def dump(name, ap, res, shape, dt=F32):
        if debug is None or not debug.get("dump"):
            return
        t = nc.dram_tensor("dump_" + name, list(shape), dt, kind="ExternalOutput").ap()
        dsd = new_dsem("dump_" + name)
        SP.op(lambda e: e.dma_start(out=t, in_=ap), reads=[res], dsem=dsd)
        dump_list.append(name)

    def barrier():
        toks = [(e.sem, e.n) for e in ENGS if e.n > 0]
        toks += [(d.sem, d.n) for d in dsems if d.n > 0]
        for e in ENGS:
            w = e._waits(toks)
            if w:
                e.ops.append((w, (lambda en: en.nop()), None))

    banks = [E(nc.psum_tensor("bank%d" % i, [128, 512], F32)) for i in range(8)]
    bank_res = [Res("bank%d" % i) for i in range(8)]
    bank_ctr = [0]

    bank_ids = [list(range(8))]

    def next_bank():
        ids = bank_ids[0]
        i = ids[bank_ctr[0] % len(ids)]
        bank_ctr[0] += 1
        return banks[i], bank_res[i]

    def sb(name, shape, dt):
        return E(nc.sbuf_tensor(name, list(shape), dt))

    cst = sb("cst", [128, NCONST], F32)
    cst_r = Res("cst")
    ds_c = new_dsem("cst")
    SP.op(lambda e: e.dma_start(out=cst[:], in_=consts[:]), writes=[cst_r], dsem=ds_c)
    identb = sb("identb", [128, 128], BF16)
    identb_r = Res()
    onesb = sb("onesb", [128, 128], BF16)
    onesb_r = Res()
    DVE.op(lambda e: e.tensor_copy(out=identb[:], in_=cst[:, C_ID:C_ID + 128]), reads=[cst_r], writes=[identb_r])
    DVE.op(lambda e: e.memset(onesb[:], 1.0), writes=[onesb_r])

    def phase_tables():
        with contextlib.ExitStack() as ps:
            P_ = ps.enter_context
            CH = 2048
            pos = P_(nc.sbuf_tensor("tb_pos", [128, CH], F32)); pos_r = Res()
            ang = P_(nc.sbuf_tensor("tb_ang", [128, CH], F32)); ang_r = Res()
            ki = P_(nc.sbuf_tensor("tb_ki", [128, CH], I32)); ki_r = Res()
            kf = P_(nc.sbuf_tensor("tb_kf", [128, CH], F32)); kf_r = Res()
            r1 = P_(nc.sbuf_tensor("tb_r1", [128, CH], F32)); r1_r = Res()
            m1 = P_(nc.sbuf_tensor("tb_m1", [128, CH], F32)); m1_r = Res()
            m2 = P_(nc.sbuf_tensor("tb_m2", [128, CH], F32)); m2_r = Res()
            ws = P_(nc.sbuf_tensor("tb_ws", [128, CH], F32)); ws_r = Res()
            wc = P_(nc.sbuf_tensor("tb_wc", [128, CH], F32)); wc_r = Res()
            so = [P_(nc.sbuf_tensor("tb_so%d" % i, [128, CH], F32)) for i in range(2)]; so_r = [Res(), Res()]
            co = [P_(nc.sbuf_tensor("tb_co%d" % i, [128, CH], F32)) for i in range(2)]; co_r = [Res(), Res()]
            ds_so = [new_dsem("so0"), new_dsem("so1")]
            ds_co = [new_dsem("co0"), new_dsem("co1")]
            ds_pos = new_dsem("pos")
            halfpi = P_(nc.sbuf_tensor("tb_hpi", [128, 1], F32)); halfpi_r = Res()
            DVE.op(lambda e: e.memset(halfpi[:], PI / 2), writes=[halfpi_r])
            jobs = [("S", i * CH, False) for i in range(LS // CH)] + [("O", 0, True)]
            for it, (which, base, addoff) in enumerate(jobs):
                b = it % 2
                SP.op(lambda e, base=base: e.dma_start(out=pos[:], in_=posv[0:1, base:base + CH].broadcast_to([128, CH])), writes=[pos_r], dsem=ds_pos)
                if addoff:
                    DVE.op(lambda e: e.tensor_scalar(out=ang[:], in0=pos[:], scalar1=cst[:, C_OFF:C_OFF + 1],
                                                     scalar2=cst[:, C_INV:C_INV + 1], op0=ALU.add, op1=ALU.mult),
                           reads=[pos_r, cst_r], writes=[ang_r])
                else:
                    DVE.op(lambda e: e.tensor_scalar(out=ang[:], in0=pos[:], scalar1=cst[:, C_INV:C_INV + 1],
                                                     scalar2=None, op0=ALU.mult), reads=[pos_r, cst_r], writes=[ang_r])
                DVE.op(lambda e: e.tensor_scalar(out=ki[:], in0=ang[:], scalar1=1.0 / TWO_PI, scalar2=None, op0=ALU.mult),
                       reads=[ang_r], writes=[ki_r])
                DVE.op(lambda e: e.tensor_copy(out=kf[:], in_=ki[:]), reads=[ki_r], writes=[kf_r])
                DVE.op(lambda e: e.scalar_tensor_tensor(out=r1[:], in0=kf[:], scalar=-CW1, in1=ang[:], op0=ALU.mult, op1=ALU.add),
                       reads=[kf_r, ang_r], writes=[r1_r])
                DVE.op(lambda e: e.scalar_tensor_tensor(out=r1[:], in0=kf[:], scalar=-CW2, in1=r1[:], op0=ALU.mult, op1=ALU.add),
                       reads=[kf_r], writes=[r1_r])
                DVE.op(lambda e: e.tensor_scalar(out=ws[:], in0=r1[:], scalar1=-PI, scalar2=PI, op0=ALU.max, op1=ALU.min),
                       reads=[r1_r], writes=[ws_r])
                ACT.op(lambda e: e.activation(out=wc[:], in_=ws[:], func=AF.Abs), reads=[ws_r], writes=[wc_r])
                ACT.op(lambda e, b=b: e.activation(out=so[b][:], in_=ws[:], func=AF.Sin, scale=cst[:, C_SGN:C_SGN + 1]), reads=[ws_r, cst_r], writes=[so_r[b]])
                ACT.op(lambda e, b=b: e.activation(out=co[b][:], in_=wc[:], func=AF.Sin, scale=-1.0, bias=halfpi[:, 0:1]), reads=[wc_r, halfpi_r], writes=[co_r[b]])
                dc = tabc_S if which == "S" else tabc_O
                dsn = tabs_S if which == "S" else tabs_O
                SP.op(lambda e, b=b, dc=dc, base=base: e.dma_start(out=dc[:, base:base + CH], in_=co[b][:]), reads=[co_r[b]], dsem=ds_co[b])
                SP.op(lambda e, b=b, dsn=dsn, base=base: e.dma_start(out=dsn[:, base:base + CH], in_=so[b][:]), reads=[so_r[b]], dsem=ds_so[b])
            barrier()

    def phase_A():
        with contextlib.ExitStack() as ps:
            P_ = ps.enter_context

            def T(name, shape, dt):
                return P_(nc.sbuf_tensor(name, list(shape), dt))

            win_b = T("win_b", [128, 8, 928], BF16); win_r = Res()
            wkr_b = T("wkr_b", [128, 8, 64], BF16); wkr_r = Res()
            wqn_b = T("wqn_b", [128, 2, 512], BF16); wqn_r = Res()
            wqr_b = T("wqr_b", [128, 2, 256], BF16); wqr_r = Res()
            wqs_b = T("wqs_b", [128, 2, 256], BF16); wqs_r = Res()
            wkn_b = T("wkn_b", [128, 512], BF16); wkn_r = Res()
            wv_b = T("wv_b", [128, 512], BF16); wv_r = Res()
            g1b = T("g1b", [128, D], F32); g1b_r = Res()
            gq_s = T("gq_s", [128, 2], F32); gq_r = Res()
            gkv_s = T("gkv_s", [128, 1], F32); gkv_r = Res()
            dsw = new_dsem("wA")
            POOL.op(lambda e: e.dma_start(out=win_b[:], in_=w_in.rearrange("(k p) n -> p k n", p=128)), writes=[win_r], dsem=new_dsem("pq%d" % next(_uid)))
            POOL.op(lambda e: e.dma_start(out=wkr_b[:], in_=wkr.rearrange("(k p) n -> p k n", p=128)), writes=[wkr_r], dsem=new_dsem("pq%d" % next(_uid)))
            POOL.op(lambda e: e.dma_start(out=wqn_b[:], in_=wq_nope.rearrange("(k p) n -> p k n", p=128)), writes=[wqn_r], dsem=new_dsem("pq%d" % next(_uid)))
            POOL.op(lambda e: e.dma_start(out=wqr_b[:], in_=wq_rope.rearrange("(k p) n -> p k n", p=128)), writes=[wqr_r], dsem=new_dsem("pq%d" % next(_uid)))
            POOL.op(lambda e: e.dma_start(out=wqs_b[:], in_=wq_rsw.rearrange("(k p) n -> p k n", p=128)), writes=[wqs_r], dsem=new_dsem("pq%d" % next(_uid)))
            POOL.op(lambda e: e.dma_start(out=wkn_b[:], in_=wk_nope[:, :]), writes=[wkn_r], dsem=new_dsem("pq%d" % next(_uid)))
            POOL.op(lambda e: e.dma_start(out=wv_b[:], in_=wv[:, :]), writes=[wv_r], dsem=new_dsem("pq%d" % next(_uid)))
            SP.op(lambda e: e.dma_start(out=g1b[:], in_=g1.broadcast_to([128, D])), writes=[g1b_r], dsem=dsw)
            SP.op(lambda e: e.dma_start(out=gq_s[:], in_=gq[:, :]), writes=[gq_r], dsem=dsw)
            SP.op(lambda e: e.dma_start(out=gkv_s[:], in_=gkv[:, :]), writes=[gkv_r], dsem=dsw)

            NB = 2
            NXB = 3
            xt = [T("xt%d" % i, [128, 4, D], F32) for i in range(NXB)]; xt_r = [Res() for _ in range(NXB)]
            ds_x = [new_dsem("x%d" % i) for i in range(NXB)]
            junk = T("junk", [128, D], BF16); junk_r = Res()
            ssq = [T("ssq%d" % i, [128, 4], F32) for i in range(NB)]; ssq_r = [Res() for _ in range(NB)]
            rstd = [T("rstd%d" % i, [128, 4], F32) for i in range(NB)]; rstd_r = [Res() for _ in range(NB)]
            xn = [T("xn%d" % i, [128, 4, D], BF16) for i in range(NB)]; xn_r = [Res() for _ in range(NB)]
            hT = [T("hT%d" % i, [128, 8, 512], BF16) for i in range(NB)]; hT_r = [Res() for _ in range(NB)]
            NST = 6
            stg = [T("stg%d" % i, [128, 512], BF16) for i in range(NST)]; stg_r = [Res() for _ in range(NST)]
            ds_stg = [new_dsem("stg%d" % i) for i in range(NST)]
            stg_c = [0]

            def next_stg():
                i = stg_c[0] % NST
                stg_c[0] += 1
                return stg[i], stg_r[i], ds_stg[i]

            sq = [T("sq%d" % i, [128, 2, 512], BF16) for i in range(2)]; sq_r = [Res(), Res()]
            lnv = [T("lnv%d" % i, [128, 512], F32) for i in range(2)]; lnv_r = [Res(), Res()]
            cqn = T("cqn", [128, 2, 512], BF16); cqn_r = Res()
            ckvn = T("ckvn", [128, 512], BF16); ckvn_r = Res()
            tcos = [T("tcos%d" % i, [128, 512], F32) for i in range(3)]; tcos_r = [Res() for _ in range(3)]
            tsin = [T("tsin%d" % i, [128, 512], F32) for i in range(3)]; tsin_r = [Res() for _ in range(3)]
            ds_tab = [new_dsem("tab%d" % i) for i in range(3)]
            rt1 = T("rt1", [128, 512], F32); rt1_r = Res()
            rt2 = T("rt2", [128, 512], F32); rt2_r = Res()
            vaug = [T("vaug%d" % i, [128, 8, 4, 65], BF16) for i in range(2)]; vaug_r = [Res(), Res()]
            ds_v = [new_dsem("v0"), new_dsem("v1")]
            for i in range(2):
                DVE.op(lambda e, i=i: e.memset(vaug[i][:], 1.0), writes=[vaug_r[i]])

            def proj_group(dst_bank, dst_res, lhs_fn, hb, M=128, prow=0):
                for kk in range(8):
                    last = kk == 7
                    PE.op(lambda e, kk=kk: e.matmul(dst_bank[prow:prow + M, :], lhsT=lhs_fn(kk), rhs=hT[hb][:, kk, :],
                                                    start=(kk == 0), stop=(kk == 7)),
                          reads=([hT_r[hb], win_r, wkr_r] if last else []), writes=([dst_res] if last else []),
                          extra=([hT_r[hb].w, win_r.w, wkr_r.w, dst_res.w] + list(dst_res.r) if kk == 0 else []), mark=last)

            def rms_part1(src_list, sqb):
                for i, (bk, br) in enumerate(src_list):
                    ACT.op(lambda e, bk=bk, i=i: e.activation(out=sq[sqb][:, i, :], in_=bk[:, :], func=AF.Square), reads=[br], writes=[sq_r[sqb]])

            def rms_part2(src_list, nfeat, gs, out_tile, out_res, sqb):
                n = len(src_list)
                sbk, sbr = next_bank()
                for i in range(n):
                    last = i == n - 1
                    PE.op(lambda e, i=i: e.matmul(sbk[:, :], lhsT=onesb[:, :], rhs=sq[sqb][:, i, :], start=(i == 0), stop=(i == n - 1)),
                          reads=([sq_r[sqb], onesb_r] if last else []), writes=([sbr] if last else []),
                          extra=([sq_r[sqb].w, onesb_r.w, sbr.w] + list(sbr.r) if i == 0 else []), mark=last)
                ACT.op(lambda e: e.activation(out=lnv[sqb][:], in_=sbk[:, :], func=AF.Ln, scale=1.0 / nfeat, bias=EPS), reads=[sbr], writes=[lnv_r[sqb]])
                ACT.op(lambda e: e.activation(out=lnv[sqb][:], in_=lnv[sqb][:], func=AF.Exp, scale=-0.5), writes=[lnv_r[sqb]])
                for i, (bk, br) in enumerate(src_list):
                    DVE.op(lambda e, bk=bk, i=i: e.scalar_tensor_tensor(out=out_tile(i), in0=bk[:, :], scalar=gs(i), in1=lnv[sqb][:],
                                                                          op0=ALU.mult, op1=ALU.mult),
                           reads=[br, lnv_r[sqb], gq_r, gkv_r], writes=[out_res])

            def do_set(name, xd, N, tabc, tabs, want_q, want_kv):
                nmt = N // 512
                def loads(mt):
                    xb = mt % NXB
                    t0 = mt * 512
                    POOL.op(lambda e, xb=xb, t0=t0: e.dma_start(out=xt[xb][:], in_=xd[t0:t0 + 512, :].rearrange("(t p) d -> p t d", p=128)),
                           writes=[xt_r[xb]], dsem=ds_x[xb])
                    tb = mt % 3
                    POOL.op(lambda e, tb=tb, t0=t0: e.dma_start(out=tcos[tb][:], in_=tabc[:, t0:t0 + 512]), writes=[tcos_r[tb]], dsem=ds_tab[tb])
                    POOL.op(lambda e, tb=tb, t0=t0: e.dma_start(out=tsin[tb][:], in_=tabs[:, t0:t0 + 512]), writes=[tsin_r[tb]], dsem=ds_tab[tb])

                def stage1(mt):
                    b = mt % NB
                    xb = mt % NXB
                    t0 = mt * 512
                    for t in range(4):
                        ACT.op(lambda e, b=b, xb=xb, t=t: e.activation(out=junk[:], in_=xt[xb][:, t, :], func=AF.Square, accum_out=ssq[b][:, t:t + 1]),
                               reads=[xt_r[xb]], writes=[junk_r, ssq_r[b]])
                    ACT.op(lambda e, b=b: e.activation(out=rstd[b][:], in_=ssq[b][:], func=AF.Ln, scale=1.0 / D, bias=EPS), reads=[ssq_r[b]], writes=[rstd_r[b]])
                    ACT.op(lambda e, b=b: e.activation(out=rstd[b][:], in_=rstd[b][:], func=AF.Exp, scale=-0.5), writes=[rstd_r[b]])
                    for t in range(4):
                        DVE.op(lambda e, b=b, xb=xb, t=t: e.scalar_tensor_tensor(out=xn[b][:, t, :], in0=xt[xb][:, t, :], scalar=rstd[b][:, t:t + 1],
                                                                            in1=g1b[:], op0=ALU.mult, op1=ALU.mult),
                               reads=[xt_r[xb], rstd_r[b], g1b_r], writes=[xn_r[b]])

                def stage1b(mt):
                    b = mt % NB
                    for kk2 in range(4):
                        bk, br = next_bank()
                        bkb = bk[:, :].bitcast(BF16)
                        cnt = 0
                        for kq in range(2):
                            kk = 2 * kk2 + kq
                            for t in range(4):
                                cnt += 1
                                last = cnt == 8
                                PE.op(lambda e, kk=kk, t=t, kq=kq, bkb=bkb, b=b: e.transpose(out=bkb[:, kq * 512 + t * 128: kq * 512 + (t + 1) * 128],
                                                                                             in_=xn[b][:, t, kk * 128:(kk + 1) * 128], identity=identb[:, :]),
                                      reads=([xn_r[b], identb_r] if last else []), writes=([br] if last else []),
                                      extra=([xn_r[b].w, identb_r.w, br.w] + list(br.r) if cnt == 1 else []), mark=last)
                        eng = ACT if kk2 % 2 == 0 else DVE
                        if eng is ACT:
                            ACT.op(lambda e, kk2=kk2, bkb=bkb, b=b: e.activation(out=hT[b][:, 2 * kk2:2 * kk2 + 2, :].rearrange("p a b -> p (a b)"), in_=bkb, func=AF.Copy),
                                   reads=[br], writes=[hT_r[b]])
                        else:
                            DVE.op(lambda e, kk2=kk2, bkb=bkb, b=b: e.tensor_copy(out=hT[b][:, 2 * kk2:2 * kk2 + 2, :].rearrange("p a b -> p (a b)"), in_=bkb),
                                   reads=[br], writes=[hT_r[b]])

                def stage2(mt):
                    b = mt % NB
                    t0 = mt * 512
                    tb = mt % 3
                    qsrcs = []
                    if want_q:
                        for c2 in range(2):
                            bk, br = banks[5 + c2], bank_res[5 + c2]
                            proj_group(bk, br, lambda kk, c2=c2: win_b[:, kk, c2 * 128:(c2 + 1) * 128], b)
                            qsrcs.append((bk, br))
                        rms_part1(qsrcs, 0)
                    kvsrc = None
                    if want_kv:
                        bk, br = banks[7], bank_res[7]
                        proj_group(bk, br, lambda kk: win_b[:, kk, 256:384], b)
                        kvsrc = [(bk, br)]
                        rms_part1(kvsrc, 1)
                    for c4 in range(4):
                        bk, br = next_bank()
                        proj_group(bk, br, lambda kk, c4=c4: win_b[:, kk, 416 + c4 * 128: 416 + (c4 + 1) * 128], b)
                        sg, sgr, sgd = next_stg()
                        ACT.op(lambda e, sg=sg, bk=bk: e.activation(out=sg[:], in_=bk[:, :], func=AF.Copy), reads=[br], writes=[sgr])
                        SP.op(lambda e, sg=sg, c4=c4, t0=t0: e.dma_start(out=uT[name][c4 * 128:(c4 + 1) * 128, t0:t0 + 512], in_=sg[:]), reads=[sgr], dsem=sgd)
                    if want_kv:
                        bk1, br1 = next_bank()
                        bk2, br2 = next_bank()
                        proj_group(bk1, br1, lambda kk: wkr_b[:, kk, 0:32], b, M=32)
                        proj_group(bk2, br2, lambda kk: wkr_b[:, kk, 32:64], b, M=32)
                        DVE.op(lambda e, bk1=bk1, tb=tb: e.tensor_tensor(out=rt1[0:32, :], in0=bk1[0:32, :], in1=tcos[tb][0:32, :], op=ALU.mult), reads=[br1, tcos_r[tb]], writes=[rt1_r])
                        DVE.op(lambda e, bk2=bk2, tb=tb: e.tensor_tensor(out=rt2[0:32, :], in0=bk2[0:32, :], in1=tsin[tb][0:32, :], op=ALU.mult), reads=[br2, tsin_r[tb]], writes=[rt2_r])
                        sg, sgr, sgd = next_stg()
                        DVE.op(lambda e, sg=sg: e.tensor_tensor(out=sg[0:32, :], in0=rt1[0:32, :], in1=rt2[0:32, :], op=ALU.add), reads=[rt1_r, rt2_r], writes=[sgr])
                        for h in range(8):
                            SP.op(lambda e, sg=sg, h=h, t0=t0: e.dma_start(out=KT[name][h, 64:96, t0:t0 + 512], in_=sg[0:32, :]), reads=[sgr], dsem=sgd)
                    if want_q:
                        rms_part2(qsrcs, 256, lambda i: gq_s[:, i:i + 1], lambda i: cqn[:, i, :], cqn_r, 0)
                    if want_kv:
                        rms_part2(kvsrc, 128, lambda i: gkv_s[:, 0:1], lambda i: ckvn[:, :], ckvn_r, 1)
                    if want_q:
                        for hp in range(4):
                            bk, br = next_bank()
                            for kc in range(2):
                                last = kc == 1
                                PE.op(lambda e, kc=kc, hp=hp, bk=bk: e.matmul(bk[:, :], lhsT=wqn_b[:, kc, hp * 128:(hp + 1) * 128], rhs=cqn[:, kc, :], start=(kc == 0), stop=(kc == 1)),
                                      reads=([cqn_r, wqn_r] if last else []), writes=([br] if last else []),
                                      extra=([cqn_r.w, wqn_r.w, br.w] + list(br.r) if kc == 0 else []), mark=last)
                            sg, sgr, sgd = next_stg()
                            ACT.op(lambda e, sg=sg, bk=bk: e.activation(out=sg[:], in_=bk[:, :], func=AF.Copy), reads=[br], writes=[sgr])
                            for hh in range(2):
                                SP.op(lambda e, sg=sg, hp=hp, hh=hh, t0=t0: e.dma_start(out=QT[name][2 * hp + hh, 0:64, t0:t0 + 512], in_=sg[hh * 64:(hh + 1) * 64, :]),
                                      reads=[sgr], dsem=sgd)
                        for hg in range(2):
                            bk1, br1 = next_bank()
                            bk2, br2 = next_bank()
                            for (bk, br, wt, wr) in ((bk1, br1, wqr_b, wqr_r), (bk2, br2, wqs_b, wqs_r)):
                                for kc in range(2):
                                    last = kc == 1
                                    PE.op(lambda e, kc=kc, hg=hg, bk=bk, wt=wt: e.matmul(bk[:, :], lhsT=wt[:, kc, hg * 128:(hg + 1) * 128], rhs=cqn[:, kc, :], start=(kc == 0), stop=(kc == 1)),
                                          reads=([cqn_r, wr] if last else []), writes=([br] if last else []),
                                          extra=([cqn_r.w, wr.w, br.w] + list(br.r) if kc == 0 else []), mark=last)
                            DVE.op(lambda e, bk1=bk1, tb=tb: e.tensor_tensor(out=rt1[:], in0=bk1[:, :], in1=tcos[tb][:], op=ALU.mult), reads=[br1, tcos_r[tb]], writes=[rt1_r])
                            DVE.op(lambda e, bk2=bk2, tb=tb: e.tensor_tensor(out=rt2[:], in0=bk2[:, :], in1=tsin[tb][:], op=ALU.mult), reads=[br2, tsin_r[tb]], writes=[rt2_r])
                            sg, sgr, sgd = next_stg()
                            DVE.op(lambda e, sg=sg: e.tensor_tensor(out=sg[:], in0=rt1[:], in1=rt2[:], op=ALU.add), reads=[rt1_r, rt2_r], writes=[sgr])
                            for hh in range(4):
                                SP.op(lambda e, sg=sg, hg=hg, hh=hh, t0=t0: e.dma_start(out=QT[name][4 * hg + hh, 64:96, t0:t0 + 512], in_=sg[hh * 32:(hh + 1) * 32, :]),
                                      reads=[sgr], dsem=sgd)
                    if want_kv:
                        for hp in range(4):
                            bk, br = next_bank()
                            PE.op(lambda e, hp=hp, bk=bk: e.matmul(bk[:, :], lhsT=wkn_b[:, hp * 128:(hp + 1) * 128], rhs=ckvn[:, :], start=True, stop=True),
                                  reads=[ckvn_r, wkn_r], writes=[br])
                            sg, sgr, sgd = next_stg()
                            ACT.op(lambda e, sg=sg, bk=bk: e.activation(out=sg[:], in_=bk[:, :], func=AF.Copy), reads=[br], writes=[sgr])
                            for hh in range(2):
                                SP.op(lambda e, sg=sg, hp=hp, hh=hh, t0=t0: e.dma_start(out=KT[name][2 * hp + hh, 0:64, t0:t0 + 512], in_=sg[hh * 64:(hh + 1) * 64, :]),
                                      reads=[sgr], dsem=sgd)
                        vb = mt % 2
                        for t in range(4):
                            bk, br = next_bank()
                            PE.op(lambda e, t=t, bk=bk: e.matmul(bk[:, :], lhsT=ckvn[:, t * 128:(t + 1) * 128], rhs=wv_b[:, :], start=True, stop=True),
                                  reads=[ckvn_r, wv_r], writes=[br])
                            ACT.op(lambda e, t=t, bk=bk, vb=vb: e.activation(out=vaug[vb][:, :, t, 0:64], in_=bk[:, :].rearrange("p (h f) -> p h f", h=8), func=AF.Copy),
                                   reads=[br], writes=[vaug_r[vb]])
                        tile0 = t0 // 128
                        SP.op(lambda e, vb=vb, tile0=tile0: e.dma_start(out=VV[name][:, :, tile0:tile0 + 4, :].rearrange("h p t f -> p h (t f)"), in_=vaug[vb][:].rearrange("p h t f -> p h (t f)")),
                              reads=[vaug_r[vb]], dsem=ds_v[vb])

                loads(0)
                if nmt > 1:
                    loads(1)
                stage1(0)
                stage1b(0)
                for mt in range(nmt):
                    if mt + 2 < nmt:
                        loads(mt + 2)
                    if mt + 1 < nmt:
                        stage1(mt + 1)
                    stage2(mt)
                    if mt + 1 < nmt:
                        stage1b(mt + 1)

            bank_ids[0] = list(range(5))
            todo = debug.get("sets", ["P", "S", "O"]) if debug else ["P", "S", "O"]
            if "P" in todo:
                do_set("P", xp, debug.get("NP", LP) if debug else LP, tabc_S, tabs_S, True, True)
            if "S" in todo:
                do_set("S", xs, debug.get("NS", LS) if debug else LS, tabc_S, tabs_S, False, True)
            if "O" in todo:
                do_set("O", xo, debug.get("NO", LO) if debug else LO, tabc_O, tabs_O, True, False)
            bank_ids[0] = list(range(8))
            barrier()


    def phase_S():
        with contextlib.ExitStack() as ps:
            P_ = ps.enter_context

            def T(name, shape, dt):
                return P_(nc.sbuf_tensor(name, list(shape), dt))

            A1 = T("A1", [128, 64], F32); A1_r = Res()
            A2 = T("A2", [128, 64], F32); A2_r = Res()
            snap = T("snap", [128, 8, 64], F32); snap_r = Res()
            dsk = T("dsk", [128, 4], F32); dsk_r = Res()
            bgl = T("bgl", [128, 4], F32); bgl_r = Res()
            gss = T("gss", [128, 4], F32); gss_r = Res()
            wglu_b = T("wglu_b", [128, 4, 512], BF16); wglu_r = Res()
            zerob = T("zerob", [128, 128], BF16); zerob_r = Res()
            dsp = new_dsem("sprm")
            SP.op(lambda e: e.dma_start(out=dsk[:], in_=dskip_p[:, :]), writes=[dsk_r], dsem=dsp)
            SP.op(lambda e: e.dma_start(out=bgl[:], in_=bglu_p[:, :]), writes=[bgl_r], dsem=dsp)
            SP.op(lambda e: e.dma_start(out=gss[:], in_=gs_p[:, :]), writes=[gss_r], dsem=dsp)
            POOL.op(lambda e: e.dma_start(out=wglu_b[:], in_=w_glu.rearrange("(k p) n -> p k n", p=128)), writes=[wglu_r], dsem=new_dsem("pq%d" % next(_uid)))
            DVE.op(lambda e: e.memset(zerob[:], 0.0), writes=[zerob_r])
            DVE.op(lambda e: e.memset(snap[:], 0.0), writes=[snap_r])

            with contextlib.ExitStack() as ps2:
                P2 = ps2.enter_context

                def T2(name, shape, dt=F32):
                    return P2(nc.sbuf_tensor(name, list(shape), dt))

                BsT = T2("BsT", [128, 2, 4, 8, 2, 128], BF16); BsT_r = Res()
                with contextlib.ExitStack() as ps3:
                    P3 = ps3.enter_context

                    def T3(name, shape, dt=F32):
                        return P3(nc.sbuf_tensor(name, list(shape), dt)), Res()

                    CsT, CsT_r = T3("CsT", [128, 9, 2, 32, 32], BF16)
                    Gm, Gm_r = T3("Gm", [128, 4, 15, 128], BF16)
                    DVE.op(lambda e: e.memset(Gm[:], 0.0), writes=[Gm_r])
                    lamr, lamr_r = T3("lamr", [128, 32]); lami, lami_r = T3("lami", [128, 32]); ldt, ldt_r = T3("ldt", [128, 32])
                    bre, bre_r = T3("bre", [128, 32, 16]); bim, bim_r = T3("bim", [128, 32, 16])
                    cre, cre_r = T3("cre", [128, 32, 16]); cim, cim_r = T3("cim", [128, 32, 16])
                    SP.op(lambda e: e.dma_start(out=lamr[:], in_=lamr_p[:, :]), writes=[lamr_r], dsem=dsp)
                    SP.op(lambda e: e.dma_start(out=lami[:], in_=lami_p[:, :]), writes=[lami_r], dsem=dsp)
                    SP.op(lambda e: e.dma_start(out=ldt[:], in_=ldt_p[:, :]), writes=[ldt_r], dsem=dsp)
                    SP.op(lambda e: e.dma_start(out=bre[:].rearrange("p a b -> p (a b)"), in_=bre_p[:, :]), writes=[bre_r], dsem=dsp)
                    SP.op(lambda e: e.dma_start(out=bim[:].rearrange("p a b -> p (a b)"), in_=bim_p[:, :]), writes=[bim_r], dsem=dsp)
                    SP.op(lambda e: e.dma_start(out=cre[:].rearrange("p a b -> p (a b)"), in_=cre_p[:, :]), writes=[cre_r], dsem=dsp)
                    SP.op(lambda e: e.dma_start(out=cim[:].rearrange("p a b -> p (a b)"), in_=cim_p[:, :]), writes=[cim_r], dsem=dsp)

                    def V(fn, reads, writes):
                        return DVE.op(fn, reads=reads, writes=writes)

                    dt_, dt_r = T3("dt_", [128, 32]); th, th_r = T3("th", [128, 32]); rl, rl_r = T3("rl", [128, 32])
                    er, er_r = T3("er", [128, 32])
                    ACT.op(lambda e: e.activation(out=dt_[:], in_=ldt[:], func=AF.Exp), reads=[ldt_r], writes=[dt_r])
                    V(lambda e: e.tensor_tensor(out=th[:], in0=lami[:], in1=dt_[:], op=ALU.mult), [lami_r, dt_r], [th_r])
                    V(lambda e: e.tensor_tensor(out=rl[:], in0=lamr[:], in1=dt_[:], op=ALU.mult), [lamr_r, dt_r], [rl_r])
                    ACT.op(lambda e: e.activation(out=er[:], in_=rl[:], func=AF.Exp), reads=[rl_r], writes=[er_r])
                    ki, ki_r = T3("ski", [128, 32], I32); kf, kf_r = T3("skf", [128, 32]); r1, r1_r = T3("sr1", [128, 32])
                    m1, m1_r = T3("sm1", [128, 32]); m2, m2_r = T3("sm2", [128, 32]); ws, ws_r = T3("sws", [128, 32]); wc, wc_r = T3("swc", [128, 32])
                    sn, sn_r = T3("ssn", [128, 32]); cs_, cs_r = T3("scs", [128, 32])
                    V(lambda e: e.tensor_scalar(out=ki[:], in0=th[:], scalar1=1.0 / TWO_PI, scalar2=None, op0=ALU.mult), [th_r], [ki_r])
                    V(lambda e: e.tensor_copy(out=kf[:], in_=ki[:]), [ki_r], [kf_r])
                    V(lambda e: e.scalar_tensor_tensor(out=r1[:], in0=kf[:], scalar=-CW1, in1=th[:], op0=ALU.mult, op1=ALU.add), [kf_r, th_r], [r1_r])
                    V(lambda e: e.scalar_tensor_tensor(out=r1[:], in0=kf[:], scalar=-CW2, in1=r1[:], op0=ALU.mult, op1=ALU.add), [kf_r], [r1_r])
                    V(lambda e: e.tensor_scalar(out=m1[:], in0=r1[:], scalar1=PI, scalar2=-TWO_PI, op0=ALU.is_gt, op1=ALU.mult), [r1_r], [m1_r])
                    V(lambda e: e.tensor_scalar(out=m2[:], in0=r1[:], scalar1=-PI, scalar2=TWO_PI, op0=ALU.is_lt, op1=ALU.mult), [r1_r], [m2_r])
                    V(lambda e: e.tensor_tensor(out=m1[:], in0=m1[:], in1=m2[:], op=ALU.add), [m2_r], [m1_r])
                    V(lambda e: e.tensor_tensor(out=ws[:], in0=r1[:], in1=m1[:], op=ALU.add), [r1_r, m1_r], [ws_r])
                    V(lambda e: e.tensor_scalar(out=m2[:], in0=r1[:], scalar1=PI / 2, scalar2=-TWO_PI, op0=ALU.is_gt, op1=ALU.mult), [r1_r], [m2_r])
                    V(lambda e: e.scalar_tensor_tensor(out=wc[:], in0=r1[:], scalar=PI / 2, in1=m2[:], op0=ALU.add, op1=ALU.add), [r1_r, m2_r], [wc_r])
                    ACT.op(lambda e: e.activation(out=sn[:], in_=ws[:], func=AF.Sin), reads=[ws_r], writes=[sn_r])
                    ACT.op(lambda e: e.activation(out=cs_[:], in_=wc[:], func=AF.Sin), reads=[wc_r], writes=[cs_r])
                    pwr, pwr_r = T3("pwr", [128, 9, 32]); pwi, pwi_r = T3("pwi", [128, 9, 32])
                    V(lambda e: e.memset(pwr[:, 0, :], 1.0), [], [pwr_r])
                    V(lambda e: e.memset(pwi[:, 0, :], 0.0), [], [pwi_r])
                    V(lambda e: e.tensor_tensor(out=pwr[:, 1, :], in0=er[:], in1=cs_[:], op=ALU.mult), [er_r, cs_r], [pwr_r])
                    V(lambda e: e.tensor_tensor(out=pwi[:, 1, :], in0=er[:], in1=sn[:], op=ALU.mult), [er_r, sn_r], [pwi_r])
                    t1, t1_r = T3("t1", [128, 32]); t2, t2_r = T3("t2", [128, 32])
                    for m in range(1, 8):
                        V(lambda e, m=m: e.tensor_tensor(out=t1[:], in0=pwr[:, m, :], in1=pwr[:, 1, :], op=ALU.mult), [pwr_r], [t1_r])
                        V(lambda e, m=m: e.tensor_tensor(out=t2[:], in0=pwi[:, m, :], in1=pwi[:, 1, :], op=ALU.mult), [pwi_r], [t2_r])
                        V(lambda e, m=m: e.tensor_tensor(out=pwr[:, m + 1, :], in0=t1[:], in1=t2[:], op=ALU.subtract), [t1_r, t2_r], [pwr_r])
                        V(lambda e, m=m: e.tensor_tensor(out=t1[:], in0=pwr[:, m, :], in1=pwi[:, 1, :], op=ALU.mult), [pwr_r, pwi_r], [t1_r])
                        V(lambda e, m=m: e.tensor_tensor(out=t2[:], in0=pwi[:, m, :], in1=pwr[:, 1, :], op=ALU.mult), [pwr_r, pwi_r], [t2_r])
                        V(lambda e, m=m: e.tensor_tensor(out=pwi[:, m + 1, :], in0=t1[:], in1=t2[:], op=ALU.add), [t1_r, t2_r], [pwi_r])
                    V(lambda e: e.tensor_copy(out=A1[:, 0:32], in_=pwr[:, 8, :]), [pwr_r], [A1_r])
                    V(lambda e: e.tensor_copy(out=A1[:, 32:64], in_=pwr[:, 8, :]), [pwr_r], [A1_r])
                    V(lambda e: e.tensor_scalar(out=A2[:, 0:32], in0=pwi[:, 8, :], scalar1=-1.0, scalar2=None, op0=ALU.mult), [pwi_r], [A2_r])
                    V(lambda e: e.tensor_copy(out=A2[:, 32:64], in_=pwi[:, 8, :]), [pwi_r], [A2_r])
                    nr, nr_r = T3("nr", [128, 32]); den, den_r = T3("den", [128, 32]); qre, qre_r = T3("qre", [128, 32]); qim, qim_r = T3("qim", [128, 32])
                    V(lambda e: e.tensor_scalar(out=nr[:], in0=pwr[:, 1, :], scalar1=-1.0, scalar2=None, op0=ALU.add), [pwr_r], [nr_r])
                    V(lambda e: e.tensor_tensor(out=t1[:], in0=lamr[:], in1=lamr[:], op=ALU.mult), [lamr_r], [t1_r])
                    V(lambda e: e.tensor_tensor(out=t2[:], in0=lami[:], in1=lami[:], op=ALU.mult), [lami_r], [t2_r])
                    V(lambda e: e.tensor_tensor(out=den[:], in0=t1[:], in1=t2[:], op=ALU.add), [t1_r, t2_r], [den_r])
                    V(lambda e: e.reciprocal(out=den[:], in_=den[:]), [], [den_r])
                    V(lambda e: e.tensor_tensor(out=t1[:], in0=nr[:], in1=lamr[:], op=ALU.mult), [nr_r, lamr_r], [t1_r])
                    V(lambda e: e.tensor_tensor(out=t2[:], in0=pwi[:, 1, :], in1=lami[:], op=ALU.mult), [pwi_r, lami_r], [t2_r])
                    V(lambda e: e.tensor_tensor(out=t1[:], in0=t1[:], in1=t2[:], op=ALU.add), [t2_r], [t1_r])
                    V(lambda e: e.tensor_tensor(out=qre[:], in0=t1[:], in1=den[:], op=ALU.mult), [t1_r, den_r], [qre_r])
                    V(lambda e: e.tensor_tensor(out=t1[:], in0=pwi[:, 1, :], in1=lamr[:], op=ALU.mult), [pwi_r, lamr_r], [t1_r])
                    V(lambda e: e.tensor_tensor(out=t2[:], in0=nr[:], in1=lami[:], op=ALU.mult), [nr_r, lami_r], [t2_r])
                    V(lambda e: e.tensor_tensor(out=t1[:], in0=t1[:], in1=t2[:], op=ALU.subtract), [t2_r], [t1_r])
                    V(lambda e: e.tensor_tensor(out=qim[:], in0=t1[:], in1=den[:], op=ALU.mult), [t1_r, den_r], [qim_r])

                    dump("pwr", pwr[:].rearrange("p a b -> p (a b)"), pwr_r, [128, 288])
                    dump("pwi", pwi[:].rearrange("p a b -> p (a b)"), pwi_r, [128, 288])
                    dump("qre", qre[:], qre_r, [128, 32])
                    dump("qim", qim[:], qim_r, [128, 32])
                    dump("sn", sn[:], sn_r, [128, 32])
                    dump("cs", cs_[:], cs_r, [128, 32])
                    dump("th", th[:], th_r, [128, 32])
                    dump("er", er[:], er_r, [128, 32])
                    def bc(ap32):
                        return ap32.unsqueeze(2).broadcast_to([128, 32, 16])

                    def cmul(outr, outr_r, outi, outi_r, sr, si, sres, xr, xr_r, xi, xi_r, ta, ta_r, tb, tb_r):
                        V(lambda e: e.tensor_tensor(out=ta[:], in0=xr[:], in1=bc(sr), op=ALU.mult), [xr_r] + sres, [ta_r])
                        V(lambda e: e.tensor_tensor(out=tb[:], in0=xi[:], in1=bc(si), op=ALU.mult), [xi_r] + sres, [tb_r])
                        V(lambda e: e.tensor_tensor(out=outr[:], in0=ta[:], in1=tb[:], op=ALU.subtract), [ta_r, tb_r], [outr_r])
                        V(lambda e: e.tensor_tensor(out=ta[:], in0=xi[:], in1=bc(sr), op=ALU.mult), [xi_r] + sres, [ta_r])
                        V(lambda e: e.tensor_tensor(out=tb[:], in0=xr[:], in1=bc(si), op=ALU.mult), [xr_r] + sres, [tb_r])
                        V(lambda e: e.tensor_tensor(out=outi[:], in0=ta[:], in1=tb[:], op=ALU.add), [ta_r, tb_r], [outi_r])

                    ta, ta_r = T3("ta", [128, 32, 16]); tb, tb_r = T3("tb", [128, 32, 16])
                    bbr, bbr_r = T3("bbr", [128, 32, 16]); bbi, bbi_r = T3("bbi", [128, 32, 16])
                    cmul(bbr, bbr_r, bbi, bbi_r, qre[:], qim[:], [qre_r, qim_r], bre, bre_r, bim, bim_r, ta, ta_r, tb, tb_r)
                    wr, wr_r = T3("wr", [128, 32, 16]); wi, wi_r = T3("wi", [128, 32, 16])
                    W2 = [[T3("W2_%d_%d" % (m, ri), [128, 32, 2, 16], BF16) for ri in range(2)] for m in range(8)]
                    for m in range(8):
                        cmul(wr, wr_r, wi, wi_r, pwr[:, m, :], pwi[:, m, :], [pwr_r, pwi_r], bbr, bbr_r, bbi, bbi_r, ta, ta_r, tb, tb_r)
                        for ri, (src, src_r) in enumerate(((wr, wr_r), (wi, wi_r))):
                            for two, col in ((0, C_ME), (1, C_MO)):
                                V(lambda e, m=m, ri=ri, src=src, two=two, col=col: e.tensor_scalar(out=W2[m][ri][0][:, :, two, :], in0=src[:], scalar1=cst[:, col:col + 1], scalar2=None, op0=ALU.mult),
                                  [src_r, cst_r], [W2[m][ri][1]])
                    for m in range(9):
                        cmul(wr, wr_r, wi, wi_r, pwr[:, m, :], pwi[:, m, :], [pwr_r, pwi_r], cre, cre_r, cim, cim_r, ta, ta_r, tb, tb_r)
                        for ri, (src, src_r) in enumerate(((wr, wr_r), (wi, wi_r))):
                            for two, col in ((0, C_ME if ri == 0 else C_NME), (1, C_MO if ri == 0 else C_NMO)):
                                V(lambda e, m=m, ri=ri, src=src, two=two, col=col: e.tensor_scalar(
                                    out=CsT[:, m, ri, :, two * 16:(two + 1) * 16], in0=src[:], scalar1=cst[:, col:col + 1], scalar2=None, op0=ALU.mult),
                                  [src_r, cst_r], [CsT_r])
                    dump("bbr", bbr[:].rearrange("p a b -> p (a b)"), bbr_r, [128, 512])
                    dump("bre", bre[:].rearrange("p a b -> p (a b)"), bre_r, [128, 512])
                    dump("W2_0_0", W2[0][0][0][:].rearrange("p a b c -> p (a b c)"), W2[0][0][1], [128, 1024], BF16)
                    dump("W2_7_1", W2[7][1][0][:].rearrange("p a b c -> p (a b c)"), W2[7][1][1], [128, 1024], BF16)
                    cnt = 0
                    for d in range(2):
                        for blk in range(4):
                            for mm in range(2):
                                bk, br = next_bank()
                                bkb = bk[:, :].bitcast(BF16)
                                n_in = 0
                                for m4 in range(4):
                                    m = mm * 4 + m4
                                    for ri in range(2):
                                        n_in += 1
                                        last = n_in == 8
                                        src = W2[m][ri][0][:, d * 16 + blk * 4: d * 16 + blk * 4 + 4, :, :].rearrange("p a b c -> p (a b c)")
                                        PE.op(lambda e, src=src, bkb=bkb, o=(m4 * 2 + ri) * 128: e.transpose(out=bkb[:, o:o + 128], in_=src, identity=identb[:, :]),
                                              reads=([W2[mx][rx][1] for mx in range(8) for rx in range(2)] + [identb_r] if last else []), writes=([br] if last else []),
                                              extra=([W2[mx][rx][1].w for mx in range(8) for rx in range(2)] + [br.w] + list(br.r) if n_in == 1 else []), mark=last)
                                dst = BsT[:, d, blk, mm * 4:(mm + 1) * 4, :, :].rearrange("p a b c -> p (a b c)")
                                if cnt % 2 == 0:
                                    ACT.op(lambda e, dst=dst, bkb=bkb: e.activation(out=dst, in_=bkb, func=AF.Copy), reads=[br], writes=[BsT_r])
                                else:
                                    DVE.op(lambda e, dst=dst, bkb=bkb: e.tensor_copy(out=dst, in_=bkb), reads=[br], writes=[BsT_r])
                                cnt += 1
                    dump("BsT", BsT[:].rearrange("p a b c d e -> p (a b c d e)"), BsT_r, [128, 2 * 4 * 8 * 2 * 128], BF16)
                    g0, g0_r = T3("g0", [128, 128])
                    for blk in range(4):
                        for d in range(2):
                            for kq in range(8):
                                bk, br = next_bank()
                                for ri in range(2):
                                    lhs = W2[kq][ri][0][:, d * 16 + blk * 4: d * 16 + blk * 4 + 4, :, :].rearrange("p a b c -> p (a b c)")
                                    rhs = CsT[:, 0, ri, d * 16 + blk * 4: d * 16 + blk * 4 + 4, :].rearrange("p a b -> p (a b)")
                                    last = ri == 1
                                    PE.op(lambda e, lhs=lhs, rhs=rhs, bk=bk, ri=ri: e.matmul(bk[:, 0:128], lhsT=lhs, rhs=rhs, start=(ri == 0), stop=(ri == 1)),
                                          reads=([W2[kq][0][1], W2[kq][1][1], CsT_r] if last else []), writes=([br] if last else []),
                                          extra=([W2[kq][0][1].w, W2[kq][1][1].w, CsT_r.w, br.w] + list(br.r) if ri == 0 else []), mark=last)
                                if kq == 0:
                                    if d == 0:
                                        V(lambda e, bk=bk: e.tensor_tensor(out=g0[:], in0=bk[:, 0:128], in1=cst[:, C_BM:C_BM + 128], op=ALU.mult), [br, cst_r], [g0_r])
                                        V(lambda e, blk=blk: e.scalar_tensor_tensor(out=g0[:], in0=cst[:, C_ID:C_ID + 128], scalar=dsk[:, blk:blk + 1], in1=g0[:], op0=ALU.mult, op1=ALU.add),
                                          [cst_r, dsk_r], [g0_r])
                                    else:
                                        V(lambda e, bk=bk: e.tensor_tensor(out=ta[:, 0:8, :].rearrange("p a b -> p (a b)"), in0=bk[:, 0:128], in1=cst[:, C_BM:C_BM + 128], op=ALU.mult), [br, cst_r], [ta_r])
                                        V(lambda e, blk=blk: e.tensor_tensor(out=Gm[:, blk, 0, :], in0=g0[:], in1=ta[:, 0:8, :].rearrange("p a b -> p (a b)"), op=ALU.add), [g0_r, ta_r], [Gm_r])
                                else:
                                    kk = kq if d == 0 else 7 + kq
                                    V(lambda e, bk=bk, blk=blk, kk=kk: e.tensor_tensor(out=Gm[:, blk, kk, :], in0=bk[:, 0:128], in1=cst[:, C_BM:C_BM + 128], op=ALU.mult), [br, cst_r], [Gm_r])
                    ds_sp = new_dsem("spill")
                    SP.op(lambda e: e.dma_start(out=CsT_d[:, :], in_=CsT[:].rearrange("p a b c d -> p (a b c d)")), reads=[CsT_r], dsem=ds_sp)
                    SP.op(lambda e: e.dma_start(out=Gm_d[:, :], in_=Gm[:].rearrange("p a b c -> p (a b c)")), reads=[Gm_r], dsem=ds_sp)
                    barrier()

                CHS = 64
                Bbuf = [T2("Bbuf%d" % i, [128, 64, CHS]) for i in range(2)]; Bbuf_r = [Res(), Res()]
                Xb = T2("Xb", [128, 96, CHS + 1]); Xb_r = Res()
                T1 = T2("T1s", [128, 64]); T1_r = Res()
                T2s = T2("T2s", [128, 64]); T2s_r = Res()
                Us = T2("Us", [128, 64]); Us_r = Res()
                ufw = [T2("ufw%d" % i, [128, 4, CHS * 8], BF16) for i in range(2)]; ufw_r = [Res(), Res()]
                ubw = [T2("ubw%d" % i, [128, 4, CHS * 8], BF16) for i in range(2)]; ubw_r = [Res(), Res()]
                ds_uf = [new_dsem("uf0"), new_dsem("uf1")]
                ds_ub = [new_dsem("ub0"), new_dsem("ub1")]
                tjb = [T2("tjb%d" % i, [128, 64, CHS], BF16) for i in range(2)]; tjb_r = [Res(), Res()]
                ds_tj = [new_dsem("tj0"), new_dsem("tj1")]

                sc_ctr = [0]

                def scan_bank():
                    i = 4 + sc_ctr[0] % 4
                    sc_ctr[0] += 1
                    return banks[i], bank_res[i]

                def produce_B(name, N, kc):
                    b = kc % 2
                    tf0 = kc * CHS * 8
                    tb0 = N - (kc + 1) * CHS * 8
                    POOL.op(lambda e, b=b, tf0=tf0: e.dma_start(out=ufw[b][:], in_=uT[name][:, tf0:tf0 + CHS * 8].rearrange("(k p) t -> p k t", p=128)), writes=[ufw_r[b]], dsem=ds_uf[b])
                    POOL.op(lambda e, b=b, tb0=tb0: e.dma_start(out=ubw[b][:], in_=uT[name][:, tb0:tb0 + CHS * 8].rearrange("(k p) t -> p k t", p=128)), writes=[ubw_r[b]], dsem=ds_ub[b])
                    for d in range(2):
                        for ri in range(2):
                            bks = [scan_bank() for _ in range(4)]
                            for q4 in range(4):
                                for j in range(8):
                                    for pl in range(4):
                                        bk, br = bks[pl]
                                        pr = 32 * pl
                                        m = 7 - j if d == 0 else j
                                        if d == 0:
                                            rhs = ufw[b][pr:pr + 32, q4, j::8]
                                        else:
                                            rhs = ubw[b][pr:pr + 32, q4, (CHS - 1) * 8 + j::-8]
                                        first = (q4 == 0 and j == 0)
                                        last = (q4 == 3 and j == 7)
                                        ures = ufw_r[b] if d == 0 else ubw_r[b]
                                        PE.op(lambda e, bk=bk, pl=pl, pr=pr, d=d, q4=q4, m=m, ri=ri, rhs=rhs, j=j: e.matmul(
                                            bk[:, q4 * CHS:(q4 + 1) * CHS], lhsT=BsT[pr:pr + 32, d, q4, m, ri, :], rhs=rhs, start=(j == 0), stop=(j == 7), tile_position=(pr, 0)),
                                            reads=([ures, BsT_r] if last else []), writes=([br] if last else []),
                                            extra=([ures.w, BsT_r.w, br.w] + list(br.r) if first else []), mark=last)
                            for pl in range(4):
                                bk, br = bks[pl]
                                c0 = ri * 32 + d * 16 + pl
                                ACT.op(lambda e, bk=bk, b=b, c0=c0, pl=pl: e.activation(out=Bbuf[b][:, c0:c0 - pl + 16:4, :], in_=bk[:, 0:4 * CHS].rearrange("p (a b) -> p a b", a=4), func=AF.Copy),
                                       reads=[br], writes=[Bbuf_r[b]])
                            yield None

                def scan(name, N, init_zero, want_traj, want_snap):
                    S_ = N // 8
                    nch = S_ // CHS
                    if init_zero:
                        DVE.op(lambda e: e.memset(Xb[:, :, 0], 0.0), writes=[Xb_r])
                    for _ in produce_B(name, N, 0):
                        pass
                    for kc in range(nch):
                        b = kc % 2
                        pb = produce_B(name, N, kc + 1) if kc + 1 < nch else iter(())
                        for r in range(CHS):
                            if r % 8 == 0:
                                if (r // 8) % 2 == 0:
                                    next(pb, None)
                                if r > 0:
                                    yield 14.0
                            DVE.op(lambda e, r=r: e.tensor_tensor(out=T1[:], in0=A1[:], in1=Xb[:, 0:64, r], op=ALU.mult), reads=[A1_r, Xb_r], writes=[T1_r])
                            DVE.op(lambda e, r=r: e.tensor_tensor(out=T2s[:], in0=A2[:], in1=Xb[:, 32:96, r], op=ALU.mult), reads=[A2_r, Xb_r], writes=[T2s_r])
                            DVE.op(lambda e: e.tensor_tensor(out=Us[:], in0=T1[:], in1=T2s[:], op=ALU.add), reads=[T1_r, T2s_r], writes=[Us_r])
                            DVE.op(lambda e, r=r, b=b: e.tensor_tensor(out=Xb[:, 0:64, r + 1], in0=Us[:], in1=Bbuf[b][:, :, r], op=ALU.add), reads=[Us_r, Bbuf_r[b]], writes=[Xb_r])
                            DVE.op(lambda e, r=r, b=b: e.tensor_tensor(out=Xb[:, 64:96, r + 1], in0=Us[:, 0:32], in1=Bbuf[b][:, 0:32, r], op=ALU.add), reads=[Us_r, Bbuf_r[b]], writes=[Xb_r])
                        if want_traj:
                            ACT.op(lambda e, b=b: e.activation(out=tjb[b][:], in_=Xb[:, 0:64, 0:CHS], func=AF.Copy), reads=[Xb_r], writes=[tjb_r[b]])
                            SP.op(lambda e, b=b, kc=kc: e.dma_start(out=traj[name][:, :, kc * CHS:(kc + 1) * CHS], in_=tjb[b][:]), reads=[tjb_r[b]], dsem=ds_tj[b])
                        if want_snap and ((kc + 1) * CHS) % 256 == 0 and kc + 1 < nch:
                            mq = ((kc + 1) * CHS) // 256
                            xv = Xb[:, 0:64, CHS].rearrange("p (a b c) -> p a b c", a=2, b=2)
                            DVE.op(lambda e, mq=mq, xv=xv: e.tensor_copy(out=snap[:, mq, :].rearrange("p (a b c) -> p a b c", a=2, b=2)[:, :, 0, :], in_=xv[:, :, 0, :]),
                                   reads=[Xb_r], writes=[snap_r])
                            DVE.op(lambda e, mq=mq, xv=xv: e.tensor_copy(out=snap[:, 7 - mq, :].rearrange("p (a b c) -> p a b c", a=2, b=2)[:, :, 1, :], in_=xv[:, :, 1, :]),
                                   reads=[Xb_r], writes=[snap_r])
                        DVE.op(lambda e: e.tensor_copy(out=Xb[:, :, 0], in_=Xb[:, :, CHS]), reads=[], writes=[Xb_r])
                        for _ in pb:
                            pass
                        yield 14.0

                todo = debug.get("ssets", ["P", "S", "O"]) if debug else ["P", "S", "O"]

                def scan_all():
                    if "P" in todo:
                        yield from scan("P", debug.get("NP", LP) if debug else LP, True, True, False)
                    if "S" in todo:
                        yield from scan("S", LS, True, False, True)
                    if "O" in todo:
                        DVE.op(lambda e: e.memset(Xb[:, :, 0], 0.0), writes=[Xb_r])
                        for q in range(8):
                            DVE.op(lambda e, q=q: e.scalar_tensor_tensor(out=Xb[:, 0:64, 0], in0=snap[:, q, :], scalar=cst[:, C_SEL + q:C_SEL + q + 1], in1=Xb[:, 0:64, 0], op0=ALU.mult, op1=ALU.add),
                                   reads=[snap_r, cst_r], writes=[Xb_r])
                        DVE.op(lambda e: e.tensor_copy(out=Xb[:, 64:96, 0], in_=Xb[:, 0:32, 0]), writes=[Xb_r])
                        yield from scan("O", LO, False, True, False)

                gens = [scan_all()]
                if not (debug and debug.get("skipT")):
                    gens.append(attention_setup(T2))
                clock = [0.0 for _ in gens]
                alive = [True for _ in gens]
                while any(alive):
                    cand = [i for i in range(len(gens)) if alive[i]]
                    i = min(cand, key=lambda i: clock[i])
                    try:
                        clock[i] += next(gens[i])
                    except StopIteration:
                        alive[i] = False
                barrier()

            with contextlib.ExitStack() as ps4:
                P4 = ps4.enter_context

                def T4(name, shape, dt=F32):
                    return P4(nc.sbuf_tensor(name, list(shape), dt)), Res()

                HC = 64
                CsTo, CsTo_r = T4("CsTo", [128, 9, 2, 32, 32], BF16)
                Gmo, Gmo_r = T4("Gmo", [128, 4, 15, 128], BF16)
                ds_rl = new_dsem("reload")
                SP.op(lambda e: e.dma_start(out=CsTo[:].rearrange("p a b c d -> p (a b c d)"), in_=CsT_d[:, :]), writes=[CsTo_r], dsem=ds_rl)
                SP.op(lambda e: e.dma_start(out=Gmo[:].rearrange("p a b c -> p (a b c)"), in_=Gm_d[:, :]), writes=[Gmo_r], dsem=ds_rl)
                uh = [T4("uh%d" % i, [128, 4, 512], BF16) for i in range(2)]
                xf = [T4("xf%d" % i, [128, 2, 16, HC], BF16) for i in range(2)]
                xw = [T4("xw%d" % i, [128, 2, 16, HC], BF16) for i in range(2)]
                ds_o = [new_dsem("so_in0"), new_dsem("so_in1")]
                _y = [T4("ysb%d" % i, [128, 4, 512]) for i in range(2)]
                ysb = [_y[0][0], _y[1][0]]; ysb_r = [_y[0][1], _y[1][1]]
                y2, y2_r = T4("y2", [128, 4, 512])
                vv, vv_r = T4("vv", [128, 4, 512])
                gf, gf_r = T4("gf", [128, 4, 512])
                gb, gb_r = T4("gb", [128, 4, 512], BF16)
                sf, sf_r = T4("sf", [128, 4, 512])
                sg2, sg2_r = T4("sg2", [128, 512])
                sqs, sqs_r = T4("sqs", [128, 4, 512], BF16)
                rsd, rsd_r = T4("rsd", [128, 512])
                snb = [T4("snb%d" % i, [128, 4, 512], BF16) for i in range(2)]
                ds_sn = [new_dsem("sn0"), new_dsem("sn1")]
                ds_yd = new_dsem("ydbg")

                def outstage(name, N):
                    S_ = N // 8
                    nhc = N // 512

                    def stP(hc):
                        b = hc % 2
                        t0 = hc * 512
                        s0 = hc * HC
                        tauf = s0
                        taub = S_ - s0 - HC
                        POOL.op(lambda e, b=b, t0=t0: e.dma_start(out=uh[b][0][:], in_=uT[name][:, t0:t0 + 512].rearrange("(k p) t -> p k t", p=128)), writes=[uh[b][1]], dsem=ds_o[b])
                        trv = traj[name].rearrange("p (a b c) s -> p a b c s", a=2, b=2)
                        POOL.op(lambda e, b=b, tauf=tauf, trv=trv: e.dma_start(out=xf[b][0][:], in_=trv[:, :, 0, :, tauf:tauf + HC]), writes=[xf[b][1]], dsem=ds_o[b])
                        POOL.op(lambda e, b=b, taub=taub, trv=trv: e.dma_start(out=xw[b][0][:], in_=trv[:, :, 1, :, taub:taub + HC]), writes=[xw[b][1]], dsem=ds_o[b])
                        deps_r = [uh[b][1], xf[b][1], xw[b][1], Gmo_r, CsTo_r, zerob_r]
                        bks4 = [next_bank() for _ in range(4)]
                        for blk in range(4):
                            bk, br = bks4[blk]
                            PE.op(lambda e, bk=bk, b=b, blk=blk: e.matmul(bk[:, :], lhsT=zerob[:, :], rhs=uh[b][0][:, blk, :], start=True, stop=False),
                                  extra=[x.w for x in deps_r] + [br.w] + list(br.r), mark=False)
                            for i in range(8):
                                for j in range(8):
                                    kk = 0 if i == j else (i - j if i > j else 7 + (j - i))
                                    PE.op(lambda e, bk=bk, b=b, blk=blk, i=i, j=j, kk=kk: e.matmul(bk[:, i * HC:(i + 1) * HC], lhsT=Gmo[:, blk, kk, :], rhs=uh[b][0][:, blk, j::8], start=False, stop=False), mark=False)
                        for d in range(2):
                            for i in range(8):
                                m = i + 1 if d == 0 else 8 - i
                                for ri in range(2):
                                    for r in range(4):
                                        for blk in range(4):
                                            bk, br = bks4[blk]
                                            pl = (blk + r) % 4
                                            Pp = blk * 4 + pl
                                            last = (d == 1 and i == 7 and ri == 1 and r == 3)
                                            if d == 0:
                                                rhs = xf[b][0][:, ri, Pp, :]
                                            else:
                                                rhs = xw[b][0][:, ri, Pp, ::-1]
                                            PE.op(lambda e, bk=bk, pl=pl, i=i, m=m, ri=ri, d=d, Pp=Pp, rhs=rhs, last=last: e.matmul(
                                                bk[32 * pl:32 * pl + 32, i * HC:(i + 1) * HC], lhsT=CsTo[:, m, ri, d * 16 + Pp, :], rhs=rhs, start=False, stop=last, tile_position=(0, 32 * pl)),
                                                reads=(deps_r if last else []), writes=([br] if last else []), mark=last)
                        for blk in range(4):
                            bk, br = bks4[blk]
                            ACT.op(lambda e, bk=bk, blk=blk: e.activation(out=ysb[b][:, blk, :].rearrange("p (s i) -> p i s", i=8), in_=bk[:, :].rearrange("p (i s) -> p i s", i=8), func=AF.Copy),
                                   reads=[br], writes=[ysb_r[b]])

                    def stQ(hc):
                        b = hc % 2
                        t0 = hc * 512
                        if debug is not None and ("ydbg_" + name) in debug:
                            SP.op(lambda e, t0=t0: e.dma_start(out=ydbg[name][:, t0:t0 + 512].rearrange("(k p) t -> p k t", p=128), in_=ysb[b][:]), reads=[ysb_r[b]], dsem=ds_yd)
                        fl = lambda t: t[:].rearrange("p a b -> p (a b)")
                        ACT.op(lambda e: e.activation(out=fl(y2), in_=fl(ysb[b]), func=AF.Square), reads=[ysb_r[b]], writes=[y2_r])
                        DVE.op(lambda e: e.tensor_scalar(out=fl(y2), in0=fl(y2), scalar1=0.044715, scalar2=1.0, op0=ALU.mult, op1=ALU.add), writes=[y2_r])
                        DVE.op(lambda e: e.tensor_tensor(out=fl(vv), in0=fl(y2), in1=fl(ysb[b]), op=ALU.mult), reads=[y2_r, ysb_r[b]], writes=[vv_r])
                        ACT.op(lambda e: e.activation(out=fl(vv), in_=fl(vv), func=AF.Sigmoid, scale=1.5957691216057308), writes=[vv_r])
                        DVE.op(lambda e: e.tensor_tensor(out=fl(gf), in0=fl(vv), in1=fl(ysb[b]), op=ALU.mult), reads=[vv_r, ysb_r[b]], writes=[gf_r])
                        ACT.op(lambda e: e.activation(out=fl(gb), in_=fl(gf), func=AF.Copy), reads=[gf_r], writes=[gb_r])
                        for bo in range(4):
                            bk, br = next_bank()
                            for bi in range(4):
                                last = bi == 3
                                PE.op(lambda e, bk=bk, bi=bi, bo=bo: e.matmul(bk[:, :], lhsT=wglu_b[:, bi, bo * 128:(bo + 1) * 128], rhs=gb[:, bi, :], start=(bi == 0), stop=(bi == 3)),
                                      reads=([gb_r, wglu_r] if last else []), writes=([br] if last else []),
                                      extra=([gb_r.w, wglu_r.w, br.w] + list(br.r) if bi == 0 else []), mark=last)
                            ACT.op(lambda e, bk=bk, bo=bo: e.activation(out=sg2[:], in_=bk[:, :], func=AF.Sigmoid, bias=bgl[:, bo:bo + 1]), reads=[br, bgl_r], writes=[sg2_r])
                            DVE.op(lambda e, bo=bo: e.tensor_tensor(out=sf[:, bo, :], in0=gf[:, bo, :], in1=sg2[:], op=ALU.mult), reads=[gf_r, sg2_r], writes=[sf_r])
                        ACT.op(lambda e: e.activation(out=fl(sqs), in_=fl(sf), func=AF.Square), reads=[sf_r], writes=[sqs_r])
                        bk, br = next_bank()
                        for bi in range(4):
                            last = bi == 3
                            PE.op(lambda e, bk=bk, bi=bi: e.matmul(bk[:, :], lhsT=onesb[:, :], rhs=sqs[:, bi, :], start=(bi == 0), stop=(bi == 3)),
                                  reads=([sqs_r, onesb_r] if last else []), writes=([br] if last else []),
                                  extra=([sqs_r.w, onesb_r.w, br.w] + list(br.r) if bi == 0 else []), mark=last)
                        ACT.op(lambda e, bk=bk: e.activation(out=rsd[:], in_=bk[:, :], func=AF.Ln, scale=1.0 / 512, bias=EPS), reads=[br], writes=[rsd_r])
                        ACT.op(lambda e: e.activation(out=rsd[:], in_=rsd[:], func=AF.Exp, scale=-0.5), writes=[rsd_r])
                        for bo in range(4):
                            DVE.op(lambda e, bo=bo, b=b: e.scalar_tensor_tensor(out=snb[b][0][:, bo, :], in0=sf[:, bo, :], scalar=gss[:, bo:bo + 1], in1=rsd[:], op0=ALU.mult, op1=ALU.mult),
                                   reads=[sf_r, gss_r, rsd_r], writes=[snb[b][1]])
                        SP.op(lambda e, b=b, t0=t0: e.dma_start(out=sT[name][:, t0:t0 + 512].rearrange("(k p) t -> p k t", p=128), in_=snb[b][0][:]), reads=[snb[b][1]], dsem=ds_sn[b])

                    stP(0)
                    for hc in range(nhc):
                        if hc + 1 < nhc:
                            stP(hc + 1)
                        stQ(hc)

                if "P" in todo:
                    outstage("P", debug.get("NP", LP) if debug else LP)
                if "O" in todo:
                    outstage("O", LO)
                barrier()


    def attention_setup(T_alloc):
        SCALE = 1.0 / math.sqrt(96.0)
        PCK = 2048
        NKV = 4
        kvk = [T_alloc("kvk%d" % i, [96, PCK], BF16) for i in range(NKV)]
        kvv = [T_alloc("kvv%d" % i, [128, PCK // 128, 65], BF16) for i in range(NKV)]
        kv_r = [Res() for _ in range(NKV)]
        ds_kv = [new_dsem("kv%d" % i) for i in range(NKV)]
        qtb = [T_alloc("qtb%d" % i, [96, LP], BF16) for i in range(2)]; qtb_r = [Res(), Res()]
        ds_q = [new_dsem("q0"), new_dsem("q1")]
        NPT = 5
        pT = [T_alloc("pT%d" % i, [128, 512], BF16) for i in range(NPT)]; pT_r = [Res() for _ in range(NPT)]
        NEP = 3
        osb = [T_alloc("osb%d" % i, [64, 512], F32) for i in range(NEP)]; osb_r = [Res() for _ in range(NEP)]
        bcs = [T_alloc("bcs%d" % i, [64, 512], F32) for i in range(NEP)]; bcs_r = [Res() for _ in range(NEP)]
        rden = T_alloc("rden", [128, 512], F32); rden_r = Res()
        lnr = T_alloc("lnr", [128, 512], F32); lnr_r = Res()
        onesf = T_alloc("onesf", [128, 64], F32); onesf_r = Res()
        asb = [T_alloc("asb%d" % i, [64, 512], F32) for i in range(NEP)]; asb_r = [Res() for _ in range(NEP)]
        ds_a = [new_dsem("a%d" % i) for i in range(NEP)]
        DVE.op(lambda e: e.memset(onesf[:], 1.0), writes=[onesf_r])
        acc, acc_r = banks[0], bank_res[0]
        st_b = [(banks[i], bank_res[i]) for i in range(1, 4)]
        cnt = {"st": 0, "pt": 0, "a": 0, "hb": 0}

        def attend(qname, kname, NQ, NK, cost):
            nkt = NK // 128
            npc = NK // PCK
            tpp = PCK // 128
            plist = [(h, qg, pc) for h in range(8) for qg in range(NQ // 512) for pc in range(npc)]
            st = {"issued": 0}

            def ensure(i):
                while st["issued"] < min(i + 3, len(plist)):
                    j = st["issued"]
                    h, qg, pc = plist[j]
                    sl = j % NKV
                    POOL.op(lambda e, sl=sl, h=h, pc=pc: e.dma_start(out=kvk[sl][:], in_=KT[kname][h, :, pc * PCK:(pc + 1) * PCK]), writes=[kv_r[sl]], dsem=ds_kv[sl])
                    POOL.op(lambda e, sl=sl, h=h, pc=pc: e.dma_start(out=kvv[sl][:], in_=VV[kname][h, :, pc * tpp:(pc + 1) * tpp, :]), writes=[kv_r[sl]], dsem=ds_kv[sl])
                    st["issued"] += 1

            pidx = 0
            for h in range(8):
                hb = cnt["hb"] % 2
                cnt["hb"] += 1
                POOL.op(lambda e, hb=hb, h=h: e.dma_start(out=qtb[hb][:, 0:NQ], in_=QT[qname][h, :, 0:NQ]), writes=[qtb_r[hb]], dsem=ds_q[hb])
                for qg in range(NQ // 512):
                    q0 = qg * 512
                    LOOK = 2
                    pend = []
                    for kt in range(nkt + LOOK):
                        if kt < nkt:
                            pc, lt = divmod(kt, tpp)
                            if lt == 0:
                                ensure(pidx + pc)
                            sl = (pidx + pc) % NKV
                            sb_, sb_r = st_b[cnt["st"] % len(st_b)]
                            cnt["st"] += 1
                            PE.op(lambda e, sb_=sb_, hb=hb, sl=sl, lt=lt, q0=q0: e.matmul(sb_[:, :], lhsT=kvk[sl][:, lt * 128:(lt + 1) * 128], rhs=qtb[hb][:, q0:q0 + 512], start=True, stop=True),
                                  reads=[kv_r[sl], qtb_r[hb]], writes=[sb_r])
                            pi = cnt["pt"] % NPT
                            cnt["pt"] += 1
                            ACT.op(lambda e, sb_=sb_, pi=pi: e.activation(out=pT[pi][:], in_=sb_[:, :], func=AF.Exp, scale=SCALE), reads=[sb_r], writes=[pT_r[pi]])
                            pend.append((pi, kt, sl, lt))
                        if kt >= LOOK:
                            ppi, pkt, psl, plt = pend.pop(0)
                            PE.op(lambda e, ppi=ppi, pkt=pkt, psl=psl, plt=plt: e.matmul(acc[0:65, :], lhsT=kvv[psl][:, plt, :], rhs=pT[ppi][:], start=(pkt == 0), stop=(pkt == nkt - 1)),
                                  reads=[kv_r[psl], pT_r[ppi]], writes=[acc_r])
                        if kt % 16 == 15:
                            yield 11.0
                    pidx += npc
                    ai = cnt["a"] % NEP
                    cnt["a"] += 1
                    ACT.op(lambda e, ai=ai: e.activation(out=osb[ai][:], in_=acc[0:64, :], func=AF.Copy), reads=[acc_r], writes=[osb_r[ai]])
                    ACT.op(lambda e: e.activation(out=lnr[64:65, :], in_=acc[64:65, :], func=AF.Ln), reads=[acc_r], writes=[lnr_r])
                    ACT.op(lambda e: e.activation(out=rden[64:65, :], in_=lnr[64:65, :], func=AF.Exp, scale=-1.0), reads=[lnr_r], writes=[rden_r])
                    bcb, bcr = st_b[cnt["st"] % len(st_b)]
                    cnt["st"] += 1
                    PE.op(lambda e, bcb=bcb: e.matmul(bcb[0:64, :], lhsT=onesf[64:65, 0:64], rhs=rden[64:65, :], start=True, stop=True), reads=[onesf_r, rden_r], writes=[bcr])
                    ACT.op(lambda e, bcb=bcb, ai=ai: e.activation(out=bcs[ai][:], in_=bcb[0:64, :], func=AF.Copy), reads=[bcr], writes=[bcs_r[ai]])
                    DVE.op(lambda e, ai=ai: e.tensor_tensor(out=asb[ai][:], in0=osb[ai][:], in1=bcs[ai][:], op=ALU.mult), reads=[osb_r[ai], bcs_r[ai]], writes=[asb_r[ai]])
                    SP.op(lambda e, ai=ai, h=h, q0=q0: e.dma_start(out=aT[qname][h, :, q0:q0 + 512], in_=asb[ai][:]), reads=[asb_r[ai]], dsem=ds_a[ai])
                    yield 4.0

        def gen():
            todo = debug.get("tsets", ["P", "O"]) if debug else ["P", "O"]
            if "P" in todo:
                NPd = debug.get("NP", LP) if debug else LP
                yield from attend("P", "P", NPd, NPd, 0.0)
            if "O" in todo:
                yield from attend("O", "S", LO, debug.get("NS", LS) if debug else LS, 0.0)

        return gen()

    def phase_E1():
        with contextlib.ExitStack() as ps:
            P_ = ps.enter_context

            def T(name, shape, dt=F32):
                return P_(nc.sbuf_tensor(name, list(shape), dt))

            woa = T("woa", [128, 4, D], BF16); woa_r = Res()
            wos = T("wos", [128, 4, D], BF16); wos_r = Res()
            ga = T("ga", [128, 4]); ga_r = Res()
            g2b = T("g2b", [128, D]); g2b_r = Res()
            dsw = new_dsem("wE1")
            POOL.op(lambda e: e.dma_start(out=woa[:], in_=woa_d.rearrange("(k p) n -> p k n", p=128)), writes=[woa_r], dsem=new_dsem("pq%d" % next(_uid)))
            POOL.op(lambda e: e.dma_start(out=wos[:], in_=wos_d.rearrange("(k p) n -> p k n", p=128)), writes=[wos_r], dsem=new_dsem("pq%d" % next(_uid)))
            SP.op(lambda e: e.dma_start(out=ga[:], in_=ga_d[:, :]), writes=[ga_r], dsem=dsw)
            SP.op(lambda e: e.dma_start(out=g2b[:], in_=g2.broadcast_to([128, D])), writes=[g2b_r], dsem=dsw)
            at = [T("at%d" % i, [128, 4, 512]) for i in range(2)]; at_r = [Res(), Res()]
            snt = [T("snt%d" % i, [128, 4, 512], BF16) for i in range(2)]; snt_r = [Res(), Res()]
            xt = [T("e1x%d" % i, [128, 4, D]) for i in range(2)]; xt_r = [Res(), Res()]
            ds_in = [new_dsem("e1in0"), new_dsem("e1in1")]
            sq = T("e1sq", [128, 4, 512], BF16); sq_r = Res()
            rs = T("e1rs", [128, 512]); rs_r = Res()
            an2 = [T("e1an%d" % i, [128, 4, 512], BF16) for i in range(2)]; an2_r = [Res(), Res()]
            x2t = [T("x2t%d" % i, [128, D]) for i in range(2)]; x2t_r = [Res(), Res()]
            ds_x2 = [new_dsem("x2o0"), new_dsem("x2o1")]
            junk = T("e1junk", [128, D], BF16); junk_r = Res()
            ss2 = [T("ss2_%d" % i, [128, 1]) for i in range(2)]; ss2_r = [Res(), Res()]
            h2 = [T("h2_%d" % i, [128, D], BF16) for i in range(2)]; h2_r = [Res(), Res()]
            h2o = [T("h2o%d" % i, [128, 8, 128], BF16) for i in range(2)]; h2o_r = [Res(), Res()]
            ds_h2 = [new_dsem("h2o0"), new_dsem("h2o1")]
            c = {"t": 0}

            def run(name, xd, N):
                nmt = N // 512

                def pro(mt):
                    b = mt % 2
                    t0 = mt * 512
                    POOL.op(lambda e, b=b, t0=t0: e.dma_start(out=at[b][:], in_=aT[name].rearrange("(k two) f t -> (two f) k t", two=2)[:, :, t0:t0 + 512]), writes=[at_r[b]], dsem=ds_in[b])
                    POOL.op(lambda e, b=b, t0=t0: e.dma_start(out=snt[b][:], in_=sT[name][:, t0:t0 + 512].rearrange("(k p) t -> p k t", p=128)), writes=[snt_r[b]], dsem=ds_in[b])
                    POOL.op(lambda e, b=b, t0=t0: e.dma_start(out=xt[b][:], in_=xd[t0:t0 + 512, :].rearrange("(t p) d -> p t d", p=128)), writes=[xt_r[b]], dsem=ds_in[b])
                    fl = lambda t: t[:].rearrange("p a b -> p (a b)")
                    ACT.op(lambda e, b=b: e.activation(out=fl(sq), in_=fl(at[b]), func=AF.Square), reads=[at_r[b]], writes=[sq_r])
                    bk, br = next_bank()
                    for h in range(4):
                        last = h == 3
                        PE.op(lambda e, bk=bk, h=h: e.matmul(bk[:, :], lhsT=onesb[:, :], rhs=sq[:, h, :], start=(h == 0), stop=(h == 3)),
                              reads=([sq_r, onesb_r] if last else []), writes=([br] if last else []),
                              extra=([sq_r.w, onesb_r.w, br.w] + list(br.r) if h == 0 else []), mark=last)
                    ACT.op(lambda e, bk=bk: e.activation(out=rs[:], in_=bk[:, :], func=AF.Ln, scale=1.0 / 512, bias=EPS), reads=[br], writes=[rs_r])
                    ACT.op(lambda e: e.activation(out=rs[:], in_=rs[:], func=AF.Exp, scale=-0.5), writes=[rs_r])
                    for h in range(4):
                        DVE.op(lambda e, b=b, h=h: e.scalar_tensor_tensor(out=an2[b][:, h, :], in0=at[b][:, h, :], scalar=ga[:, h:h + 1], in1=rs[:], op0=ALU.mult, op1=ALU.mult),
                               reads=[at_r[b], ga_r, rs_r], writes=[an2_r[b]])

                def tiles(mt):
                    b = mt % 2
                    t0 = mt * 512
                    def stA(t):
                        xb = t % 2
                        for half in range(2):
                            bk, br = next_bank()
                            n = 0
                            for h in range(4):
                                n += 1
                                PE.op(lambda e, bk=bk, h=h, t=t, half=half: e.matmul(bk[:, :], lhsT=an2[b][:, h, t * 128:(t + 1) * 128], rhs=woa[:, h, half * 512:(half + 1) * 512], start=(h == 0), stop=False),
                                      extra=([an2_r[b].w, woa_r.w, wos_r.w, snt_r[b].w, br.w] + list(br.r) if n == 1 else []), mark=False)
                            for k4 in range(4):
                                last = k4 == 3
                                PE.op(lambda e, bk=bk, k4=k4, t=t, half=half, b=b: e.matmul(bk[:, :], lhsT=snt[b][:, k4, t * 128:(t + 1) * 128], rhs=wos[:, k4, half * 512:(half + 1) * 512], start=False, stop=(k4 == 3)),
                                      reads=([an2_r[b], woa_r, wos_r, snt_r[b]] if last else []), writes=([br] if last else []), mark=last)
                            DVE.op(lambda e, bk=bk, xb=xb, b=b, t=t, half=half: e.tensor_tensor(out=x2t[xb][:, half * 512:(half + 1) * 512], in0=bk[:, :], in1=xt[b][:, t, half * 512:(half + 1) * 512], op=ALU.add),
                                   reads=[br, xt_r[b]], writes=[x2t_r[xb]])
                        SP.op(lambda e, xb=xb, t0=t0, t=t: e.dma_start(out=x2s[name][t0 + t * 128:t0 + (t + 1) * 128, :], in_=x2t[xb][:]), reads=[x2t_r[xb]], dsem=ds_x2[xb])
                        ACT.op(lambda e, xb=xb: e.activation(out=junk[:], in_=x2t[xb][:], func=AF.Square, accum_out=ss2[xb][:, 0:1]), reads=[x2t_r[xb]], writes=[junk_r, ss2_r[xb]])
                        ACT.op(lambda e, xb=xb: e.activation(out=ss2[xb][:], in_=ss2[xb][:], func=AF.Ln, scale=1.0 / D, bias=EPS), writes=[ss2_r[xb]])
                        ACT.op(lambda e, xb=xb: e.activation(out=ss2[xb][:], in_=ss2[xb][:], func=AF.Exp, scale=-0.5), writes=[ss2_r[xb]])
                        DVE.op(lambda e, xb=xb: e.scalar_tensor_tensor(out=h2[xb][:], in0=x2t[xb][:], scalar=ss2[xb][:, 0:1], in1=g2b[:], op0=ALU.mult, op1=ALU.mult),
                               reads=[x2t_r[xb], ss2_r[xb], g2b_r], writes=[h2_r[xb]])

                    def stB(t):
                        xb = t % 2
                        bk, br = next_bank()
                        bkb = bk[:, :].bitcast(BF16)
                        for kk in range(8):
                            last = kk == 7
                            PE.op(lambda e, bkb=bkb, kk=kk, xb=xb: e.transpose(out=bkb[:, kk * 128:(kk + 1) * 128], in_=h2[xb][:, kk * 128:(kk + 1) * 128], identity=identb[:, :]),
                                  reads=([h2_r[xb], identb_r] if last else []), writes=([br] if last else []),
                                  extra=([h2_r[xb].w, identb_r.w, br.w] + list(br.r) if kk == 0 else []), mark=last)
                        ACT.op(lambda e, bkb=bkb, xb=xb: e.activation(out=h2o[xb][:].rearrange("p a b -> p (a b)"), in_=bkb, func=AF.Copy), reads=[br], writes=[h2o_r[xb]])
                        SP.op(lambda e, xb=xb, t0=t0, t=t: e.dma_start(out=h2T[name][:, t0 + t * 128:t0 + (t + 1) * 128].rearrange("(k p) t -> p k t", p=128), in_=h2o[xb][:]),
                              reads=[h2o_r[xb]], dsem=ds_h2[xb])

                    stA(0)
                    for t in range(4):
                        if t + 1 < 4:
                            stA(t + 1)
                        stB(t)
                        if t == 1 and mt + 1 < nmt:
                            pro(mt + 1)

                pro(0)
                for mt in range(nmt):
                    tiles(mt)

            todo = debug.get("tsets", ["P", "O"]) if debug else ["P", "O"]
            if "P" in todo:
                run("P", xp, debug.get("NP", LP) if debug else LP)
            if "O" in todo:
                run("O", xo, LO)
            barrier()

    def phase_E2():
        with contextlib.ExitStack() as ps:
            P_ = ps.enter_context

            def T(name, shape, dt=F32):
                return P_(nc.sbuf_tensor(name, list(shape), dt))

            w1 = T("w1", [128, 8, 4096], BF16); w1_rs = [Res(), Res()]
            w2 = T("w2", [128, 32, D], BF16); w2_rs = [Res() for _ in range(4)]
            gfb = T("gfb", [128, D]); gfb_r = Res()
            dsw = new_dsem("wE2")
            POOL.op(lambda e: e.dma_start(out=w1[:, :, 0:2048], in_=w1_d.rearrange("(k p) n -> p k n", p=128)[:, :, 0:2048]), writes=[w1_rs[0]], dsem=new_dsem("pq%d" % next(_uid)))
            for qq in range(2):
                POOL.op(lambda e, qq=qq: e.dma_start(out=w2[:, qq * 8:(qq + 1) * 8, :], in_=w2_d.rearrange("(f p) n -> p f n", p=128)[:, qq * 8:(qq + 1) * 8, :]), writes=[w2_rs[qq]], dsem=new_dsem("pq%d" % next(_uid)))
            POOL.op(lambda e: e.dma_start(out=w1[:, :, 2048:4096], in_=w1_d.rearrange("(k p) n -> p k n", p=128)[:, :, 2048:4096]), writes=[w1_rs[1]], dsem=new_dsem("pq%d" % next(_uid)))
            for qq in range(2, 4):
                POOL.op(lambda e, qq=qq: e.dma_start(out=w2[:, qq * 8:(qq + 1) * 8, :], in_=w2_d.rearrange("(f p) n -> p f n", p=128)[:, qq * 8:(qq + 1) * 8, :]), writes=[w2_rs[qq]], dsem=new_dsem("pq%d" % next(_uid)))
            SP.op(lambda e: e.dma_start(out=gfb[:], in_=gf_d.broadcast_to([128, D])), writes=[gfb_r], dsem=dsw)
            MT = 256
            hT_ = [T("e2h%d" % i, [128, 8, MT], BF16) for i in range(2)]; hT_r = [Res(), Res()]
            x2t = [T("e2x%d" % i, [128, 2, D]) for i in range(2)]; x2t_r = [Res(), Res()]
            ds_in = [new_dsem("e2in0"), new_dsem("e2in1")]
            NR = 4
            rl = [T("e2r%d" % i, [128, MT], BF16) for i in range(NR)]; rl_r = [Res() for _ in range(NR)]
            hid = [T("e2hid%d" % i, [128, MT], BF16) for i in range(NR)]; hid_r = [Res() for _ in range(NR)]
            x3 = T("x3", [128, D]); x3_r = Res()
            junk = T("e2junk", [128, D], BF16); junk_r = Res()
            ss = T("e2ss", [128, 1]); ss_r = Res()
            yt = [T("yt%d" % i, [128, D]) for i in range(2)]; yt_r = [Res(), Res()]
            ds_y = [new_dsem("y0"), new_dsem("y1")]
            acc_b = [[(banks[0], bank_res[0]), (banks[1], bank_res[1])], [(banks[2], bank_res[2]), (banks[3], bank_res[3])]]
            m1_b = [(banks[i], bank_res[i]) for i in range(4, 8)]
            c = {"m1": 0, "r": 0, "y": 0}

            def run(name, yd, N):
                for mt in range(N // MT):
                    b = mt % 2
                    t0 = mt * MT
                    POOL.op(lambda e, b=b, t0=t0: e.dma_start(out=hT_[b][:], in_=h2T[name][:, t0:t0 + MT].rearrange("(k p) t -> p k t", p=128)), writes=[hT_r[b]], dsem=ds_in[b])
                    POOL.op(lambda e, b=b, t0=t0: e.dma_start(out=x2t[b][:], in_=x2s[name][t0:t0 + MT, :].rearrange("(t p) d -> p t d", p=128)), writes=[x2t_r[b]], dsem=ds_in[b])
                    pend = []
                    for f in range(33):
                        if f < 32:
                            bk, br = m1_b[c["m1"] % 4]
                            c["m1"] += 1
                            for kk in range(8):
                                last = kk == 7
                                PE.op(lambda e, bk=bk, kk=kk, f=f, b=b: e.matmul(bk[:, 0:MT], lhsT=w1[:, kk, f * 128:(f + 1) * 128], rhs=hT_[b][:, kk, :], start=(kk == 0), stop=(kk == 7)),
                                      reads=([w1_rs[f // 16], hT_r[b]] if last else []), writes=([br] if last else []),
                                      extra=([w1_rs[f // 16].w, hT_r[b].w, br.w] + list(br.r) if kk == 0 else []), mark=last)
                            ri = c["r"] % NR
                            c["r"] += 1
                            ACT.op(lambda e, bk=bk, ri=ri: e.activation(out=rl[ri][:], in_=bk[:, 0:MT], func=AF.Relu), reads=[br], writes=[rl_r[ri]])
                            DVE.op(lambda e, ri=ri: e.tensor_tensor(out=hid[ri][:], in0=rl[ri][:], in1=rl[ri][:], op=ALU.mult), reads=[rl_r[ri]], writes=[hid_r[ri]])
                            pend.append((ri, f))
                        if f >= 1:
                            pri, pf = pend.pop(0)
                            n = 0
                            for t in range(2):
                                for half in range(2):
                                    n += 1
                                    ab, ar = acc_b[t][half]
                                    lastf = pf == 31
                                    PE.op(lambda e, ab=ab, pri=pri, t=t, half=half, pf=pf: e.matmul(ab[:, :], lhsT=hid[pri][:, t * 128:(t + 1) * 128], rhs=w2[:, pf, half * 512:(half + 1) * 512], start=(pf == 0), stop=(pf == 31)),
                                          reads=([hid_r[pri], w2_rs[pf // 8]] if n == 4 else []), writes=([ar] if lastf else []),
                                          extra=([hid_r[pri].w, w2_rs[pf // 8].w] + ([ar.w] + list(ar.r) if pf == 0 else [])), mark=(n == 4 or lastf))
                    for t in range(2):
                        for half in range(2):
                            ab, ar = acc_b[t][half]
                            DVE.op(lambda e, ab=ab, b=b, t=t, half=half: e.tensor_tensor(out=x3[:, half * 512:(half + 1) * 512], in0=ab[:, :], in1=x2t[b][:, t, half * 512:(half + 1) * 512], op=ALU.add),
                                   reads=[ar, x2t_r[b]], writes=[x3_r])
                        ACT.op(lambda e: e.activation(out=junk[:], in_=x3[:], func=AF.Square, accum_out=ss[:, 0:1]), reads=[x3_r], writes=[junk_r, ss_r])
                        ACT.op(lambda e: e.activation(out=ss[:], in_=ss[:], func=AF.Ln, scale=1.0 / D, bias=EPS), writes=[ss_r])
                        ACT.op(lambda e: e.activation(out=ss[:], in_=ss[:], func=AF.Exp, scale=-0.5), writes=[ss_r])
                        yi = c["y"] % 2
                        c["y"] += 1
                        DVE.op(lambda e, yi=yi: e.scalar_tensor_tensor(out=yt[yi][:], in0=x3[:], scalar=ss[:, 0:1], in1=gfb[:], op0=ALU.mult, op1=ALU.mult),
                               reads=[x3_r, ss_r, gfb_r], writes=[yt_r[yi]])
                        SP.op(lambda e, yi=yi, t0=t0, t=t: e.dma_start(out=yd[t0 + t * 128:t0 + (t + 1) * 128, :], in_=yt[yi][:]), reads=[yt_r[yi]], dsem=ds_y[yi])

            todo = debug.get("tsets", ["P", "O"]) if debug else ["P", "O"]
            if "P" in todo:
                run("P", yp, debug.get("NP", LP) if debug else LP)
            if "O" in todo:
                run("O", yo, LO)
            barrier()

    phase_tables()
    phase_A()
    if not (debug and debug.get('skipS')):
        phase_S()
    if not (debug and debug.get('skipT')):
        phase_E1()
        phase_E2()

    block = E(nc.Block())

    @block.sync
    def _(e):
        SP.emit(e)

    @block.scalar
    def _(e):
        ACT.emit(e)

    @block.vector
    def _(e):
        DVE.emit(e)

    @block.gpsimd
    def _(e):
        POOL.emit(e)

    @block.tensor
    def _(e):
        PE.emit(e)

    st.close()
    return nc, dbg_outs


def _host_inputs(inputs, debug=None):
    f = lambda a: np.ascontiguousarray(np.asarray(a, dtype=np.float32))
    x_prompt = f(inputs["x_prompt"])
    x_sample = f(inputs["x_sample"])[0]
    w_in = f(inputs["w_in"])[0]
    kr = w_in[:, 384:416]
    wkr = np.concatenate([kr, np.concatenate([kr[:, 16:32], kr[:, 0:16]], axis=1)], axis=1)
    w_uq = f(inputs["w_uq"])[0].reshape(256, 8, 96)
    wq_nope = w_uq[:, :, 0:64].reshape(256, 512)
    wq_rope = w_uq[:, :, 64:96].reshape(256, 256)
    wq_rsw = np.concatenate([w_uq[:, :, 80:96], w_uq[:, :, 64:80]], axis=2).reshape(256, 256)
    w_ukv = f(inputs["w_ukv"])[0].reshape(128, 8, 128)
    wk_nope = w_ukv[:, :, 0:64].reshape(128, 512)
    wv = w_ukv[:, :, 64:128].reshape(128, 512)
    common = dict(
        posv=np.arange(LS, dtype=np.float32).reshape(1, LS), xs=x_sample, w_in=w_in, wkr=f(wkr), wq_nope=f(wq_nope), wq_rope=f(wq_rope), wq_rsw=f(wq_rsw),
        wk_nope=f(wk_nope), wv=f(wv), g1=f(inputs["norm1_g"]).reshape(1, D),
        gq=f(f(inputs["q_norm_g"]).reshape(2, 128).T), gkv=f(inputs["kv_norm_g"]).reshape(128, 1),
    )
    def pair32(a):
        return f(a.reshape(2, 16, 2, 64).transpose(2, 3, 0, 1).reshape(128, 32))

    def pairB(a):
        return f(a.reshape(2, 16, 2, 64, 16).transpose(2, 3, 0, 1, 4).reshape(128, 512))

    def pairC(a):
        return f(a.reshape(2, 16, 2, 16, 64).transpose(2, 4, 0, 1, 3).reshape(128, 512))

    ldt = f(inputs["log_dt"])[0]
    common.update(dict(
        lamr_p=pair32(f(inputs["lam_re"])[0]), lami_p=pair32(f(inputs["lam_im"])[0]),
        ldt_p=pair32(np.broadcast_to(ldt[:, :, None], (2, 32, 64))),
        bre_p=pairB(f(inputs["b_re"])[0]), bim_p=pairB(f(inputs["b_im"])[0]),
        cre_p=pairC(f(inputs["c_re"])[0]), cim_p=pairC(f(inputs["c_im"])[0]),
        dskip_p=f(f(inputs["d_skip"])[0].reshape(4, 128).T), bglu_p=f(f(inputs["b_glu"])[0].reshape(4, 128).T),
        gs_p=f(f(inputs["ssm_out_g"])[0].reshape(4, 128).T), w_glu=f(inputs["w_glu"])[0],
    ))
    w_out = f(inputs["w_out"])[0]
    common.update(dict(
        woa_d=f(w_out[:512]), wos_d=f(w_out[512:]),
        ga_d=f(f(inputs["attn_out_g"])[0].reshape(4, 128).T), g2=f(inputs["norm2_g"]).reshape(1, D),
        gf_d=f(inputs["final_g"]).reshape(1, D), w1_d=f(inputs["w_mlp1"])[0], w2_d=f(inputs["w_mlp2"])[0],
    ))
    p = np.arange(128)
    maps = []
    for c in range(NCORES):
        cs = np.zeros((128, NCONST), np.float32)
        cs[:, C_ID:C_ID + 128] = np.eye(128, dtype=np.float32)
        cs[:, C_BM:C_BM + 128] = np.kron(np.eye(8, dtype=np.float32), np.ones((16, 16), np.float32))
        cs[:, C_ME] = (p < 64)
        cs[:, C_MO] = (p >= 64)
        cs[:, C_NME] = -(p < 64).astype(np.float32)
        cs[:, C_NMO] = -(p >= 64).astype(np.float32)
        fi = (p % 32) % 16
        cs[:, C_INV] = (10000.0 ** (-(2.0 * fi) / 32.0)).astype(np.float32)
        cs[:, C_SGN] = np.where((p % 32) < 16, -1.0, 1.0)
        cs[:, C_OFF] = 2048.0 * c
        cs[:, C_SEL + c] = 1.0
        m = dict(common)
        m["xp"] = x_prompt[c]
        m["xo"] = np.ascontiguousarray(x_sample[c * LO:(c + 1) * LO])
        m["consts"] = cs
        if debug:
            m["xs"] = m["xs"][:debug.get("LSd", LS)]
            m["xo"] = m["xo"][:debug.get("LOd", LO)]
        maps.append(m)
    return maps


def kernel(**inputs):
    nc, _ = _build()
    maps = _host_inputs(inputs)
    res = run_bass_kernel_spmd(nc, maps, core_ids=list(range(NCORES)))
    yp = np.stack([res.results[c]["yp"] for c in range(NCORES)], axis=0)
    ys = np.concatenate([res.results[c]["yo"] for c in range(NCORES)], axis=0)[None]
    return (yp.astype(np.float32), ys.astype(np.float32))
```

```python
import contextlib
import math
import numpy as np
import concourse.bass as bass
import concourse.mybir as mybir
from concourse.bass_utils import run_bass_kernel_spmd
from concourse.alu_op_type import AluOpType as ALU

F32 = mybir.dt.float32
BF16 = mybir.dt.bfloat16
I32 = mybir.dt.int32
AF = mybir.ActivationFunctionType

NCORES = 8
D = 1024
LP = 4096
LS = 16384
LO = 2048
EPS = 1e-6
PI = math.pi
TWO_PI = 2.0 * math.pi
CW1 = 6.28125
CW2 = TWO_PI - 6.28125

C_ID = 0
C_BM = 128
C_ME = 256
C_MO = 257
C_INV = 258
C_SGN = 259
C_OFF = 260
C_SEL = 261
C_NME = 269
C_NMO = 270
NCONST = 272


class Res:
    __slots__ = ("w", "r", "name")

    def __init__(self, name=""):
        self.w = None
        self.r = []
        self.name = name


class DSem:
    def __init__(self, sem):
        self.sem = sem
        self.n = 0


class Eng:
    def __init__(self, name, sem):
        self.name = name
        self.sem = sem
        self.ops = []
        self.n = 0
        self.seen = {}

    def _waits(self, toks):
        w = []
        for t in toks:
            if t is None:
                continue
            s, v = t
            if isinstance(s, DSem):
                v = max(v, s.n)
                s = s.sem
            if self.name == "pe" and s is self.sem:
                continue
            kk = id(s)
            if self.seen.get(kk, 0) >= v:
                continue
            self.seen[kk] = v
            w.append((s, v))
        return w

    def op(self, fn, reads=(), writes=(), extra=(), mark=True, dsem=None):
        toks = list(extra)
        for r in reads:
            toks.append(r.w)
        for r in writes:
            toks.append(r.w)
            toks.extend(r.r)
        w = self._waits(toks)
        if dsem is not None:
            dsem.n += 16
            tok = (dsem, dsem.n)
            inc = (dsem.sem, 16)
        elif mark:
            self.n += 1
            tok = (self.sem, self.n)
            inc = (self.sem, 1)
        else:
            tok = None
            inc = None
        self.ops.append((w, fn, inc))
        if tok is not None:
            for r in reads:
                r.r.append(tok)
            for r in writes:
                r.w = tok
                r.r = []
        return tok

    def emit(self, e):
        for (w, fn, inc) in self.ops:
            for (s, v) in w:
                e.wait_ge(s, v)
            inst = fn(e)
            if inc is not None:
                inst.then_inc(inc[0], inc[1])


class Ctx:
    pass


def _build(debug=None):
    nc = bass.Bass("TRN2", target_bir_lowering=False)
    st = contextlib.ExitStack()
    E = st.enter_context

    dbg_outs = {}

    def din(name, shape, dt=F32):
        return nc.dram_tensor(name, list(shape), dt, kind="ExternalInput").ap()

    def dscr(name, shape, dt):
        if debug is not None and name in debug:
            dbg_outs[name] = (list(shape), dt)
            return nc.dram_tensor(name, list(shape), dt, kind="ExternalOutput").ap()
        return nc.dram_tensor(name, list(shape), dt, kind="Internal").ap()

    def dout(name, shape, dt=F32):
        return nc.dram_tensor(name, list(shape), dt, kind="ExternalOutput").ap()

    xp = din("xp", [LP, D])
    xs = din("xs", [debug.get("LSd", LS) if debug else LS, D])
    xo = din("xo", [debug.get("LOd", LO) if debug else LO, D])
    w_in = din("w_in", [D, 928])
    wkr = din("wkr", [D, 64])
    wq_nope = din("wq_nope", [256, 512])
    wq_rope = din("wq_rope", [256, 256])
    wq_rsw = din("wq_rsw", [256, 256])
    wk_nope = din("wk_nope", [128, 512])
    wv = din("wv", [128, 512])
    g1 = din("g1", [1, D])
    gq = din("gq", [128, 2])
    gkv = din("gkv", [128, 1])
    consts = din("consts", [128, NCONST])
    posv = din("posv", [1, LS])
    lamr_p = din("lamr_p", [128, 32])
    lami_p = din("lami_p", [128, 32])
    ldt_p = din("ldt_p", [128, 32])
    bre_p = din("bre_p", [128, 512])
    bim_p = din("bim_p", [128, 512])
    cre_p = din("cre_p", [128, 512])
    cim_p = din("cim_p", [128, 512])
    dskip_p = din("dskip_p", [128, 4])
    bglu_p = din("bglu_p", [128, 4])
    gs_p = din("gs_p", [128, 4])
    w_glu = din("w_glu", [512, 512])
    woa_d = din("woa_d", [512, D])
    wos_d = din("wos_d", [512, D])
    ga_d = din("ga_d", [128, 4])
    g2 = din("g2", [1, D])
    gf_d = din("gf_d", [1, D])
    w1_d = din("w1_d", [D, 4096])
    w2_d = din("w2_d", [4096, D])

    yp = dout("yp", [LP, D])
    yo = dout("yo", [LO, D])

    tabc_S = dscr("tabc_S", [128, LS], F32)
    tabs_S = dscr("tabs_S", [128, LS], F32)
    tabc_O = dscr("tabc_O", [128, LO], F32)
    tabs_O = dscr("tabs_O", [128, LO], F32)
    uT = {"P": dscr("uT_P", [512, LP], BF16), "S": dscr("uT_S", [512, LS], BF16), "O": dscr("uT_O", [512, LO], BF16)}
    QT = {"P": dscr("QT_P", [8, 96, LP], BF16), "O": dscr("QT_O", [8, 96, LO], BF16)}
    KT = {"P": dscr("KT_P", [8, 96, LP], BF16), "S": dscr("KT_S", [8, 96, LS], BF16)}
    VV = {"P": dscr("V_P", [8, 128, LP // 128, 65], BF16), "S": dscr("V_S", [8, 128, LS // 128, 65], BF16)}

    CsT_d = dscr("CsT_d", [128, 9 * 2 * 32 * 32], BF16)
    Gm_d = dscr("Gm_d", [128, 4 * 15 * 128], BF16)
    traj = {"P": dscr("traj_P", [128, 64, LP // 8], BF16), "O": dscr("traj_O", [128, 64, LO // 8], BF16)}
    sT = {"P": dscr("sT_P", [512, LP], BF16), "O": dscr("sT_O", [512, LO], BF16)}
    ydbg = {"P": dscr("ydbg_P", [512, LP], F32), "O": dscr("ydbg_O", [512, LO], F32)}
    aT = {"P": dscr("aT_P", [8, 64, LP], F32), "O": dscr("aT_O", [8, 64, LO], F32)}
    x2s = {"P": dscr("x2_P", [LP, D], F32), "O": dscr("x2_O", [LO, D], F32)}
    h2T = {"P": dscr("h2T_P", [D, LP], BF16), "O": dscr("h2T_O", [D, LO], BF16)}
    sems = {}
    for nm in ["sp", "act", "dve", "pool", "pe"]:
        sems[nm] = E(nc.semaphore("sem_" + nm))
    SP = Eng("sp", sems["sp"])
    ACT = Eng("act", sems["act"])
    DVE = Eng("dve", sems["dve"])
    POOL = Eng("pool", sems["pool"])
    PE = Eng("pe", sems["pe"])
    ENGS = [SP, ACT, DVE, POOL, PE]
    dsems = []
    import itertools
    _uid = itertools.count()

    def new_dsem(name):
        d = DSem(E(nc.semaphore("ds_" + name)))
        dsems.append(d)
        return d

    dump_list = []

    def dump(name, ap, res, shape, dt=F32):
        if debug is None or not debug.get("dump"):
            return
        t = nc.dram_tensor("dump_" + name, list(shape), dt, kind="ExternalOutput").ap()
        dsd = new_dsem("dump_" + name)
        SP.op(lambda e: e.dma_start(out=t, in_=ap), reads=[res], dsem=dsd)
        dump_list.append(name)

    def barrier():
        toks = [(e.sem, e.n) for e in ENGS if e.n > 0]
        toks += [(d.sem, d.n) for d in dsems if d.n > 0]
        for e in ENGS:
            w = e._waits(toks)
            if w:
                e.ops.append((w, (lambda en: en.nop()), None))

    banks = [E(nc.psum_tensor("bank%d" % i, [128, 512], F32)) for i in range(8)]
    bank_res = [Res("bank%d" % i) for i in range(8)]
    bank_ctr = [0]

    bank_ids = [list(range(8))]

    def next_bank():
        ids = bank_ids[0]
        i = ids[bank_ctr[0] % len(ids)]
        bank_ctr[0] += 1
        return banks[i], bank_res[i]

    def sb(name, shape, dt):
        return E(nc.sbuf_tensor(name, list(shape), dt))

    cst = sb("cst", [128, NCONST], F32)
    cst_r = Res("cst")
    ds_c = new_dsem("cst")
    SP.op(lambda e: e.dma_start(out=cst[:], in_=consts[:]), writes=[cst_r], dsem=ds_c)
    identb = sb("identb", [128, 128], BF16)
    identb_r = Res()
    onesb = sb("onesb", [128, 128], BF16)
    onesb_r = Res()
    DVE.op(lambda e: e.tensor_copy(out=identb[:], in_=cst[:, C_ID:C_ID + 128]), reads=[cst_r], writes=[identb_r])
    DVE.op(lambda e: e.memset(onesb[:], 1.0), writes=[onesb_r])

    def phase_tables():
        with contextlib.ExitStack() as ps:
            P_ = ps.enter_context
            CH = 2048
            pos = P_(nc.sbuf_tensor("tb_pos", [128, CH], F32)); pos_r = Res()
            ang = P_(nc.sbuf_tensor("tb_ang", [128, CH], F32)); ang_r = Res()
            ki = P_(nc.sbuf_tensor("tb_ki", [128, CH], I32)); ki_r = Res()
            kf = P_(nc.sbuf_tensor("tb_kf", [128, CH], F32)); kf_r = Res()
            r1 = P_(nc.sbuf_tensor("tb_r1", [128, CH], F32)); r1_r = Res()
            m1 = P_(nc.sbuf_tensor("tb_m1", [128, CH], F32)); m1_r = Res()
            m2 = P_(nc.sbuf_tensor("tb_m2", [128, CH], F32)); m2_r = Res()
            ws = P_(nc.sbuf_tensor("tb_ws", [128, CH], F32)); ws_r = Res()
            wc = P_(nc.sbuf_tensor("tb_wc", [128, CH], F32)); wc_r = Res()
            so = [P_(nc.sbuf_tensor("tb_so%d" % i, [128, CH], F32)) for i in range(2)]; so_r = [Res(), Res()]
            co = [P_(nc.sbuf_tensor("tb_co%d" % i, [128, CH], F32)) for i in range(2)]; co_r = [Res(), Res()]
            ds_so = [new_dsem("so0"), new_dsem("so1")]
            ds_co = [new_dsem("co0"), new_dsem("co1")]
            ds_pos = new_dsem("pos")
            halfpi = P_(nc.sbuf_tensor("tb_hpi", [128, 1], F32)); halfpi_r = Res()
            DVE.op(lambda e: e.memset(halfpi[:], PI / 2), writes=[halfpi_r])
            jobs = [("S", i * CH, False) for i in range(LS // CH)] + [("O", 0, True)]
            for it, (which, base, addoff) in enumerate(jobs):
                b = it % 2
                SP.op(lambda e, base=base: e.dma_start(out=pos[:], in_=posv[0:1, base:base + CH].broadcast_to([128, CH])), writes=[pos_r], dsem=ds_pos)
                if addoff:
                    DVE.op(lambda e: e.tensor_scalar(out=ang[:], in0=pos[:], scalar1=cst[:, C_OFF:C_OFF + 1],
                                                     scalar2=cst[:, C_INV:C_INV + 1], op0=ALU.add, op1=ALU.mult),
                           reads=[pos_r, cst_r], writes=[ang_r])
                else:
                    DVE.op(lambda e: e.tensor_scalar(out=ang[:], in0=pos[:], scalar1=cst[:, C_INV:C_INV + 1],
                                                     scalar2=None, op0=ALU.mult), reads=[pos_r, cst_r], writes=[ang_r])
                DVE.op(lambda e: e.tensor_scalar(out=ki[:], in0=ang[:], scalar1=1.0 / TWO_PI, scalar2=None, op0=ALU.mult),
                       reads=[ang_r], writes=[ki_r])
                DVE.op(lambda e: e.tensor_copy(out=kf[:], in_=ki[:]), reads=[ki_r], writes=[kf_r])
                DVE.op(lambda e: e.scalar_tensor_tensor(out=r1[:], in0=kf[:], scalar=-CW1, in1=ang[:], op0=ALU.mult, op1=ALU.add),
                       reads=[kf_r, ang_r], writes=[r1_r])
                DVE.op(lambda e: e.scalar_tensor_tensor(out=r1[:], in0=kf[:], scalar=-CW2, in1=r1[:], op0=ALU.mult, op1=ALU.add),
                       reads=[kf_r], writes=[r1_r])
                DVE.op(lambda e: e.tensor_scalar(out=ws[:], in0=r1[:], scalar1=-PI, scalar2=PI, op0=ALU.max, op1=ALU.min),
                       reads=[r1_r], writes=[ws_r])
                ACT.op(lambda e: e.activation(out=wc[:], in_=ws[:], func=AF.Abs), reads=[ws_r], writes=[wc_r])
                ACT.op(lambda e, b=b: e.activation(out=so[b][:], in_=ws[:], func=AF.Sin, scale=cst[:, C_SGN:C_SGN + 1]), reads=[ws_r, cst_r], writes=[so_r[b]])
                ACT.op(lambda e, b=b: e.activation(out=co[b][:], in_=wc[:], func=AF.Sin, scale=-1.0, bias=halfpi[:, 0:1]), reads=[wc_r, halfpi_r], writes=[co_r[b]])
                dc = tabc_S if which == "S" else tabc_O
                dsn = tabs_S if which == "S" else tabs_O
                SP.op(lambda e, b=b, dc=dc, base=base: e.dma_start(out=dc[:, base:base + CH], in_=co[b][:]), reads=[co_r[b]], dsem=ds_co[b])
                SP.op(lambda e, b=b, dsn=dsn, base=base: e.dma_start(out=dsn[:, base:base + CH], in_=so[b][:]), reads=[so_r[b]], dsem=ds_so[b])
            barrier()

    def phase_A():
        with contextlib.ExitStack() as ps:
            P_ = ps.enter_context

            def T(name, shape, dt):
                return P_(nc.sbuf_tensor(name, list(shape), dt))

            win_b = T("win_b", [128, 8, 928], BF16); win_r = Res()
            wkr_b = T("wkr_b", [128, 8, 64], BF16); wkr_r = Res()
            wqn_b = T("wqn_b", [128, 2, 512], BF16); wqn_r = Res()
            wqr_b = T("wqr_b", [128, 2, 256], BF16); wqr_r = Res()
            wqs_b = T("wqs_b", [128, 2, 256], BF16); wqs_r = Res()
            wkn_b = T("wkn_b", [128, 512], BF16); wkn_r = Res()
            wv_b = T("wv_b", [128, 512], BF16); wv_r = Res()
            g1b = T("g1b", [128, D], F32); g1b_r = Res()
            gq_s = T("gq_s", [128, 2], F32); gq_r = Res()
            gkv_s = T("gkv_s", [128, 1], F32); gkv_r = Res()
            dsw = new_dsem("wA")
            POOL.op(lambda e: e.dma_start(out=win_b[:], in_=w_in.rearrange("(k p) n -> p k n", p=128)), writes=[win_r], dsem=new_dsem("pq%d" % next(_uid)))
            POOL.op(lambda e: e.dma_start(out=wkr_b[:], in_=wkr.rearrange("(k p) n -> p k n", p=128)), writes=[wkr_r], dsem=new_dsem("pq%d" % next(_uid)))
            POOL.op(lambda e: e.dma_start(out=wqn_b[:], in_=wq_nope.rearrange("(k p) n -> p k n", p=128)), writes=[wqn_r], dsem=new_dsem("pq%d" % next(_uid)))
            POOL.op(lambda e: e.dma_start(out=wqr_b[:], in_=wq_rope.rearrange("(k p) n -> p k n", p=128)), writes=[wqr_r], dsem=new_dsem("pq%d" % next(_uid)))
            POOL.op(lambda e: e.dma_start(out=wqs_b[:], in_=wq_rsw.rearrange("(k p) n -> p k n", p=128)), writes=[wqs_r], dsem=new_dsem("pq%d" % next(_uid)))
            POOL.op(lambda e: e.dma_start(out=wkn_b[:], in_=wk_nope[:, :]), writes=[wkn_r], dsem=new_dsem("pq%d" % next(_uid)))
            POOL.op(lambda e: e.dma_start(out=wv_b[:], in_=wv[:, :]), writes=[wv_r], dsem=new_dsem("pq%d" % next(_uid)))
            SP.op(lambda e: e.dma_start(out=g1b[:], in_=g1.broadcast_to([128, D])), writes=[g1b_r], dsem=dsw)
            SP.op(lambda e: e.dma_start(out=gq_s[:], in_=gq[:, :]), writes=[gq_r], dsem=dsw)
            SP.op(lambda e: e.dma_start(out=gkv_s[:], in_=gkv[:, :]), writes=[gkv_r], dsem=dsw)

            NB = 2
            NXB = 3
            xt = [T("xt%d" % i, [128, 4, D], F32) for i in range(NXB)]; xt_r = [Res() for _ in range(NXB)]
            ds_x = [new_dsem("x%d" % i) for i in range(NXB)]
            junk = T("junk", [128, D], BF16); junk_r = Res()
            ssq = [T("ssq%d" % i, [128, 4], F32) for i in range(NB)]; ssq_r = [Res() for _ in range(NB)]
            rstd = [T("rstd%d" % i, [128, 4], F32) for i in range(NB)]; rstd_r = [Res() for _ in range(NB)]
            xn = [T("xn%d" % i, [128, 4, D], BF16) for i in range(NB)]; xn_r = [Res() for _ in range(NB)]
            hT = [T("hT%d" % i, [128, 8, 512], BF16) for i in range(NB)]; hT_r = [Res() for _ in range(NB)]
            NST = 6
            stg = [T("stg%d" % i, [128, 512], BF16) for i in range(NST)]; stg_r = [Res() for _ in range(NST)]
            ds_stg = [new_dsem("stg%d" % i) for i in range(NST)]
            stg_c = [0]

            def next_stg():
                i = stg_c[0] % NST
                stg_c[0] += 1
                return stg[i], stg_r[i], ds_stg[i]

            sq = [T("sq%d" % i, [128, 2, 512], BF16) for i in range(2)]; sq_r = [Res(), Res()]
            lnv = [T("lnv%d" % i, [128, 512], F32) for i in range(2)]; lnv_r = [Res(), Res()]
            cqn = T("cqn", [128, 2, 512], BF16); cqn_r = Res()
            ckvn = T("ckvn", [128, 512], BF16); ckvn_r = Res()
            tcos = [T("tcos%d" % i, [128, 512], F32) for i in range(3)]; tcos_r = [Res() for _ in range(3)]
            tsin = [T("tsin%d" % i, [128, 512], F32) for i in range(3)]; tsin_r = [Res() for _ in range(3)]
            ds_tab = [new_dsem("tab%d" % i) for i in range(3)]
            rt1 = T("rt1", [128, 512], F32); rt1_r = Res()
            rt2 = T("rt2", [128, 512], F32); rt2_r = Res()
            vaug = [T("vaug%d" % i, [128, 8, 4, 65], BF16) for i in range(2)]; vaug_r = [Res(), Res()]
            ds_v = [new_dsem("v0"), new_dsem("v1")]
            for i in range(2):
                DVE.op(lambda e, i=i: e.memset(vaug[i][:], 1.0), writes=[vaug_r[i]])

            def proj_group(dst_bank, dst_res, lhs_fn, hb, M=128, prow=0):
                for kk in range(8):
                    last = kk == 7
                    PE.op(lambda e, kk=kk: e.matmul(dst_bank[prow:prow + M, :], lhsT=lhs_fn(kk), rhs=hT[hb][:, kk, :],
                                                    start=(kk == 0), stop=(kk == 7)),
                          reads=([hT_r[hb], win_r, wkr_r] if last else []), writes=([dst_res] if last else []),
                          extra=([hT_r[hb].w, win_r.w, wkr_r.w, dst_res.w] + list(dst_res.r) if kk == 0 else []), mark=last)

            def rms_part1(src_list, sqb):
                for i, (bk, br) in enumerate(src_list):
                    ACT.op(lambda e, bk=bk, i=i: e.activation(out=sq[sqb][:, i, :], in_=bk[:, :], func=AF.Square), reads=[br], writes=[sq_r[sqb]])

            def rms_part2(src_list, nfeat, gs, out_tile, out_res, sqb):
                n = len(src_list)
                sbk, sbr = next_bank()
                for i in range(n):
                    last = i == n - 1
                    PE.op(lambda e, i=i: e.matmul(sbk[:, :], lhsT=onesb[:, :], rhs=sq[sqb][:, i, :], start=(i == 0), stop=(i == n - 1)),
                          reads=([sq_r[sqb], onesb_r] if last else []), writes=([sbr] if last else []),
                          extra=([sq_r[sqb].w, onesb_r.w, sbr.w] + list(sbr.r) if i == 0 else []), mark=last)
                ACT.op(lambda e: e.activation(out=lnv[sqb][:], in_=sbk[:, :], func=AF.Ln, scale=1.0 / nfeat, bias=EPS), reads=[sbr], writes=[lnv_r[sqb]])
                ACT.op(lambda e: e.activation(out=lnv[sqb][:], in_=lnv[sqb][:], func=AF.Exp, scale=-0.5), writes=[lnv_r[sqb]])
                for i, (bk, br) in enumerate(src_list):
                    DVE.op(lambda e, bk=bk, i=i: e.scalar_tensor_tensor(out=out_tile(i), in0=bk[:, :], scalar=gs(i), in1=lnv[sqb][:],
                                                                          op0=ALU.mult, op1=ALU.mult),
                           reads=[br, lnv_r[sqb], gq_r, gkv_r], writes=[out_res])

            def do_set(name, xd, N, tabc, tabs, want_q, want_kv):
                nmt = N // 512
                def loads(mt):
                    xb = mt % NXB
                    t0 = mt * 512
                    POOL.op(lambda e, xb=xb, t0=t0: e.dma_start(out=xt[xb][:], in_=xd[t0:t0 + 512, :].rearrange("(t p) d -> p t d", p=128)),
                           writes=[xt_r[xb]], dsem=ds_x[xb])
                    tb = mt % 3
                    POOL.op(lambda e, tb=tb, t0=t0: e.dma_start(out=tcos[tb][:], in_=tabc[:, t0:t0 + 512]), writes=[tcos_r[tb]], dsem=ds_tab[tb])
                    POOL.op(lambda e, tb=tb, t0=t0: e.dma_start(out=tsin[tb][:], in_=tabs[:, t0:t0 + 512]), writes=[tsin_r[tb]], dsem=ds_tab[tb])

                def stage1(mt):
                    b = mt % NB
                    xb = mt % NXB
                    t0 = mt * 512
                    for t in range(4):
                        ACT.op(lambda e, b=b, xb=xb, t=t: e.activation(out=junk[:], in_=xt[xb][:, t, :], func=AF.Square, accum_out=ssq[b][:, t:t + 1]),
                               reads=[xt_r[xb]], writes=[junk_r, ssq_r[b]])
                    ACT.op(lambda e, b=b: e.activation(out=rstd[b][:], in_=ssq[b][:], func=AF.Ln, scale=1.0 / D, bias=EPS), reads=[ssq_r[b]], writes=[rstd_r[b]])
                    ACT.op(lambda e, b=b: e.activation(out=rstd[b][:], in_=rstd[b][:], func=AF.Exp, scale=-0.5), writes=[rstd_r[b]])
                    for t in range(4):
                        DVE.op(lambda e, b=b, xb=xb, t=t: e.scalar_tensor_tensor(out=xn[b][:, t, :], in0=xt[xb][:, t, :], scalar=rstd[b][:, t:t + 1],
                                                                            in1=g1b[:], op0=ALU.mult, op1=ALU.mult),
                               reads=[xt_r[xb], rstd_r[b], g1b_r], writes=[xn_r[b]])

                def stage1b(mt):
                    b = mt % NB
                    for kk2 in range(4):
                        bk, br = next_bank()
                        bkb = bk[:, :].bitcast(BF16)
                        cnt = 0
                        for kq in range(2):
                            kk = 2 * kk2 + kq
                            for t in range(4):
                                cnt += 1
                                last = cnt == 8
                                PE.op(lambda e, kk=kk, t=t, kq=kq, bkb=bkb, b=b: e.transpose(out=bkb[:, kq * 512 + t * 128: kq * 512 + (t + 1) * 128],
                                                                                             in_=xn[b][:, t, kk * 128:(kk + 1) * 128], identity=identb[:, :]),
                                      reads=([xn_r[b], identb_r] if last else []), writes=([br] if last else []),
                                      extra=([xn_r[b].w, identb_r.w, br.w] + list(br.r) if cnt == 1 else []), mark=last)
                        eng = ACT if kk2 % 2 == 0 else DVE
                        if eng is ACT:
                            ACT.op(lambda e, kk2=kk2, bkb=bkb, b=b: e.activation(out=hT[b][:, 2 * kk2:2 * kk2 + 2, :].rearrange("p a b -> p (a b)"), in_=bkb, func=AF.Copy),
                                   reads=[br], writes=[hT_r[b]])
                        else:
                            DVE.op(lambda e, kk2=kk2, bkb=bkb, b=b: e.tensor_copy(out=hT[b][:, 2 * kk2:2 * kk2 + 2, :].rearrange("p a b -> p (a b)"), in_=bkb),
                                   reads=[br], writes=[hT_r[b]])

                def stage2(mt):
                    b = mt % NB
                    t0 = mt * 512
                    tb = mt % 3
                    qsrcs = []
                    if want_q:
                        for c2 in range(2):
                            bk, br = banks[5 + c2], bank_res[5 + c2]
                            proj_group(bk, br, lambda kk, c2=c2: win_b[:, kk, c2 * 128:(c2 + 1) * 128], b)
                            qsrcs.append((bk, br))
                        rms_part1(qsrcs, 0)
                    kvsrc = None
                    if want_kv:
                        bk, br = banks[7], bank_res[7]
                        proj_group(bk, br, lambda kk: win_b[:, kk, 256:384], b)
                        kvsrc = [(bk, br)]
                        rms_part1(kvsrc, 1)
                    for c4 in range(4):
                        bk, br = next_bank()
                        proj_group(bk, br, lambda kk, c4=c4: win_b[:, kk, 416 + c4 * 128: 416 + (c4 + 1) * 128], b)
                        sg, sgr, sgd = next_stg()
                        ACT.op(lambda e, sg=sg, bk=bk: e.activation(out=sg[:], in_=bk[:, :], func=AF.Copy), reads=[br], writes=[sgr])
                        SP.op(lambda e, sg=sg, c4=c4, t0=t0: e.dma_start(out=uT[name][c4 * 128:(c4 + 1) * 128, t0:t0 + 512], in_=sg[:]), reads=[sgr], dsem=sgd)
                    if want_kv:
                        bk1, br1 = next_bank()
                        bk2, br2 = next_bank()
                        proj_group(bk1, br1, lambda kk: wkr_b[:, kk, 0:32], b, M=32)
                        proj_group(bk2, br2, lambda kk: wkr_b[:, kk, 32:64], b, M=32)
                        DVE.op(lambda e, bk1=bk1, tb=tb: e.tensor_tensor(out=rt1[0:32, :], in0=bk1[0:32, :], in1=tcos[tb][0:32, :], op=ALU.mult), reads=[br1, tcos_r[tb]], writes=[rt1_r])
                        DVE.op(lambda e, bk2=bk2, tb=tb: e.tensor_tensor(out=rt2[0:32, :], in0=bk2[0:32, :], in1=tsin[tb][0:32, :], op=ALU.mult), reads=[br2, tsin_r[tb]], writes=[rt2_r])
                        sg, sgr, sgd = next_stg()
                        DVE.op(lambda e, sg=sg: e.tensor_tensor(out=sg[0:32, :], in0=rt1[0:32, :], in1=rt2[0:32, :], op=ALU.add), reads=[rt1_r, rt2_r], writes=[sgr])
                        for h in range(8):
                            SP.op(lambda e, sg=sg, h=h, t0=t0: e.dma_start(out=KT[name][h, 64:96, t0:t0 + 512], in_=sg[0:32, :]), reads=[sgr], dsem=sgd)
                    if want_q:
                        rms_part2(qsrcs, 256, lambda i: gq_s[:, i:i + 1], lambda i: cqn[:, i, :], cqn_r, 0)
                    if want_kv:
                        rms_part2(kvsrc, 128, lambda i: gkv_s[:, 0:1], lambda i: ckvn[:, :], ckvn_r, 1)
                    if want_q:
                        for hp in range(4):
                            bk, br = next_bank()
                            for kc in range(2):
                                last = kc == 1
                                PE.op(lambda e, kc=kc, hp=hp, bk=bk: e.matmul(bk[:, :], lhsT=wqn_b[:, kc, hp * 128:(hp + 1) * 128], rhs=cqn[:, kc, :], start=(kc == 0), stop=(kc == 1)),
                                      reads=([cqn_r, wqn_r] if last else []), writes=([br] if last else []),
                                      extra=([cqn_r.w, wqn_r.w, br.w] + list(br.r) if kc == 0 else []), mark=last)
                            sg, sgr, sgd = next_stg()
                            ACT.op(lambda e, sg=sg, bk=bk: e.activation(out=sg[:], in_=bk[:, :], func=AF.Copy), reads=[br], writes=[sgr])
                            for hh in range(2):
                                SP.op(lambda e, sg=sg, hp=hp, hh=hh, t0=t0: e.dma_start(out=QT[name][2 * hp + hh, 0:64, t0:t0 + 512], in_=sg[hh * 64:(hh + 1) * 64, :]),
                                      reads=[sgr], dsem=sgd)
                        for hg in range(2):
                            bk1, br1 = next_bank()
                            bk2, br2 = next_bank()
                            for (bk, br, wt, wr) in ((bk1, br1, wqr_b, wqr_r), (bk2, br2, wqs_b, wqs_r)):
                                for kc in range(2):
                                    last = kc == 1
                                    PE.op(lambda e, kc=kc, hg=hg, bk=bk, wt=wt: e.matmul(bk[:, :], lhsT=wt[:, kc, hg * 128:(hg + 1) * 128], rhs=cqn[:, kc, :], start=(kc == 0), stop=(kc == 1)),
                                          reads=([cqn_r, wr] if last else []), writes=([br] if last else []),
                                          extra=([cqn_r.w, wr.w, br.w] + list(br.r) if kc == 0 else []), mark=last)
                            DVE.op(lambda e, bk1=bk1, tb=tb: e.tensor_tensor(out=rt1[:], in0=bk1[:, :], in1=tcos[tb][:], op=ALU.mult), reads=[br1, tcos_r[tb]], writes=[rt1_r])
                            DVE.op(lambda e, bk2=bk2, tb=tb: e.tensor_tensor(out=rt2[:], in0=bk2[:, :], in1=tsin[tb][:], op=ALU.mult), reads=[br2, tsin_r[tb]], writes=[rt2_r])
                            sg, sgr, sgd = next_stg()
                            DVE.op(lambda e, sg=sg: e.tensor_tensor(out=sg[:], in0=rt1[:], in1=rt2[:], op=ALU.add), reads=[rt1_r, rt2_r], writes=[sgr])
                            for hh in range(4):
                                SP.op(lambda e, sg=sg, hg=hg, hh=hh, t0=t0: e.dma_start(out=QT[name][4 * hg + hh, 64:96, t0:t0 + 512], in_=sg[hh * 32:(hh + 1) * 32, :]),
                                      reads=[sgr], dsem=sgd)
                    if want_kv:
                        for hp in range(4):
                            bk, br = next_bank()
                            PE.op(lambda e, hp=hp, bk=bk: e.matmul(bk[:, :], lhsT=wkn_b[:, hp * 128:(hp + 1) * 128], rhs=ckvn[:, :], start=True, stop=True),
                                  reads=[ckvn_r, wkn_r], writes=[br])
                            sg, sgr, sgd = next_stg()
                            ACT.op(lambda e, sg=sg, bk=bk: e.activation(out=sg[:], in_=bk[:, :], func=AF.Copy), reads=[br], writes=[sgr])
                            for hh in range(2):
                                SP.op(lambda e, sg=sg, hp=hp, hh=hh, t0=t0: e.dma_start(out=KT[name][2 * hp + hh, 0:64, t0:t0 + 512], in_=sg[hh * 64:(hh + 1) * 64, :]),
                                      reads=[sgr], dsem=sgd)
                        vb = mt % 2
                        for t in range(4):
                            bk, br = next_bank()
                            PE.op(lambda e, t=t, bk=bk: e.matmul(bk[:, :], lhsT=ckvn[:, t * 128:(t + 1) * 128], rhs=wv_b[:, :], start=True, stop=True),
                                  reads=[ckvn_r, wv_r], writes=[br])
                            ACT.op(lambda e, t=t, bk=bk, vb=vb: e.activation(out=vaug[vb][:, :, t, 0:64], in_=bk[:, :].rearrange("p (h f) -> p h f", h=8), func=AF.Copy),
                                   reads=[br], writes=[vaug_r[vb]])
                        tile0 = t0 // 128
                        SP.op(lambda e, vb=vb, tile0=tile0: e.dma_start(out=VV[name][:, :, tile0:tile0 + 4, :].rearrange("h p t f -> p h (t f)"), in_=vaug[vb][:].rearrange("p h t f -> p h (t f)")),
                              reads=[vaug_r[vb]], dsem=ds_v[vb])

                loads(0)
                if nmt > 1:
                    loads(1)
                stage1(0)
                stage1b(0)
                for mt in range(nmt):
                    if mt + 2 < nmt:
                        loads(mt + 2)
                    if mt + 1 < nmt:
                        stage1(mt + 1)
                    stage2(mt)
                    if mt + 1 < nmt:
                        stage1b(mt + 1)

            bank_ids[0] = list(range(5))
            todo = debug.get("sets", ["P", "S", "O"]) if debug else ["P", "S", "O"]
            if "P" in todo:
                do_set("P", xp, debug.get("NP", LP) if debug else LP, tabc_S, tabs_S, True, True)
            if "S" in todo:
                do_set("S", xs, debug.get("NS", LS) if debug else LS, tabc_S, tabs_S, False, True)
            if "O" in todo:
                do_set("O", xo, debug.get("NO", LO) if debug else LO, tabc_O, tabs_O, True, False)
            bank_ids[0] = list(range(8))
            barrier()


    def phase_S():
        with contextlib.ExitStack() as ps:
            P_ = ps.enter_context

            def T(name, shape, dt):
                return P_(nc.sbuf_tensor(name, list(shape), dt))

            A1 = T("A1", [128, 64], F32); A1_r = Res()
            A2 = T("A2", [128, 64], F32); A2_r = Res()
            snap = T("snap", [128, 8, 64], F32); snap_r = Res()
            dsk = T("dsk", [128, 4], F32); dsk_r = Res()
            bgl = T("bgl", [128, 4], F32); bgl_r = Res()
            gss = T("gss", [128, 4], F32); gss_r = Res()
            wglu_b = T("wglu_b", [128, 4, 512], BF16); wglu_r = Res()
            zerob = T("zerob", [128, 128], BF16); zerob_r = Res()
            dsp = new_dsem("sprm")
            SP.op(lambda e: e.dma_start(out=dsk[:], in_=dskip_p[:, :]), writes=[dsk_r], dsem=dsp)
            SP.op(lambda e: e.dma_start(out=bgl[:], in_=bglu_p[:, :]), writes=[bgl_r], dsem=dsp)
            SP.op(lambda e: e.dma_start(out=gss[:], in_=gs_p[:, :]), writes=[gss_r], dsem=dsp)
            POOL.op(lambda e: e.dma_start(out=wglu_b[:], in_=w_glu.rearrange("(k p) n -> p k n", p=128)), writes=[wglu_r], dsem=new_dsem("pq%d" % next(_uid)))
            DVE.op(lambda e: e.memset(zerob[:], 0.0), writes=[zerob_r])
            DVE.op(lambda e: e.memset(snap[:], 0.0), writes=[snap_r])

            with contextlib.ExitStack() as ps2:
                P2 = ps2.enter_context

                def T2(name, shape, dt=F32):
                    return P2(nc.sbuf_tensor(name, list(shape), dt))

                BsT = T2("BsT", [128, 2, 4, 8, 2, 128], BF16); BsT_r = Res()
                with contextlib.ExitStack() as ps3:
                    P3 = ps3.enter_context

                    def T3(name, shape, dt=F32):
                        return P3(nc.sbuf_tensor(name, list(shape), dt)), Res()

                    CsT, CsT_r = T3("CsT", [128, 9, 2, 32, 32], BF16)
                    Gm, Gm_r = T3("Gm", [128, 4, 15, 128], BF16)
                    DVE.op(lambda e: e.memset(Gm[:], 0.0), writes=[Gm_r])
                    lamr, lamr_r = T3("lamr", [128, 32]); lami, lami_r = T3("lami", [128, 32]); ldt, ldt_r = T3("ldt", [128, 32])
                    bre, bre_r = T3("bre", [128, 32, 16]); bim, bim_r = T3("bim", [128, 32, 16])
                    cre, cre_r = T3("cre", [128, 32, 16]); cim, cim_r = T3("cim", [128, 32, 16])
                    SP.op(lambda e: e.dma_start(out=lamr[:], in_=lamr_p[:, :]), writes=[lamr_r], dsem=dsp)
                    SP.op(lambda e: e.dma_start(out=lami[:], in_=lami_p[:, :]), writes=[lami_r], dsem=dsp)
                    SP.op(lambda e: e.dma_start(out=ldt[:], in_=ldt_p[:, :]), writes=[ldt_r], dsem=dsp)
                    SP.op(lambda e: e.dma_start(out=bre[:].rearrange("p a b -> p (a b)"), in_=bre_p[:, :]), writes=[bre_r], dsem=dsp)
                    SP.op(lambda e: e.dma_start(out=bim[:].rearrange("p a b -> p (a b)"), in_=bim_p[:, :]), writes=[bim_r], dsem=dsp)
                    SP.op(lambda e: e.dma_start(out=cre[:].rearrange("p a b -> p (a b)"), in_=cre_p[:, :]), writes=[cre_r], dsem=dsp)
                    SP.op(lambda e: e.dma_start(out=cim[:].rearrange("p a b -> p (a b)"), in_=cim_p[:, :]), writes=[cim_r], dsem=dsp)

                    def V(fn, reads, writes):
                        return DVE.op(fn, reads=reads, writes=writes)

                    dt_, dt_r = T3("dt_", [128, 32]); th, th_r = T3("th", [128, 32]); rl, rl_r = T3("rl", [128, 32])
                    er, er_r = T3("er", [128, 32])
                    ACT.op(lambda e: e.activation(out=dt_[:], in_=ldt[:], func=AF.Exp), reads=[ldt_r], writes=[dt_r])
                    V(lambda e: e.tensor_tensor(out=th[:], in0=lami[:], in1=dt_[:], op=ALU.mult), [lami_r, dt_r], [th_r])
                    V(lambda e: e.tensor_tensor(out=rl[:], in0=lamr[:], in1=dt_[:], op=ALU.mult), [lamr_r, dt_r], [rl_r])
                    ACT.op(lambda e: e.activation(out=er[:], in_=rl[:], func=AF.Exp), reads=[rl_r], writes=[er_r])
                    ki, ki_r = T3("ski", [128, 32], I32); kf, kf_r = T3("skf", [128, 32]); r1, r1_r = T3("sr1", [128, 32])
                    m1, m1_r = T3("sm1", [128, 32]); m2, m2_r = T3("sm2", [128, 32]); ws, ws_r = T3("sws", [128, 32]); wc, wc_r = T3("swc", [128, 32])
                    sn, sn_r = T3("ssn", [128, 32]); cs_, cs_r = T3("scs", [128, 32])
                    V(lambda e: e.tensor_scalar(out=ki[:], in0=th[:], scalar1=1.0 / TWO_PI, scalar2=None, op0=ALU.mult), [th_r], [ki_r])
                    V(lambda e: e.tensor_copy(out=kf[:], in_=ki[:]), [ki_r], [kf_r])
                    V(lambda e: e.scalar_tensor_tensor(out=r1[:], in0=kf[:], scalar=-CW1, in1=th[:], op0=ALU.mult, op1=ALU.add), [kf_r, th_r], [r1_r])
                    V(lambda e: e.scalar_tensor_tensor(out=r1[:], in0=kf[:], scalar=-CW2, in1=r1[:], op0=ALU.mult, op1=ALU.add), [kf_r], [r1_r])
                    V(lambda e: e.tensor_scalar(out=m1[:], in0=r1[:], scalar1=PI, scalar2=-TWO_PI, op0=ALU.is_gt, op1=ALU.mult), [r1_r], [m1_r])
                    V(lambda e: e.tensor_scalar(out=m2[:], in0=r1[:], scalar1=-PI, scalar2=TWO_PI, op0=ALU.is_lt, op1=ALU.mult), [r1_r], [m2_r])
                    V(lambda e: e.tensor_tensor(out=m1[:], in0=m1[:], in1=m2[:], op=ALU.add), [m2_r], [m1_r])
                    V(lambda e: e.tensor_tensor(out=ws[:], in0=r1[:], in1=m1[:], op=ALU.add), [r1_r, m1_r], [ws_r])
                    V(lambda e: e.tensor_scalar(out=m2[:], in0=r1[:], scalar1=PI / 2, scalar2=-TWO_PI, op0=ALU.is_gt, op1=ALU.mult), [r1_r], [m2_r])
                    V(lambda e: e.scalar_tensor_tensor(out=wc[:], in0=r1[:], scalar=PI / 2, in1=m2[:], op0=ALU.add, op1=ALU.add), [r1_r, m2_r], [wc_r])
                    ACT.op(lambda e: e.activation(out=sn[:], in_=ws[:], func=AF.Sin), reads=[ws_r], writes=[sn_r])
                    ACT.op(lambda e: e.activation(out=cs_[:], in_=wc[:], func=AF.Sin), reads=[wc_r], writes=[cs_r])
                    pwr, pwr_r = T3("pwr", [128, 9, 32]); pwi, pwi_r = T3("pwi", [128, 9, 32])
                    V(lambda e: e.memset(pwr[:, 0, :], 1.0), [], [pwr_r])
                    V(lambda e: e.memset(pwi[:, 0, :], 0.0), [], [pwi_r])
                    V(lambda e: e.tensor_tensor(out=pwr[:, 1, :], in0=er[:], in1=cs_[:], op=ALU.mult), [er_r, cs_r], [pwr_r])
                    V(lambda e: e.tensor_tensor(out=pwi[:, 1, :], in0=er[:], in1=sn[:], op=ALU.mult), [er_r, sn_r], [pwi_r])
                    t1, t1_r = T3("t1", [128, 32]); t2, t2_r = T3("t2", [128, 32])
                    for m in range(1, 8):
                        V(lambda e, m=m: e.tensor_tensor(out=t1[:], in0=pwr[:, m, :], in1=pwr[:, 1, :], op=ALU.mult), [pwr_r], [t1_r])
                        V(lambda e, m=m: e.tensor_tensor(out=t2[:], in0=pwi[:, m, :], in1=pwi[:, 1, :], op=ALU.mult), [pwi_r], [t2_r])
                        V(lambda e, m=m: e.tensor_tensor(out=pwr[:, m + 1, :], in0=t1[:], in1=t2[:], op=ALU.subtract), [t1_r, t2_r], [pwr_r])
                        V(lambda e, m=m: e.tensor_tensor(out=t1[:], in0=pwr[:, m, :], in1=pwi[:, 1, :], op=ALU.mult), [pwr_r, pwi_r], [t1_r])
                        V(lambda e, m=m: e.tensor_tensor(out=t2[:], in0=pwi[:, m, :], in1=pwr[:, 1, :], op=ALU.mult), [pwr_r, pwi_r], [t2_r])
                        V(lambda e, m=m: e.tensor_tensor(out=pwi[:, m + 1, :], in0=t1[:], in1=t2[:], op=ALU.add), [t1_r, t2_r], [pwi_r])
                    V(lambda e: e.tensor_copy(out=A1[:, 0:32], in_=pwr[:, 8, :]), [pwr_r], [A1_r])
                    V(lambda e: e.tensor_copy(out=A1[:, 32:64], in_=pwr[:, 8, :]), [pwr_r], [A1_r])
                    V(lambda e: e.tensor_scalar(out=A2[:, 0:32], in0=pwi[:, 8, :], scalar1=-1.0, scalar2=None, op0=ALU.mult), [pwi_r], [A2_r])
                    V(lambda e: e.tensor_copy(out=A2[:, 32:64], in_=pwi[:, 8, :]), [pwi_r], [A2_r])
                    nr, nr_r = T3("nr", [128, 32]); den, den_r = T3("den", [128, 32]); qre, qre_r = T3("qre", [128, 32]); qim, qim_r = T3("qim", [128, 32])
                    V(lambda e: e.tensor_scalar(out=nr[:], in0=pwr[:, 1, :], scalar1=-1.0, scalar2=None, op0=ALU.add), [pwr_r], [nr_r])
                    V(lambda e: e.tensor_tensor(out=t1[:], in0=lamr[:], in1=lamr[:], op=ALU.mult), [lamr_r], [t1_r])
                    V(lambda e: e.tensor_tensor(out=t2[:], in0=lami[:], in1=lami[:], op=ALU.mult), [lami_r], [t2_r])
                    V(lambda e: e.tensor_tensor(out=den[:], in0=t1[:], in1=t2[:], op=ALU.add), [t1_r, t2_r], [den_r])
                    V(lambda e: e.reciprocal(out=den[:], in_=den[:]), [], [den_r])
                    V(lambda e: e.tensor_tensor(out=t1[:], in0=nr[:], in1=lamr[:], op=ALU.mult), [nr_r, lamr_r], [t1_r])
                    V(lambda e: e.tensor_tensor(out=t2[:], in0=pwi[:, 1, :], in1=lami[:], op=ALU.mult), [pwi_r, lami_r], [t2_r])
                    V(lambda e: e.tensor_tensor(out=t1[:], in0=t1[:], in1=t2[:], op=ALU.add), [t2_r], [t1_r])
                    V(lambda e: e.tensor_tensor(out=qre[:], in0=t1[:], in1=den[:], op=ALU.mult), [t1_r, den_r], [qre_r])
                    V(lambda e: e.tensor_tensor(out=t1[:], in0=pwi[:, 1, :], in1=lamr[:], op=ALU.mult), [pwi_r, lamr_r], [t1_r])
                    V(lambda e: e.tensor_tensor(out=t2[:], in0=nr[:], in1=lami[:], op=ALU.mult), [nr_r, lami_r], [t2_r])
                    V(lambda e: e.tensor_tensor(out=t1[:], in0=t1[:], in1=t2[:], op=ALU.subtract), [t2_r], [t1_r])
                    V(lambda e: e.tensor_tensor(out=qim[:], in0=t1[:], in1=den[:], op=ALU.mult), [t1_r, den_r], [qim_r])

                    dump("pwr", pwr[:].rearrange("p a b -> p (a b)"), pwr_r, [128, 288])
                    dump("pwi", pwi[:].rearrange("p a b -> p (a b)"), pwi_r, [128, 288])
                    dump("qre", qre[:], qre_r, [128, 32])
                    dump("qim", qim[:], qim_r, [128, 32])
                    dump("sn", sn[:], sn_r, [128, 32])
                    dump("cs", cs_[:], cs_r, [128, 32])
                    dump("th", th[:], th_r, [128, 32])
                    dump("er", er[:], er_r, [128, 32])
                    def bc(ap32):
                        return ap32.unsqueeze(2).broadcast_to([128, 32, 16])

                    def cmul(outr, outr_r, outi, outi_r, sr, si, sres, xr, xr_r, xi, xi_r, ta, ta_r, tb, tb_r):
                        V(lambda e: e.tensor_tensor(out=ta[:], in0=xr[:], in1=bc(sr), op=ALU.mult), [xr_r] + sres, [ta_r])
                        V(lambda e: e.tensor_tensor(out=tb[:], in0=xi[:], in1=bc(si), op=ALU.mult), [xi_r] + sres, [tb_r])
                        V(lambda e: e.tensor_tensor(out=outr[:], in0=ta[:], in1=tb[:], op=ALU.subtract), [ta_r, tb_r], [outr_r])
                        V(lambda e: e.tensor_tensor(out=ta[:], in0=xi[:], in1=bc(sr), op=ALU.mult), [xi_r] + sres, [ta_r])
                        V(lambda e: e.tensor_tensor(out=tb[:], in0=xr[:], in1=bc(si), op=ALU.mult), [xr_r] + sres, [tb_r])
                        V(lambda e: e.tensor_tensor(out=outi[:], in0=ta[:], in1=tb[:], op=ALU.add), [ta_r, tb_r], [outi_r])

                    ta, ta_r = T3("ta", [128, 32, 16]); tb, tb_r = T3("tb", [128, 32, 16])
                    bbr, bbr_r = T3("bbr", [128, 32, 16]); bbi, bbi_r = T3("bbi", [128, 32, 16])
                    cmul(bbr, bbr_r, bbi, bbi_r, qre[:], qim[:], [qre_r, qim_r], bre, bre_r, bim, bim_r, ta, ta_r, tb, tb_r)
                    wr, wr_r = T3("wr", [128, 32, 16]); wi, wi_r = T3("wi", [128, 32, 16])
                    W2 = [[T3("W2_%d_%d" % (m, ri), [128, 32, 2, 16], BF16) for ri in range(2)] for m in range(8)]
                    for m in range(8):
                        cmul(wr, wr_r, wi, wi_r, pwr[:, m, :], pwi[:, m, :], [pwr_r, pwi_r], bbr, bbr_r, bbi, bbi_r, ta, ta_r, tb, tb_r)
                        for ri, (src, src_r) in enumerate(((wr, wr_r), (wi, wi_r))):
                            for two, col in ((0, C_ME), (1, C_MO)):
                                V(lambda e, m=m, ri=ri, src=src, two=two, col=col: e.tensor_scalar(out=W2[m][ri][0][:, :, two, :], in0=src[:], scalar1=cst[:, col:col + 1], scalar2=None, op0=ALU.mult),
                                  [src_r, cst_r], [W2[m][ri][1]])
                    for m in range(9):
                        cmul(wr, wr_r, wi, wi_r, pwr[:, m, :], pwi[:, m, :], [pwr_r, pwi_r], cre, cre_r, cim, cim_r, ta, ta_r, tb, tb_r)
                        for ri, (src, src_r) in enumerate(((wr, wr_r), (wi, wi_r))):
                            for two, col in ((0, C_ME if ri == 0 else C_NME), (1, C_MO if ri == 0 else C_NMO)):
                                V(lambda e, m=m, ri=ri, src=src, two=two, col=col: e.tensor_scalar(
                                    out=CsT[:, m, ri, :, two * 16:(two + 1) * 16], in0=src[:], scalar1=cst[:, col:col + 1], scalar2=None, op0=ALU.mult),
                                  [src_r, cst_r], [CsT_r])
                    dump("bbr", bbr[:].rearrange("p a b -> p (a b)"), bbr_r, [128, 512])
                    dump("bre", bre[:].rearrange("p a b -> p (a b)"), bre_r, [128, 512])
                    dump("W2_0_0", W2[0][0][0][:].rearrange("p a b c -> p (a b c)"), W2[0][0][1], [128, 1024], BF16)
                    dump("W2_7_1", W2[7][1][0][:].rearrange("p a b c -> p (a b c)"), W2[7][1][1], [128, 1024], BF16)
                    cnt = 0
                    for d in range(2):
                        for blk in range(4):
                            for mm in range(2):
                                bk, br = next_bank()
                                bkb = bk[:, :].bitcast(BF16)
                                n_in = 0
                                for m4 in range(4):
                                    m = mm * 4 + m4
                                    for ri in range(2):
                                        n_in += 1
                                        last = n_in == 8
                                        src = W2[m][ri][0][:, d * 16 + blk * 4: d * 16 + blk * 4 + 4, :, :].rearrange("p a b c -> p (a b c)")
                                        PE.op(lambda e, src=src, bkb=bkb, o=(m4 * 2 + ri) * 128: e.transpose(out=bkb[:, o:o + 128], in_=src, identity=identb[:, :]),
                                              reads=([W2[mx][rx][1] for mx in range(8) for rx in range(2)] + [identb_r] if last else []), writes=([br] if last else []),
                                              extra=([W2[mx][rx][1].w for mx in range(8) for rx in range(2)] + [br.w] + list(br.r) if n_in == 1 else []), mark=last)
                                dst = BsT[:, d, blk, mm * 4:(mm + 1) * 4, :, :].rearrange("p a b c -> p (a b c)")
                                if cnt % 2 == 0:
                                    ACT.op(lambda e, dst=dst, bkb=bkb: e.activation(out=dst, in_=bkb, func=AF.Copy), reads=[br], writes=[BsT_r])
                                else:
                                    DVE.op(lambda e, dst=dst, bkb=bkb: e.tensor_copy(out=dst, in_=bkb), reads=[br], writes=[BsT_r])
                                cnt += 1
                    dump("BsT", BsT[:].rearrange("p a b c d e -> p (a b c d e)"), BsT_r, [128, 2 * 4 * 8 * 2 * 128], BF16)
                    g0, g0_r = T3("g0", [128, 128])
                    for blk in range(4):
                        for d in range(2):
                            for kq in range(8):
                                bk, br = next_bank()
                                for ri in range(2):
                                    lhs = W2[kq][ri][0][:, d * 16 + blk * 4: d * 16 + blk * 4 + 4, :, :].rearrange("p a b c -> p (a b c)")
                                    rhs = CsT[:, 0, ri, d * 16 + blk * 4: d * 16 + blk * 4 + 4, :].rearrange("p a b -> p (a b)")
                                    last = ri == 1
                                    PE.op(lambda e, lhs=lhs, rhs=rhs, bk=bk, ri=ri: e.matmul(bk[:, 0:128], lhsT=lhs, rhs=rhs, start=(ri == 0), stop=(ri == 1)),
                                          reads=([W2[kq][0][1], W2[kq][1][1], CsT_r] if last else []), writes=([br] if last else []),
                                          extra=([W2[kq][0][1].w, W2[kq][1][1].w, CsT_r.w, br.w] + list(br.r) if ri == 0 else []), mark=last)
                                if kq == 0:
                                    if d == 0:
                                        V(lambda e, bk=bk: e.tensor_tensor(out=g0[:], in0=bk[:, 0:128], in1=cst[:, C_BM:C_BM + 128], op=ALU.mult), [br, cst_r], [g0_r])
                                        V(lambda e, blk=blk: e.scalar_tensor_tensor(out=g0[:], in0=cst[:, C_ID:C_ID + 128], scalar=dsk[:, blk:blk + 1], in1=g0[:], op0=ALU.mult, op1=ALU.add),
                                          [cst_r, dsk_r], [g0_r])
                                    else:
                                        V(lambda e, bk=bk: e.tensor_tensor(out=ta[:, 0:8, :].rearrange("p a b -> p (a b)"), in0=bk[:, 0:128], in1=cst[:, C_BM:C_BM + 128], op=ALU.mult), [br, cst_r], [ta_r])
                                        V(lambda e, blk=blk: e.tensor_tensor(out=Gm[:, blk, 0, :], in0=g0[:], in1=ta[:, 0:8, :].rearrange("p a b -> p (a b)"), op=ALU.add), [g0_r, ta_r], [Gm_r])
                                else:
                                    kk = kq if d == 0 else 7 + kq
                                    V(lambda e, bk=bk, blk=blk, kk=kk: e.tensor_tensor(out=Gm[:, blk, kk, :], in0=bk[:, 0:128], in1=cst[:, C_BM:C_BM + 128], op=ALU.mult), [br, cst_r], [Gm_r])
                    ds_sp = new_dsem("spill")
                    SP.op(lambda e: e.dma_start(out=CsT_d[:, :], in_=CsT[:].rearrange("p a b c d -> p (a b c d)")), reads=[CsT_r], dsem=ds_sp)
                    SP.op(lambda e: e.dma_start(out=Gm_d[:, :], in_=Gm[:].rearrange("p a b c -> p (a b c)")), reads=[Gm_r], dsem=ds_sp)
                    barrier()

                CHS = 64
                Bbuf = [T2("Bbuf%d" % i, [128, 64, CHS]) for i in range(2)]; Bbuf_r = [Res(), Res()]
                Xb = T2("Xb", [128, 96, CHS + 1]); Xb_r = Res()
                T1 = T2("T1s", [128, 64]); T1_r = Res()
                T2s = T2("T2s", [128, 64]); T2s_r = Res()
                Us = T2("Us", [128, 64]); Us_r = Res()
                ufw = [T2("ufw%d" % i, [128, 4, CHS * 8], BF16) for i in range(2)]; ufw_r = [Res(), Res()]
                ubw = [T2("ubw%d" % i, [128, 4, CHS * 8], BF16) for i in range(2)]; ubw_r = [Res(), Res()]
                ds_uf = [new_dsem("uf0"), new_dsem("uf1")]
                ds_ub = [new_dsem("ub0"), new_dsem("ub1")]
                tjb = [T2("tjb%d" % i, [128, 64, CHS], BF16) for i in range(2)]; tjb_r = [Res(), Res()]
                ds_tj = [new_dsem("tj0"), new_dsem("tj1")]

                sc_ctr = [0]

                def scan_bank():
                    i = 4 + sc_ctr[0] % 4
                    sc_ctr[0] += 1
                    return banks[i], bank_res[i]

                def produce_B(name, N, kc):
                    b = kc % 2
                    tf0 = kc * CHS * 8
                    tb0 = N - (kc + 1) * CHS * 8
                    POOL.op(lambda e, b=b, tf0=tf0: e.dma_start(out=ufw[b][:], in_=uT[name][:, tf0:tf0 + CHS * 8].rearrange("(k p) t -> p k t", p=128)), writes=[ufw_r[b]], dsem=ds_uf[b])
                    POOL.op(lambda e, b=b, tb0=tb0: e.dma_start(out=ubw[b][:], in_=uT[name][:, tb0:tb0 + CHS * 8].rearrange("(k p) t -> p k t", p=128)), writes=[ubw_r[b]], dsem=ds_ub[b])
                    for d in range(2):
                        for ri in range(2):
                            bks = [scan_bank() for _ in range(4)]
                            for q4 in range(4):
                                for j in range(8):
                                    for pl in range(4):
                                        bk, br = bks[pl]
                                        pr = 32 * pl
                                        m = 7 - j if d == 0 else j
                                        if d == 0:
                                            rhs = ufw[b][pr:pr + 32, q4, j::8]
                                        else:
                                            rhs = ubw[b][pr:pr + 32, q4, (CHS - 1) * 8 + j::-8]
                                        first = (q4 == 0 and j == 0)
                                        last = (q4 == 3 and j == 7)
                                        ures = ufw_r[b] if d == 0 else ubw_r[b]
                                        PE.op(lambda e, bk=bk, pl=pl, pr=pr, d=d, q4=q4, m=m, ri=ri, rhs=rhs, j=j: e.matmul(
                                            bk[:, q4 * CHS:(q4 + 1) * CHS], lhsT=BsT[pr:pr + 32, d, q4, m, ri, :], rhs=rhs, start=(j == 0), stop=(j == 7), tile_position=(pr, 0)),
                                            reads=([ures, BsT_r] if last else []), writes=([br] if last else []),
                                            extra=([ures.w, BsT_r.w, br.w] + list(br.r) if first else []), mark=last)
                            for pl in range(4):
                                bk, br = bks[pl]
                                c0 = ri * 32 + d * 16 + pl
                                ACT.op(lambda e, bk=bk, b=b, c0=c0, pl=pl: e.activation(out=Bbuf[b][:, c0:c0 - pl + 16:4, :], in_=bk[:, 0:4 * CHS].rearrange("p (a b) -> p a b", a=4), func=AF.Copy),
                                       reads=[br], writes=[Bbuf_r[b]])
                            yield None

                def scan(name, N, init_zero, want_traj, want_snap):
                    S_ = N // 8
                    nch = S_ // CHS
                    if init_zero:
                        DVE.op(lambda e: e.memset(Xb[:, :, 0], 0.0), writes=[Xb_r])
                    for _ in produce_B(name, N, 0):
                        pass
                    for kc in range(nch):
                        b = kc % 2
                        pb = produce_B(name, N, kc + 1) if kc + 1 < nch else iter(())
                        for r in range(CHS):
                            if r % 8 == 0:
                                if (r // 8) % 2 == 0:
                                    next(pb, None)
                                if r > 0:
                                    yield 14.0
                            DVE.op(lambda e, r=r: e.tensor_tensor(out=T1[:], in0=A1[:], in1=Xb[:, 0:64, r], op=ALU.mult), reads=[A1_r, Xb_r], writes=[T1_r])
                            DVE.op(lambda e, r=r: e.tensor_tensor(out=T2s[:], in0=A2[:], in1=Xb[:, 32:96, r], op=ALU.mult), reads=[A2_r, Xb_r], writes=[T2s_r])
                            DVE.op(lambda e: e.tensor_tensor(out=Us[:], in0=T1[:], in1=T2s[:], op=ALU.add), reads=[T1_r, T2s_r], writes=[Us_r])
                            DVE.op(lambda e, r=r, b=b: e.tensor_tensor(out=Xb[:, 0:64, r + 1], in0=Us[:], in1=Bbuf[b][:, :, r], op=ALU.add), reads=[Us_r, Bbuf_r[b]], writes=[Xb_r])
                            DVE.op(lambda e, r=r, b=b: e.tensor_tensor(out=Xb[:, 64:96, r + 1], in0=Us[:, 0:32], in1=Bbuf[b][:, 0:32, r], op=ALU.add), reads=[Us_r, Bbuf_r[b]], writes=[Xb_r])
                        if want_traj:
                            ACT.op(lambda e, b=b: e.activation(out=tjb[b][:], in_=Xb[:, 0:64, 0:CHS], func=AF.Copy), reads=[Xb_r], writes=[tjb_r[b]])
                            SP.op(lambda e, b=b, kc=kc: e.dma_start(out=traj[name][:, :, kc * CHS:(kc + 1) * CHS], in_=tjb[b][:]), reads=[tjb_r[b]], dsem=ds_tj[b])
                        if want_snap and ((kc + 1) * CHS) % 256 == 0 and kc + 1 < nch:
                            mq = ((kc + 1) * CHS) // 256
                            xv = Xb[:, 0:64, CHS].rearrange("p (a b c) -> p a b c", a=2, b=2)
                            DVE.op(lambda e, mq=mq, xv=xv: e.tensor_copy(out=snap[:, mq, :].rearrange("p (a b c) -> p a b c", a=2, b=2)[:, :, 0, :], in_=xv[:, :, 0, :]),
                                   reads=[Xb_r], writes=[snap_r])
                            DVE.op(lambda e, mq=mq, xv=xv: e.tensor_copy(out=snap[:, 7 - mq, :].rearrange("p (a b c) -> p a b c", a=2, b=2)[:, :, 1, :], in_=xv[:, :, 1, :]),
                                   reads=[Xb_r], writes=[snap_r])
                        DVE.op(lambda e: e.tensor_copy(out=Xb[:, :, 0], in_=Xb[:, :, CHS]), reads=[], writes=[Xb_r])
                        for _ in pb:
                            pass
                        yield 14.0

                todo = debug.get("ssets", ["P", "S", "O"]) if debug else ["P", "S", "O"]

                def scan_all():
                    if "P" in todo:
                        yield from scan("P", debug.get("NP", LP) if debug else LP, True, True, False)
                    if "S" in todo:
                        yield from scan("S", LS, True, False, True)
                    if "O" in todo:
                        DVE.op(lambda e: e.memset(Xb[:, :, 0], 0.0), writes=[Xb_r])
                        for q in range(8):
                            DVE.op(lambda e, q=q: e.scalar_tensor_tensor(out=Xb[:, 0:64, 0], in0=snap[:, q, :], scalar=cst[:, C_SEL + q:C_SEL + q + 1], in1=Xb[:, 0:64, 0], op0=ALU.mult, op1=ALU.add),
                                   reads=[snap_r, cst_r], writes=[Xb_r])
                        DVE.op(lambda e: e.tensor_copy(out=Xb[:, 64:96, 0], in_=Xb[:, 0:32, 0]), writes=[Xb_r])
                        yield from scan("O", LO, False, True, False)

                gens = [scan_all()]
                if not (debug and debug.get("skipT")):
                    gens.append(attention_setup(T2))
                clock = [0.0 for _ in gens]
                alive = [True for _ in gens]
                while any(alive):
                    cand = [i for i in range(len(gens)) if alive[i]]
                    i = min(cand, key=lambda i: clock[i])
                    try:
                        clock[i] += next(gens[i])
                    except StopIteration:
                        alive[i] = False
                barrier()

            with contextlib.ExitStack() as ps4:
                P4 = ps4.enter_context

                def T4(name, shape, dt=F32):
                    return P4(nc.sbuf_tensor(name, list(shape), dt)), Res()

                HC = 64
                CsTo, CsTo_r = T4("CsTo", [128, 9, 2, 32, 32], BF16)
                Gmo, Gmo_r = T4("Gmo", [128, 4, 15, 128], BF16)
                ds_rl = new_dsem("reload")
                SP.op(lambda e: e.dma_start(out=CsTo[:].rearrange("p a b c d -> p (a b c d)"), in_=CsT_d[:, :]), writes=[CsTo_r], dsem=ds_rl)
                SP.op(lambda e: e.dma_start(out=Gmo[:].rearrange("p a b c -> p (a b c)"), in_=Gm_d[:, :]), writes=[Gmo_r], dsem=ds_rl)
                uh = [T4("uh%d" % i, [128, 4, 512], BF16) for i in range(2)]
                xf = [T4("xf%d" % i, [128, 2, 16, HC], BF16) for i in range(2)]
                xw = [T4("xw%d" % i, [128, 2, 16, HC], BF16) for i in range(2)]
                ds_o = [new_dsem("so_in0"), new_dsem("so_in1")]
                _y = [T4("ysb%d" % i, [128, 4, 512]) for i in range(2)]
                ysb = [_y[0][0], _y[1][0]]; ysb_r = [_y[0][1], _y[1][1]]
                y2, y2_r = T4("y2", [128, 4, 512])
                vv, vv_r = T4("vv", [128, 4, 512])
                gf, gf_r = T4("gf", [128, 4, 512])
                gb, gb_r = T4("gb", [128, 4, 512], BF16)
                sf, sf_r = T4("sf", [128, 4, 512])
                sg2, sg2_r = T4("sg2", [128, 512])
                sqs, sqs_r = T4("sqs", [128, 4, 512], BF16)
                rsd, rsd_r = T4("rsd", [128, 512])
                snb = [T4("snb%d" % i, [128, 4, 512], BF16) for i in range(2)]
                ds_sn = [new_dsem("sn0"), new_dsem("sn1")]
                ds_yd = new_dsem("ydbg")

                def outstage(name, N):
                    S_ = N // 8
                    nhc = N // 512

                    def stP(hc):
                        b = hc % 2
                        t0 = hc * 512
                        s0 = hc * HC
                        tauf = s0
                        taub = S_ - s0 - HC
                        POOL.op(lambda e, b=b, t0=t0: e.dma_start(out=uh[b][0][:], in_=uT[name][:, t0:t0 + 512].rearrange("(k p) t -> p k t", p=128)), writes=[uh[b][1]], dsem=ds_o[b])
                        trv = traj[name].rearrange("p (a b c) s -> p a b c s", a=2, b=2)
                        POOL.op(lambda e, b=b, tauf=tauf, trv=trv: e.dma_start(out=xf[b][0][:], in_=trv[:, :, 0, :, tauf:tauf + HC]), writes=[xf[b][1]], dsem=ds_o[b])
                        POOL.op(lambda e, b=b, taub=taub, trv=trv: e.dma_start(out=xw[b][0][:], in_=trv[:, :, 1, :, taub:taub + HC]), writes=[xw[b][1]], dsem=ds_o[b])
                        for blk in range(4):
                            bk, br = next_bank()
                            deps_r = [uh[b][1], xf[b][1], xw[b][1], Gmo_r, CsTo_r, zerob_r]
                            PE.op(lambda e, bk=bk, b=b, blk=blk: e.matmul(bk[:, :], lhsT=zerob[:, :], rhs=uh[b][0][:, blk, :], start=True, stop=False),
                                  extra=[x.w for x in deps_r] + [br.w] + list(br.r), mark=False)
                            for k in range(8):
                                PE.op(lambda e, bk=bk, b=b, blk=blk, k=k: e.matmul(
                                    bk[:, k * HC:8 * HC].rearrange("p (i s) -> p i s", s=HC), lhsT=Gmo[:, blk, k, :],
                                    rhs=uh[b][0][:, blk, :].rearrange("p (s j) -> p j s", j=8)[:, 0:8 - k, :], start=False, stop=False), mark=False)
                            for k in range(1, 8):
                                PE.op(lambda e, bk=bk, b=b, blk=blk, k=k: e.matmul(
                                    bk[:, 0:(8 - k) * HC].rearrange("p (i s) -> p i s", s=HC), lhsT=Gmo[:, blk, 7 + k, :],
                                    rhs=uh[b][0][:, blk, :].rearrange("p (s j) -> p j s", j=8)[:, k:8, :], start=False, stop=False), mark=False)
                            n = 0
                            for d in range(2):
                                for pl in range(4):
                                    Pp = blk * 4 + pl
                                    for i in range(8):
                                        m = i + 1 if d == 0 else 8 - i
                                        for ri in range(2):
                                            n += 1
                                            last = n == 128
                                            if d == 0:
                                                rhs = xf[b][0][:, ri, Pp, :]
                                            else:
                                                rhs = xw[b][0][:, ri, Pp, ::-1]
                                            PE.op(lambda e, bk=bk, pl=pl, i=i, m=m, ri=ri, d=d, Pp=Pp, rhs=rhs, last=last: e.matmul(
                                                bk[32 * pl:32 * pl + 32, i * HC:(i + 1) * HC], lhsT=CsTo[:, m, ri, d * 16 + Pp, :], rhs=rhs, start=False, stop=last, tile_position=(0, 32 * pl)),
                                                reads=(deps_r if last else []), writes=([br] if last else []), mark=last)
                            ACT.op(lambda e, bk=bk, blk=blk: e.activation(out=ysb[b][:, blk, :].rearrange("p (s i) -> p i s", i=8), in_=bk[:, :].rearrange("p (i s) -> p i s", i=8), func=AF.Copy),
                                   reads=[br], writes=[ysb_r[b]])

                    def stQ(hc):
                        b = hc % 2
                        t0 = hc * 512
                        if debug is not None and ("ydbg_" + name) in debug:
                            SP.op(lambda e, t0=t0: e.dma_start(out=ydbg[name][:, t0:t0 + 512].rearrange("(k p) t -> p k t", p=128), in_=ysb[b][:]), reads=[ysb_r[b]], dsem=ds_yd)
                        fl = lambda t: t[:].rearrange("p a b -> p (a b)")
                        ACT.op(lambda e: e.activation(out=fl(y2), in_=fl(ysb[b]), func=AF.Square), reads=[ysb_r[b]], writes=[y2_r])
                        DVE.op(lambda e: e.tensor_scalar(out=fl(y2), in0=fl(y2), scalar1=0.044715, scalar2=1.0, op0=ALU.mult, op1=ALU.add), writes=[y2_r])
                        DVE.op(lambda e: e.tensor_tensor(out=fl(vv), in0=fl(y2), in1=fl(ysb[b]), op=ALU.mult), reads=[y2_r, ysb_r[b]], writes=[vv_r])
                        ACT.op(lambda e: e.activation(out=fl(vv), in_=fl(vv), func=AF.Sigmoid, scale=1.5957691216057308), writes=[vv_r])
                        DVE.op(lambda e: e.tensor_tensor(out=fl(gf), in0=fl(vv), in1=fl(ysb[b]), op=ALU.mult), reads=[vv_r, ysb_r[b]], writes=[gf_r])
                        ACT.op(lambda e: e.activation(out=fl(gb), in_=fl(gf), func=AF.Copy), reads=[gf_r], writes=[gb_r])
                        for bo in range(4):
                            bk, br = next_bank()
                            for bi in range(4):
                                last = bi == 3
                                PE.op(lambda e, bk=bk, bi=bi, bo=bo: e.matmul(bk[:, :], lhsT=wglu_b[:, bi, bo * 128:(bo + 1) * 128], rhs=gb[:, bi, :], start=(bi == 0), stop=(bi == 3)),
                                      reads=([gb_r, wglu_r] if last else []), writes=([br] if last else []),
                                      extra=([gb_r.w, wglu_r.w, br.w] + list(br.r) if bi == 0 else []), mark=last)
                            ACT.op(lambda e, bk=bk, bo=bo: e.activation(out=sg2[:], in_=bk[:, :], func=AF.Sigmoid, bias=bgl[:, bo:bo + 1]), reads=[br, bgl_r], writes=[sg2_r])
                            DVE.op(lambda e, bo=bo: e.tensor_tensor(out=sf[:, bo, :], in0=gf[:, bo, :], in1=sg2[:], op=ALU.mult), reads=[gf_r, sg2_r], writes=[sf_r])
                        ACT.op(lambda e: e.activation(out=fl(sqs), in_=fl(sf), func=AF.Square), reads=[sf_r], writes=[sqs_r])
                        bk, br = next_bank()
                        for bi in range(4):
                            last = bi == 3
                            PE.op(lambda e, bk=bk, bi=bi: e.matmul(bk[:, :], lhsT=onesb[:, :], rhs=sqs[:, bi, :], start=(bi == 0), stop=(bi == 3)),
                                  reads=([sqs_r, onesb_r] if last else []), writes=([br] if last else []),
                                  extra=([sqs_r.w, onesb_r.w, br.w] + list(br.r) if bi == 0 else []), mark=last)
                        ACT.op(lambda e, bk=bk: e.activation(out=rsd[:], in_=bk[:, :], func=AF.Ln, scale=1.0 / 512, bias=EPS), reads=[br], writes=[rsd_r])
                        ACT.op(lambda e: e.activation(out=rsd[:], in_=rsd[:], func=AF.Exp, scale=-0.5), writes=[rsd_r])
                        for bo in range(4):
                            DVE.op(lambda e, bo=bo, b=b: e.scalar_tensor_tensor(out=snb[b][0][:, bo, :], in0=sf[:, bo, :], scalar=gss[:, bo:bo + 1], in1=rsd[:], op0=ALU.mult, op1=ALU.mult),
                                   reads=[sf_r, gss_r, rsd_r], writes=[snb[b][1]])
                        SP.op(lambda e, b=b, t0=t0: e.dma_start(out=sT[name][:, t0:t0 + 512].rearrange("(k p) t -> p k t", p=128), in_=snb[b][0][:]), reads=[snb[b][1]], dsem=ds_sn[b])

                    stP(0)
                    for hc in range(nhc):
                        if hc + 1 < nhc:
                            stP(hc + 1)
                        stQ(hc)

                if "P" in todo:
                    outstage("P", debug.get("NP", LP) if debug else LP)
                if "O" in todo:
                    outstage("O", LO)
                barrier()


    def attention_setup(T_alloc):
        SCALE = 1.0 / math.sqrt(96.0)
        PCK = 2048
        NKV = 4
        kvk = [T_alloc("kvk%d" % i, [96, PCK], BF16) for i in range(NKV)]
        kvv = [T_alloc("kvv%d" % i, [128, PCK // 128, 65], BF16) for i in range(NKV)]
        kv_r = [Res() for _ in range(NKV)]
        ds_kv = [new_dsem("kv%d" % i) for i in range(NKV)]
        qtb = [T_alloc("qtb%d" % i, [96, LP], BF16) for i in range(2)]; qtb_r = [Res(), Res()]
        ds_q = [new_dsem("q0"), new_dsem("q1")]
        NPT = 5
        pT = [T_alloc("pT%d" % i, [128, 512], BF16) for i in range(NPT)]; pT_r = [Res() for _ in range(NPT)]
        NEP = 3
        osb = [T_alloc("osb%d" % i, [64, 512], F32) for i in range(NEP)]; osb_r = [Res() for _ in range(NEP)]
        bcs = [T_alloc("bcs%d" % i, [64, 512], F32) for i in range(NEP)]; bcs_r = [Res() for _ in range(NEP)]
        rden = T_alloc("rden", [128, 512], F32); rden_r = Res()
        lnr = T_alloc("lnr", [128, 512], F32); lnr_r = Res()
        onesf = T_alloc("onesf", [128, 64], F32); onesf_r = Res()
        asb = [T_alloc("asb%d" % i, [64, 512], F32) for i in range(NEP)]; asb_r = [Res() for _ in range(NEP)]
        ds_a = [new_dsem("a%d" % i) for i in range(NEP)]
        DVE.op(lambda e: e.memset(onesf[:], 1.0), writes=[onesf_r])
        acc, acc_r = banks[0], bank_res[0]
        st_b = [(banks[i], bank_res[i]) for i in range(1, 4)]
        cnt = {"st": 0, "pt": 0, "a": 0, "hb": 0}

        def attend(qname, kname, NQ, NK, cost):
            nkt = NK // 128
            npc = NK // PCK
            tpp = PCK // 128
            plist = [(h, qg, pc) for h in range(8) for qg in range(NQ // 512) for pc in range(npc)]
            st = {"issued": 0}

            def ensure(i):
                while st["issued"] < min(i + 3, len(plist)):
                    j = st["issued"]
                    h, qg, pc = plist[j]
                    sl = j % NKV
                    POOL.op(lambda e, sl=sl, h=h, pc=pc: e.dma_start(out=kvk[sl][:], in_=KT[kname][h, :, pc * PCK:(pc + 1) * PCK]), writes=[kv_r[sl]], dsem=ds_kv[sl])
                    POOL.op(lambda e, sl=sl, h=h, pc=pc: e.dma_start(out=kvv[sl][:], in_=VV[kname][h, :, pc * tpp:(pc + 1) * tpp, :]), writes=[kv_r[sl]], dsem=ds_kv[sl])
                    st["issued"] += 1

            pidx = 0
            for h in range(8):
                hb = cnt["hb"] % 2
                cnt["hb"] += 1
                POOL.op(lambda e, hb=hb, h=h: e.dma_start(out=qtb[hb][:, 0:NQ], in_=QT[qname][h, :, 0:NQ]), writes=[qtb_r[hb]], dsem=ds_q[hb])
                for qg in range(NQ // 512):
                    q0 = qg * 512
                    LOOK = 2
                    pend = []
                    for kt in range(nkt + LOOK):
                        if kt < nkt:
                            pc, lt = divmod(kt, tpp)
                            if lt == 0:
                                ensure(pidx + pc)
                            sl = (pidx + pc) % NKV
                            sb_, sb_r = st_b[cnt["st"] % len(st_b)]
                            cnt["st"] += 1
                            PE.op(lambda e, sb_=sb_, hb=hb, sl=sl, lt=lt, q0=q0: e.matmul(sb_[:, :], lhsT=kvk[sl][:, lt * 128:(lt + 1) * 128], rhs=qtb[hb][:, q0:q0 + 512], start=True, stop=True),
                                  reads=[kv_r[sl], qtb_r[hb]], writes=[sb_r])
                            pi = cnt["pt"] % NPT
                            cnt["pt"] += 1
                            ACT.op(lambda e, sb_=sb_, pi=pi: e.activation(out=pT[pi][:], in_=sb_[:, :], func=AF.Exp, scale=SCALE), reads=[sb_r], writes=[pT_r[pi]])
                            pend.append((pi, kt, sl, lt))
                        if kt >= LOOK:
                            ppi, pkt, psl, plt = pend.pop(0)
                            PE.op(lambda e, ppi=ppi, pkt=pkt, psl=psl, plt=plt: e.matmul(acc[0:65, :], lhsT=kvv[psl][:, plt, :], rhs=pT[ppi][:], start=(pkt == 0), stop=(pkt == nkt - 1)),
                                  reads=[kv_r[psl], pT_r[ppi]], writes=[acc_r])
                        if kt % 16 == 15:
                            yield 11.0
                    pidx += npc
                    ai = cnt["a"] % NEP
                    cnt["a"] += 1
                    ACT.op(lambda e, ai=ai: e.activation(out=osb[ai][:], in_=acc[0:64, :], func=AF.Copy), reads=[acc_r], writes=[osb_r[ai]])
                    ACT.op(lambda e: e.activation(out=lnr[64:65, :], in_=acc[64:65, :], func=AF.Ln), reads=[acc_r], writes=[lnr_r])
                    ACT.op(lambda e: e.activation(out=rden[64:65, :], in_=lnr[64:65, :], func=AF.Exp, scale=-1.0), reads=[lnr_r], writes=[rden_r])
                    bcb, bcr = st_b[cnt["st"] % len(st_b)]
                    cnt["st"] += 1
                    PE.op(lambda e, bcb=bcb: e.matmul(bcb[0:64, :], lhsT=onesf[64:65, 0:64], rhs=rden[64:65, :], start=True, stop=True), reads=[onesf_r, rden_r], writes=[bcr])
                    ACT.op(lambda e, bcb=bcb, ai=ai: e.activation(out=bcs[ai][:], in_=bcb[0:64, :], func=AF.Copy), reads=[bcr], writes=[bcs_r[ai]])
                    DVE.op(lambda e, ai=ai: e.tensor_tensor(out=asb[ai][:], in0=osb[ai][:], in1=bcs[ai][:], op=ALU.mult), reads=[osb_r[ai], bcs_r[ai]], writes=[asb_r[ai]])
                    SP.op(lambda e, ai=ai, h=h, q0=q0: e.dma_start(out=aT[qname][h, :, q0:q0 + 512], in_=asb[ai][:]), reads=[asb_r[ai]], dsem=ds_a[ai])
                    yield 4.0

        def gen():
            todo = debug.get("tsets", ["P", "O"]) if debug else ["P", "O"]
            if "P" in todo:
                NPd = debug.get("NP", LP) if debug else LP
                yield from attend("P", "P", NPd, NPd, 0.0)
            if "O" in todo:
                yield from attend("O", "S", LO, debug.get("NS", LS) if debug else LS, 0.0)

        return gen()

    def phase_E1():
        with contextlib.ExitStack() as ps:
            P_ = ps.enter_context

            def T(name, shape, dt=F32):
                return P_(nc.sbuf_tensor(name, list(shape), dt))

            woa = T("woa", [128, 4, D], BF16); woa_r = Res()
            wos = T("wos", [128, 4, D], BF16); wos_r = Res()
            ga = T("ga", [128, 4]); ga_r = Res()
            g2b = T("g2b", [128, D]); g2b_r = Res()
            dsw = new_dsem("wE1")
            POOL.op(lambda e: e.dma_start(out=woa[:], in_=woa_d.rearrange("(k p) n -> p k n", p=128)), writes=[woa_r], dsem=new_dsem("pq%d" % next(_uid)))
            POOL.op(lambda e: e.dma_start(out=wos[:], in_=wos_d.rearrange("(k p) n -> p k n", p=128)), writes=[wos_r], dsem=new_dsem("pq%d" % next(_uid)))
            SP.op(lambda e: e.dma_start(out=ga[:], in_=ga_d[:, :]), writes=[ga_r], dsem=dsw)
            SP.op(lambda e: e.dma_start(out=g2b[:], in_=g2.broadcast_to([128, D])), writes=[g2b_r], dsem=dsw)
            at = [T("at%d" % i, [128, 4, 512]) for i in range(2)]; at_r = [Res(), Res()]
            snt = [T("snt%d" % i, [128, 4, 512], BF16) for i in range(2)]; snt_r = [Res(), Res()]
            xt = [T("e1x%d" % i, [128, 4, D]) for i in range(2)]; xt_r = [Res(), Res()]
            ds_in = [new_dsem("e1in0"), new_dsem("e1in1")]
            sq = T("e1sq", [128, 4, 512], BF16); sq_r = Res()
            rs = T("e1rs", [128, 512]); rs_r = Res()
            an2 = [T("e1an%d" % i, [128, 4, 512], BF16) for i in range(2)]; an2_r = [Res(), Res()]
            x2t = [T("x2t%d" % i, [128, D]) for i in range(2)]; x2t_r = [Res(), Res()]
            ds_x2 = [new_dsem("x2o0"), new_dsem("x2o1")]
            junk = T("e1junk", [128, D], BF16); junk_r = Res()
            ss2 = [T("ss2_%d" % i, [128, 1]) for i in range(2)]; ss2_r = [Res(), Res()]
            h2 = [T("h2_%d" % i, [128, D], BF16) for i in range(2)]; h2_r = [Res(), Res()]
            h2o = [T("h2o%d" % i, [128, 8, 128], BF16) for i in range(2)]; h2o_r = [Res(), Res()]
            ds_h2 = [new_dsem("h2o0"), new_dsem("h2o1")]
            c = {"t": 0}

            def run(name, xd, N):
                nmt = N // 512

                def pro(mt):
                    b = mt % 2
                    t0 = mt * 512
                    POOL.op(lambda e, b=b, t0=t0: e.dma_start(out=at[b][:], in_=aT[name].rearrange("(k two) f t -> (two f) k t", two=2)[:, :, t0:t0 + 512]), writes=[at_r[b]], dsem=ds_in[b])
                    POOL.op(lambda e, b=b, t0=t0: e.dma_start(out=snt[b][:], in_=sT[name][:, t0:t0 + 512].rearrange("(k p) t -> p k t", p=128)), writes=[snt_r[b]], dsem=ds_in[b])
                    POOL.op(lambda e, b=b, t0=t0: e.dma_start(out=xt[b][:], in_=xd[t0:t0 + 512, :].rearrange("(t p) d -> p t d", p=128)), writes=[xt_r[b]], dsem=ds_in[b])
                    fl = lambda t: t[:].rearrange("p a b -> p (a b)")
                    ACT.op(lambda e, b=b: e.activation(out=fl(sq), in_=fl(at[b]), func=AF.Square), reads=[at_r[b]], writes=[sq_r])
                    bk, br = next_bank()
                    for h in range(4):
                        last = h == 3
                        PE.op(lambda e, bk=bk, h=h: e.matmul(bk[:, :], lhsT=onesb[:, :], rhs=sq[:, h, :], start=(h == 0), stop=(h == 3)),
                              reads=([sq_r, onesb_r] if last else []), writes=([br] if last else []),
                              extra=([sq_r.w, onesb_r.w, br.w] + list(br.r) if h == 0 else []), mark=last)
                    ACT.op(lambda e, bk=bk: e.activation(out=rs[:], in_=bk[:, :], func=AF.Ln, scale=1.0 / 512, bias=EPS), reads=[br], writes=[rs_r])
                    ACT.op(lambda e: e.activation(out=rs[:], in_=rs[:], func=AF.Exp, scale=-0.5), writes=[rs_r])
                    for h in range(4):
                        DVE.op(lambda e, b=b, h=h: e.scalar_tensor_tensor(out=an2[b][:, h, :], in0=at[b][:, h, :], scalar=ga[:, h:h + 1], in1=rs[:], op0=ALU.mult, op1=ALU.mult),
                               reads=[at_r[b], ga_r, rs_r], writes=[an2_r[b]])

                def tiles(mt):
                    b = mt % 2
                    t0 = mt * 512
                    def stA(t):
                        xb = t % 2
                        for half in range(2):
                            bk, br = next_bank()
                            n = 0
                            for h in range(4):
                                n += 1
                                PE.op(lambda e, bk=bk, h=h, t=t, half=half: e.matmul(bk[:, :], lhsT=an2[b][:, h, t * 128:(t + 1) * 128], rhs=woa[:, h, half * 512:(half + 1) * 512], start=(h == 0), stop=False),
                                      extra=([an2_r[b].w, woa_r.w, wos_r.w, snt_r[b].w, br.w] + list(br.r) if n == 1 else []), mark=False)
                            for k4 in range(4):
                                last = k4 == 3
                                PE.op(lambda e, bk=bk, k4=k4, t=t, half=half, b=b: e.matmul(bk[:, :], lhsT=snt[b][:, k4, t * 128:(t + 1) * 128], rhs=wos[:, k4, half * 512:(half + 1) * 512], start=False, stop=(k4 == 3)),
                                      reads=([an2_r[b], woa_r, wos_r, snt_r[b]] if last else []), writes=([br] if last else []), mark=last)
                            DVE.op(lambda e, bk=bk, xb=xb, b=b, t=t, half=half: e.tensor_tensor(out=x2t[xb][:, half * 512:(half + 1) * 512], in0=bk[:, :], in1=xt[b][:, t, half * 512:(half + 1) * 512], op=ALU.add),
                                   reads=[br, xt_r[b]], writes=[x2t_r[xb]])
                        SP.op(lambda e, xb=xb, t0=t0, t=t: e.dma_start(out=x2s[name][t0 + t * 128:t0 + (t + 1) * 128, :], in_=x2t[xb][:]), reads=[x2t_r[xb]], dsem=ds_x2[xb])
                        ACT.op(lambda e, xb=xb: e.activation(out=junk[:], in_=x2t[xb][:], func=AF.Square, accum_out=ss2[xb][:, 0:1]), reads=[x2t_r[xb]], writes=[junk_r, ss2_r[xb]])
                        ACT.op(lambda e, xb=xb: e.activation(out=ss2[xb][:], in_=ss2[xb][:], func=AF.Ln, scale=1.0 / D, bias=EPS), writes=[ss2_r[xb]])
                        ACT.op(lambda e, xb=xb: e.activation(out=ss2[xb][:], in_=ss2[xb][:], func=AF.Exp, scale=-0.5), writes=[ss2_r[xb]])
                        DVE.op(lambda e, xb=xb: e.scalar_tensor_tensor(out=h2[xb][:], in0=x2t[xb][:], scalar=ss2[xb][:, 0:1], in1=g2b[:], op0=ALU.mult, op1=ALU.mult),
                               reads=[x2t_r[xb], ss2_r[xb], g2b_r], writes=[h2_r[xb]])

                    def stB(t):
                        xb = t % 2
                        bk, br = next_bank()
                        bkb = bk[:, :].bitcast(BF16)
                        for kk in range(8):
                            last = kk == 7
                            PE.op(lambda e, bkb=bkb, kk=kk, xb=xb: e.transpose(out=bkb[:, kk * 128:(kk + 1) * 128], in_=h2[xb][:, kk * 128:(kk + 1) * 128], identity=identb[:, :]),
                                  reads=([h2_r[xb], identb_r] if last else []), writes=([br] if last else []),
                                  extra=([h2_r[xb].w, identb_r.w, br.w] + list(br.r) if kk == 0 else []), mark=last)
                        ACT.op(lambda e, bkb=bkb, xb=xb: e.activation(out=h2o[xb][:].rearrange("p a b -> p (a b)"), in_=bkb, func=AF.Copy), reads=[br], writes=[h2o_r[xb]])
                        SP.op(lambda e, xb=xb, t0=t0, t=t: e.dma_start(out=h2T[name][:, t0 + t * 128:t0 + (t + 1) * 128].rearrange("(k p) t -> p k t", p=128), in_=h2o[xb][:]),
                              reads=[h2o_r[xb]], dsem=ds_h2[xb])

                    stA(0)
                    for t in range(4):
                        if t + 1 < 4:
                            stA(t + 1)
                        stB(t)
                        if t == 1 and mt + 1 < nmt:
                            pro(mt + 1)

                pro(0)
                for mt in range(nmt):
                    tiles(mt)

            todo = debug.get("tsets", ["P", "O"]) if debug else ["P", "O"]
            if "P" in todo:
                run("P", xp, debug.get("NP", LP) if debug else LP)
            if "O" in todo:
                run("O", xo, LO)
            barrier()

    def phase_E2():
        with contextlib.ExitStack() as ps:
            P_ = ps.enter_context

            def T(name, shape, dt=F32):
                return P_(nc.sbuf_tensor(name, list(shape), dt))

            w1 = T("w1", [128, 8, 4096], BF16); w1_rs = [Res(), Res()]
            w2 = T("w2", [128, 32, D], BF16); w2_rs = [Res() for _ in range(4)]
            gfb = T("gfb", [128, D]); gfb_r = Res()
            dsw = new_dsem("wE2")
            POOL.op(lambda e: e.dma_start(out=w1[:, :, 0:2048], in_=w1_d.rearrange("(k p) n -> p k n", p=128)[:, :, 0:2048]), writes=[w1_rs[0]], dsem=new_dsem("pq%d" % next(_uid)))
            for qq in range(2):
                POOL.op(lambda e, qq=qq: e.dma_start(out=w2[:, qq * 8:(qq + 1) * 8, :], in_=w2_d.rearrange("(f p) n -> p f n", p=128)[:, qq * 8:(qq + 1) * 8, :]), writes=[w2_rs[qq]], dsem=new_dsem("pq%d" % next(_uid)))
            POOL.op(lambda e: e.dma_start(out=w1[:, :, 2048:4096], in_=w1_d.rearrange("(k p) n -> p k n", p=128)[:, :, 2048:4096]), writes=[w1_rs[1]], dsem=new_dsem("pq%d" % next(_uid)))
            for qq in range(2, 4):
                POOL.op(lambda e, qq=qq: e.dma_start(out=w2[:, qq * 8:(qq + 1) * 8, :], in_=w2_d.rearrange("(f p) n -> p f n", p=128)[:, qq * 8:(qq + 1) * 8, :]), writes=[w2_rs[qq]], dsem=new_dsem("pq%d" % next(_uid)))
            SP.op(lambda e: e.dma_start(out=gfb[:], in_=gf_d.broadcast_to([128, D])), writes=[gfb_r], dsem=dsw)
            MT = 256
            hT_ = [T("e2h%d" % i, [128, 8, MT], BF16) for i in range(2)]; hT_r = [Res(), Res()]
            x2t = [T("e2x%d" % i, [128, 2, D]) for i in range(2)]; x2t_r = [Res(), Res()]
            ds_in = [new_dsem("e2in0"), new_dsem("e2in1")]
            NR = 4
            rl = [T("e2r%d" % i, [128, MT], BF16) for i in range(NR)]; rl_r = [Res() for _ in range(NR)]
            hid = [T("e2hid%d" % i, [128, MT], BF16) for i in range(NR)]; hid_r = [Res() for _ in range(NR)]
            x3 = T("x3", [128, D]); x3_r = Res()
            junk = T("e2junk", [128, D], BF16); junk_r = Res()
            ss = T("e2ss", [128, 1]); ss_r = Res()
            yt = [T("yt%d" % i, [128, D]) for i in range(2)]; yt_r = [Res(), Res()]
            ds_y = [new_dsem("y0"), new_dsem("y1")]
            acc_b = [[(banks[0], bank_res[0]), (banks[1], bank_res[1])], [(banks[2], bank_res[2]), (banks[3], bank_res[3])]]
            m1_b = [(banks[i], bank_res[i]) for i in range(4, 8)]
            c = {"m1": 0, "r": 0, "y": 0}

            def run(name, yd, N):
                for mt in range(N // MT):
                    b = mt % 2
                    t0 = mt * MT
                    POOL.op(lambda e, b=b, t0=t0: e.dma_start(out=hT_[b][:], in_=h2T[name][:, t0:t0 + MT].rearrange("(k p) t -> p k t", p=128)), writes=[hT_r[b]], dsem=ds_in[b])
                    POOL.op(lambda e, b=b, t0=t0: e.dma_start(out=x2t[b][:], in_=x2s[name][t0:t0 + MT, :].rearrange("(t p) d -> p t d", p=128)), writes=[x2t_r[b]], dsem=ds_in[b])
                    pend = []
                    for f in range(33):
                        if f < 32:
                            bk, br = m1_b[c["m1"] % 4]
                            c["m1"] += 1
                            for kk in range(8):
                                last = kk == 7
                                PE.op(lambda e, bk=bk, kk=kk, f=f, b=b: e.matmul(bk[:, 0:MT], lhsT=w1[:, kk, f * 128:(f + 1) * 128], rhs=hT_[b][:, kk, :], start=(kk == 0), stop=(kk == 7)),
                                      reads=([w1_rs[f // 16], hT_r[b]] if last else []), writes=([br] if last else []),
                                      extra=([w1_rs[f // 16].w, hT_r[b].w, br.w] + list(br.r) if kk == 0 else []), mark=last)
                            ri = c["r"] % NR
                            c["r"] += 1
                            ACT.op(lambda e, bk=bk, ri=ri: e.activation(out=rl[ri][:], in_=bk[:, 0:MT], func=AF.Relu), reads=[br], writes=[rl_r[ri]])
                            DVE.op(lambda e, ri=ri: e.tensor_tensor(out=hid[ri][:], in0=rl[ri][:], in1=rl[ri][:], op=ALU.mult), reads=[rl_r[ri]], writes=[hid_r[ri]])
                            pend.append((ri, f))
                        if f >= 1:
                            pri, pf = pend.pop(0)
                            n = 0
                            for t in range(2):
                                for half in range(2):
                                    n += 1
                                    ab, ar = acc_b[t][half]
                                    lastf = pf == 31
                                    PE.op(lambda e, ab=ab, pri=pri, t=t, half=half, pf=pf: e.matmul(ab[:, :], lhsT=hid[pri][:, t * 128:(t + 1) * 128], rhs=w2[:, pf, half * 512:(half + 1) * 512], start=(pf == 0), stop=(pf == 31)),
                                          reads=([hid_r[pri], w2_rs[pf // 8]] if n == 4 else []), writes=([ar] if lastf else []),
                                          extra=([hid_r[pri].w, w2_rs[pf // 8].w] + ([ar.w] + list(ar.r) if pf == 0 else [])), mark=(n == 4 or lastf))
                    for t in range(2):
                        for half in range(2):
                            ab, ar = acc_b[t][half]
                            DVE.op(lambda e, ab=ab, b=b, t=t, half=half: e.tensor_tensor(out=x3[:, half * 512:(half + 1) * 512], in0=ab[:, :], in1=x2t[b][:, t, half * 512:(half + 1) * 512], op=ALU.add),
                                   reads=[ar, x2t_r[b]], writes=[x3_r])
                        ACT.op(lambda e: e.activation(out=junk[:], in_=x3[:], func=AF.Square, accum_out=ss[:, 0:1]), reads=[x3_r], writes=[junk_r, ss_r])
                        ACT.op(lambda e: e.activation(out=ss[:], in_=ss[:], func=AF.Ln, scale=1.0 / D, bias=EPS), writes=[ss_r])
                        ACT.op(lambda e: e.activation(out=ss[:], in_=ss[:], func=AF.Exp, scale=-0.5), writes=[ss_r])
                        yi = c["y"] % 2
                        c["y"] += 1
                        DVE.op(lambda e, yi=yi: e.scalar_tensor_tensor(out=yt[yi][:], in0=x3[:], scalar=ss[:, 0:1], in1=gfb[:], op0=ALU.mult, op1=ALU.mult),
                               reads=[x3_r, ss_r, gfb_r], writes=[yt_r[yi]])
                        SP.op(lambda e, yi=yi, t0=t0, t=t: e.dma_start(out=yd[t0 + t * 128:t0 + (t + 1) * 128, :], in_=yt[yi][:]), reads=[yt_r[yi]], dsem=ds_y[yi])

            todo = debug.get("tsets", ["P", "O"]) if debug else ["P", "O"]
            if "P" in todo:
                run("P", yp, debug.get("NP", LP) if debug else LP)
            if "O" in todo:
                run("O", yo, LO)
            barrier()

    phase_tables()
    phase_A()
    if not (debug and debug.get('skipS')):
        phase_S()
    if not (debug and debug.get('skipT')):
        phase_E1()
        phase_E2()

    block = E(nc.Block())

    @block.sync
    def _(e):
        SP.emit(e)

    @block.scalar
    def _(e):
        ACT.emit(e)

    @block.vector
    def _(e):
        DVE.emit(e)

    @block.gpsimd
    def _(e):
        POOL.emit(e)

    @block.tensor
    def _(e):
        PE.emit(e)

    st.close()
    return nc, dbg_outs


def _host_inputs(inputs, debug=None):
    f = lambda a: np.ascontiguousarray(np.asarray(a, dtype=np.float32))
    x_prompt = f(inputs["x_prompt"])
    x_sample = f(inputs["x_sample"])[0]
    w_in = f(inputs["w_in"])[0]
    kr = w_in[:, 384:416]
    wkr = np.concatenate([kr, np.concatenate([kr[:, 16:32], kr[:, 0:16]], axis=1)], axis=1)
    w_uq = f(inputs["w_uq"])[0].reshape(256, 8, 96)
    wq_nope = w_uq[:, :, 0:64].reshape(256, 512)
    wq_rope = w_uq[:, :, 64:96].reshape(256, 256)
    wq_rsw = np.concatenate([w_uq[:, :, 80:96], w_uq[:, :, 64:80]], axis=2).reshape(256, 256)
    w_ukv = f(inputs["w_ukv"])[0].reshape(128, 8, 128)
    wk_nope = w_ukv[:, :, 0:64].reshape(128, 512)
    wv = w_ukv[:, :, 64:128].reshape(128, 512)
    common = dict(
        posv=np.arange(LS, dtype=np.float32).reshape(1, LS), xs=x_sample, w_in=w_in, wkr=f(wkr), wq_nope=f(wq_nope), wq_rope=f(wq_rope), wq_rsw=f(wq_rsw),
        wk_nope=f(wk_nope), wv=f(wv), g1=f(inputs["norm1_g"]).reshape(1, D),
        gq=f(f(inputs["q_norm_g"]).reshape(2, 128).T), gkv=f(inputs["kv_norm_g"]).reshape(128, 1),
    )
    def pair32(a):
        return f(a.reshape(2, 16, 2, 64).transpose(2, 3, 0, 1).reshape(128, 32))

    def pairB(a):
        return f(a.reshape(2, 16, 2, 64, 16).transpose(2, 3, 0, 1, 4).reshape(128, 512))

    def pairC(a):
        return f(a.reshape(2, 16, 2, 16, 64).transpose(2, 4, 0, 1, 3).reshape(128, 512))

    ldt = f(inputs["log_dt"])[0]
    common.update(dict(
        lamr_p=pair32(f(inputs["lam_re"])[0]), lami_p=pair32(f(inputs["lam_im"])[0]),
        ldt_p=pair32(np.broadcast_to(ldt[:, :, None], (2, 32, 64))),
        bre_p=pairB(f(inputs["b_re"])[0]), bim_p=pairB(f(inputs["b_im"])[0]),
        cre_p=pairC(f(inputs["c_re"])[0]), cim_p=pairC(f(inputs["c_im"])[0]),
        dskip_p=f(f(inputs["d_skip"])[0].reshape(4, 128).T), bglu_p=f(f(inputs["b_glu"])[0].reshape(4, 128).T),
        gs_p=f(f(inputs["ssm_out_g"])[0].reshape(4, 128).T), w_glu=f(inputs["w_glu"])[0],
    ))
    w_out = f(inputs["w_out"])[0]
    common.update(dict(
        woa_d=f(w_out[:512]), wos_d=f(w_out[512:]),
        ga_d=f(f(inputs["attn_out_g"])[0].reshape(4, 128).T), g2=f(inputs["norm2_g"]).reshape(1, D),
        gf_d=f(inputs["final_g"]).reshape(1, D), w1_d=f(inputs["w_mlp1"])[0], w2_d=f(inputs["w_mlp2"])[0],
    ))
    p = np.arange(128)
    maps = []
    for c in range(NCORES):
        cs = np.zeros((128, NCONST), np.float32)
        cs[:, C_ID:C_ID + 128] = np.eye(128, dtype=np.float32)
        cs[:, C_BM:C_BM + 128] = np.kron(np.eye(8, dtype=np.float32), np.ones((16, 16), np.float32))
        cs[:, C_ME] = (p < 64)
        cs[:, C_MO] = (p >= 64)
        cs[:, C_NME] = -(p < 64).astype(np.float32)
        cs[:, C_NMO] = -(p >= 64).astype(np.float32)
        fi = (p % 32) % 16
        cs[:, C_INV] = (10000.0 ** (-(2.0 * fi) / 32.0)).astype(np.float32)
        cs[:, C_SGN] = np.where((p % 32) < 16, -1.0, 1.0)
        cs[:, C_OFF] = 2048.0 * c
        cs[:, C_SEL + c] = 1.0
        m = dict(common)
        m["xp"] = x_prompt[c]
        m["xo"] = np.ascontiguousarray(x_sample[c * LO:(c + 1) * LO])
        m["consts"] = cs
        if debug:
            m["xs"] = m["xs"][:debug.get("LSd", LS)]
            m["xo"] = m["xo"][:debug.get("LOd", LO)]
        maps.append(m)
    return maps


def kernel(**inputs):
    nc, _ = _build()
    maps = _host_inputs(inputs)
    res = run_bass_kernel_spmd(nc, maps, core_ids=list(range(NCORES)))
    yp = np.stack([res.results[c]["yp"] for c in range(NCORES)], axis=0)
    ys = np.concatenate([res.results[c]["yo"] for c in range(NCORES)], axis=0)[None]
    return (yp.astype(np.float32), ys.astype(np.float32))
```
